# Optimizing a Trainium2 kernel written in Bass

```python
import math
import jax
import jax.numpy as jnp
from jax import lax
import numpy as np

D_MODEL = 1024
BATCH = 4
SEQ = 8192
DEPTH = 2

GRID_W = 64
CTX_LEN = 256
N_ADA = 9
EPS = 1e-6
NEG_INF = -1e30
ROPE_BASE = 10000.0
D_FF = ((8 * D_MODEL // 3 + 127) // 128) * 128

HEAD_DIM = 64
NA_HEADS = 4
NA_WIN_ROWS = 8
NA_WIN_COLS = 16
NA_QBLOCK_COLS = 16
NA_KBLOCK_COLS = 32
D_A = NA_HEADS * HEAD_DIM
GM_GROUPS = 4
GM_WIDTH = 64
GM_CHUNK = 128
D_B = GM_GROUPS * GM_WIDTH
DA_HEADS = 4
DA_QK_DIM = 64
DA_V_DIM = 2 * DA_QK_DIM
DA_QBLOCK = 128
D_C = DA_HEADS * DA_V_DIM
D_QK_C = DA_HEADS * 2 * DA_QK_DIM

D_MIX = D_A + D_B + D_C
SPLIT_SIZES = (D_A, D_A, D_A, D_B, D_B, D_QK_C, D_QK_C, D_C)
SPLIT_POINTS = (D_A, 2 * D_A, 3 * D_A, 3 * D_A + D_B, 3 * D_A + 2 * D_B,
                3 * D_A + 2 * D_B + D_QK_C, 3 * D_A + 2 * D_B + 2 * D_QK_C)
D_IN = 3 * D_A + 2 * D_B + 2 * D_QK_C + D_C

kernel_name = "hybrid_natten_gmlp_diffattn_dit_block"


def rms_norm(x, gain=None, eps=EPS):
    xf = x.astype(jnp.float32)
    y = xf * lax.rsqrt(jnp.mean(xf * xf, axis=-1, keepdims=True) + eps)
    if gain is not None:
        y = y * gain.astype(jnp.float32)
    return y.astype(x.dtype)


def modulate(x, shift, scale):
    return rms_norm(x) * (1 + scale) + shift


def swiglu(h, w1, w3, w2):
    return (jax.nn.silu(h @ w1) * (h @ w3)) @ w2


def axial_rope_tables(n_tokens):
    t = jnp.arange(n_tokens)
    row = (t // GRID_W).astype(jnp.float32)
    col = (t % GRID_W).astype(jnp.float32)
    n_freq = DA_QK_DIM // 4
    freqs = ROPE_BASE ** (-jnp.arange(n_freq, dtype=jnp.float32) / n_freq)
    ang = jnp.concatenate([row[:, None] * freqs, col[:, None] * freqs], axis=-1)
    return jnp.cos(ang), jnp.sin(ang)


def apply_rope(x, cos, sin):
    half = x.shape[-1] // 2
    xf = x.astype(jnp.float32)
    x1, x2 = xf[..., :half], xf[..., half:]
    out = jnp.concatenate([x1 * cos - x2 * sin, x1 * sin + x2 * cos], axis=-1)
    return out.astype(x.dtype)


def split_projection(z):
    B, n, _ = z.shape
    qa, ka, va, ub, vb, qd, kd, vd = jnp.split(z, SPLIT_POINTS, axis=-1)
    heads = lambda t: t.reshape(B, n, NA_HEADS, HEAD_DIM).transpose(0, 2, 1, 3)
    groups = lambda t: jax.nn.gelu(t, approximate=False).reshape(B, n, GM_GROUPS, GM_WIDTH)
    qk_c = lambda t: t.reshape(B, n, DA_HEADS, 2, DA_QK_DIM)
    return (heads(qa), heads(ka), heads(va), groups(ub), groups(vb),
            qk_c(qd), qk_c(kd), vd.reshape(B, n, DA_HEADS, DA_V_DIM))


def neighborhood_attention(q, k, v, kc, vc, rpb):
    B, H, L, d = q.shape
    rows = L // GRID_W
    kh = min(NA_WIN_ROWS, rows)
    scale = d ** -0.5
    qg = q.reshape(B, H, rows, GRID_W, d)
    kg = k.reshape(B, H, rows, GRID_W, d)
    vg = v.reshape(B, H, rows, GRID_W, d)
    r = jnp.arange(rows)
    key_rows = jnp.clip(r - kh // 2, 0, rows - kh)[:, None] + jnp.arange(kh)[None, :]
    dr_idx = key_rows - r[:, None] + (NA_WIN_ROWS - 1)
    s_ctx = (jnp.einsum('bhld,bhcd->bhlc', q, kc).astype(jnp.float32) * scale
             ).reshape(B, H, rows, GRID_W, -1)
    n_win = kh * NA_KBLOCK_COLS
    outs = []
    for j in range(GRID_W // NA_QBLOCK_COLS):
        qc0 = j * NA_QBLOCK_COLS
        kc0 = min(max(qc0 - NA_WIN_COLS // 2, 0), GRID_W - NA_KBLOCK_COLS)
        qcols = np.arange(qc0, qc0 + NA_QBLOCK_COLS)
        kcols = np.arange(kc0, kc0 + NA_KBLOCK_COLS)
        w0 = np.clip(qcols - NA_WIN_COLS // 2, 0, GRID_W - NA_WIN_COLS)
        in_win = (kcols[None, :] >= w0[:, None]) & (kcols[None, :] < w0[:, None] + NA_WIN_COLS)
        mask = np.broadcast_to(in_win[:, None, :], (NA_QBLOCK_COLS, kh, NA_KBLOCK_COLS)
                               ).reshape(NA_QBLOCK_COLS, n_win)
        dc_idx = np.clip(kcols[None, :] - qcols[:, None] + NA_WIN_COLS - 1, 0, 2 * NA_WIN_COLS - 2)
        bias = rpb[:, dr_idx[:, :, None, None], dc_idx[None, None, :, :]]
        bias = bias.transpose(0, 1, 3, 2, 4).reshape(H, rows, NA_QBLOCK_COLS, n_win)
        kb = kg[:, :, :, kc0:kc0 + NA_KBLOCK_COLS][:, :, key_rows].reshape(B, H, rows, n_win, d)
        vb = vg[:, :, :, kc0:kc0 + NA_KBLOCK_COLS][:, :, key_rows].reshape(B, H, rows, n_win, d)
        qb = qg[:, :, :, qc0:qc0 + NA_QBLOCK_COLS]
        s_win = jnp.einsum('bhrqd,bhrkd->bhrqk', qb, kb).astype(jnp.float32) * scale \
            + bias.astype(jnp.float32)[None]
        s_win = jnp.where(mask, s_win, NEG_INF)
        s = jnp.concatenate([s_win, s_ctx[:, :, :, qc0:qc0 + NA_QBLOCK_COLS]], axis=-1)
        p = jax.nn.softmax(s, axis=-1).astype(v.dtype)
        o = jnp.einsum('bhrqk,bhrkd->bhrqd', p[..., :n_win], vb) \
            + jnp.einsum('bhrqc,bhcd->bhrqd', p[..., n_win:], vc)
        outs.append(o)
    return jnp.concatenate(outs, axis=3).reshape(B, H, L, d)


def dense_attention(q, k, v):
    s = jnp.einsum('bhqd,bhkd->bhqk', q, k).astype(jnp.float32) * (q.shape[-1] ** -0.5)
    p = jax.nn.softmax(s, axis=-1).astype(v.dtype)
    return jnp.einsum('bhqk,bhkd->bhqd', p, v)


def chunk_gmlp(u, v, w_s, b_s, g_norm):
    B, n, G, W = u.shape
    vn = rms_norm(v, g_norm).reshape(B, n // GM_CHUNK, GM_CHUNK, G, W)
    s = jnp.einsum('gpq,bcqgw->bcpgw', w_s, vn) + b_s.T[None, None, :, :, None]
    return u * s.reshape(B, n, G, W)


def diff_weighted_values(q, k, v, lam, scale):
    s = jnp.einsum('bqhcd,bkhcd->bhcqk', q, k).astype(jnp.float32) * scale
    p = jax.nn.softmax(s, axis=-1)
    w = p[:, :, 0] - lam * p[:, :, 1]
    return jnp.einsum('bhqk,bkhd->bqhd', w.astype(v.dtype), v)


def diff_attention_latent(q, k, v, kc, vc, lam):
    B, L, H, _, dq = q.shape
    scale = dq ** -0.5
    keys = jnp.concatenate([k, kc], axis=1)
    vals = jnp.concatenate([v, vc], axis=1)
    nb = L // DA_QBLOCK
    q_blocks = q.reshape(B, nb, DA_QBLOCK, H, 2, dq).transpose(1, 0, 2, 3, 4, 5)
    out = lax.map(lambda qb: diff_weighted_values(qb, keys, vals, lam, scale), q_blocks)
    return out.transpose(1, 0, 2, 3, 4).reshape(B, L, H, v.shape[-1])


def merge_mixers(out_a, out_b, out_c, w_out, da_subln, lambda_init):
    B, n = out_b.shape[0], out_b.shape[1]
    a = out_a.transpose(0, 2, 1, 3).reshape(B, n, D_A)
    b = out_b.reshape(B, n, D_B)
    cmix = (rms_norm(out_c, da_subln) * (1.0 - lambda_init)).reshape(B, n, D_C)
    return jnp.concatenate([a, b, cmix], axis=-1) @ w_out


def hybrid_layer(x, xc, c_silu, cc_silu, rope_cos, rope_sin, layer_idx, ctx_out,
                 w_ada, b_ada, ffn1_w1, ffn1_w3, ffn1_w2, w_in, w_out, na_rpb,
                 gm_ws, gm_bs, gm_norm, da_lq1, da_lk1, da_lq2, da_lk2, da_subln,
                 ffn2_w1, ffn2_w3, ffn2_w2):
    mod = jnp.split((c_silu @ w_ada + b_ada)[:, None, :], N_ADA, axis=-1)
    mod_c = jnp.split((cc_silu @ w_ada + b_ada)[None, None, :], N_ADA, axis=-1)

    x = x + 0.5 * mod[2] * swiglu(modulate(x, mod[0], mod[1]), ffn1_w1, ffn1_w3, ffn1_w2)
    xc = xc + 0.5 * mod_c[2] * swiglu(modulate(xc, mod_c[0], mod_c[1]), ffn1_w1, ffn1_w3, ffn1_w2)

    qa, ka, va, ub, vb, qd, kd, vd = split_projection(modulate(x, mod[3], mod[4]) @ w_in)
    qac, kac, vac, ubc, vbc, qdc, kdc, vdc = split_projection(modulate(xc, mod_c[3], mod_c[4]) @ w_in)

    lambda_init = 0.8 - 0.6 * math.exp(-0.3 * layer_idx)
    lam = (jnp.exp(jnp.sum(da_lq1.astype(jnp.float32) * da_lk1.astype(jnp.float32)))
           - jnp.exp(jnp.sum(da_lq2.astype(jnp.float32) * da_lk2.astype(jnp.float32))) + lambda_init)

    out_a = neighborhood_attention(qa, ka, va, kac, vac, na_rpb)
    out_b = chunk_gmlp(ub, vb, gm_ws, gm_bs, gm_norm)
    cos = rope_cos[None, :, None, None, :]
    sin = rope_sin[None, :, None, None, :]
    out_c = diff_attention_latent(apply_rope(qd, cos, sin), apply_rope(kd, cos, sin), vd, kdc, vdc, lam)
    x = x + mod[5] * merge_mixers(out_a, out_b, out_c, w_out, da_subln, lambda_init)

    if ctx_out:
        out_ac = dense_attention(qac, kac, vac)
        out_bc = chunk_gmlp(ubc, vbc, gm_ws, gm_bs, gm_norm)
        out_cc = diff_weighted_values(qdc, kdc, vdc, lam, DA_QK_DIM ** -0.5)
        xc = xc + mod_c[5] * merge_mixers(out_ac, out_bc, out_cc, w_out, da_subln, lambda_init)
        xc = xc + 0.5 * mod_c[8] * swiglu(modulate(xc, mod_c[6], mod_c[7]), ffn2_w1, ffn2_w3, ffn2_w2)

    x = x + 0.5 * mod[8] * swiglu(modulate(x, mod[6], mod[7]), ffn2_w1, ffn2_w3, ffn2_w2)
    return x, xc


def setup_inputs(seed: int = 0) -> dict:
    key = jax.random.key(seed)
    ks = jax.random.split(key, 24)
    f32 = jnp.float32

    def nrm(k, shape, scale):
        return jax.random.normal(k, shape, f32) * scale

    return {
        "x": nrm(ks[0], (BATCH, SEQ, D_MODEL), 1.0),
        "c": nrm(ks[1], (BATCH, D_MODEL), 1.0),
        "ctx": nrm(ks[2], (BATCH, CTX_LEN, D_MODEL), 1.0),
        "c_ctx": nrm(ks[3], (D_MODEL,), 1.0),
        "w_ada": nrm(ks[4], (DEPTH, D_MODEL, N_ADA * D_MODEL), D_MODEL ** -0.5),
        "b_ada": nrm(ks[5], (DEPTH, N_ADA * D_MODEL), 0.02),
        "ffn1_w1": nrm(ks[6], (DEPTH, D_MODEL, D_FF), D_MODEL ** -0.5),
        "ffn1_w3": nrm(ks[7], (DEPTH, D_MODEL, D_FF), D_MODEL ** -0.5),
        "ffn1_w2": nrm(ks[8], (DEPTH, D_FF, D_MODEL), D_FF ** -0.5),
        "w_in": nrm(ks[9], (DEPTH, D_MODEL, D_IN), D_MODEL ** -0.5),
        "w_out": nrm(ks[10], (DEPTH, D_MIX, D_MODEL), D_MIX ** -0.5),
        "na_rpb": nrm(ks[11], (DEPTH, NA_HEADS, 2 * NA_WIN_ROWS - 1, 2 * NA_WIN_COLS - 1), 0.5),
        "gm_ws": nrm(ks[12], (DEPTH, GM_GROUPS, GM_CHUNK, GM_CHUNK), GM_CHUNK ** -0.5),
        "gm_bs": 1.0 + nrm(ks[13], (DEPTH, GM_GROUPS, GM_CHUNK), 0.02),
        "gm_norm": 1.0 + nrm(ks[14], (DEPTH, GM_GROUPS, GM_WIDTH), 0.02),
        "da_lq1": nrm(ks[15], (DEPTH, DA_QK_DIM), 0.1),
        "da_lk1": nrm(ks[16], (DEPTH, DA_QK_DIM), 0.1),
        "da_lq2": nrm(ks[17], (DEPTH, DA_QK_DIM), 0.1),
        "da_lk2": nrm(ks[18], (DEPTH, DA_QK_DIM), 0.1),
        "da_subln": 1.0 + nrm(ks[19], (DEPTH, DA_V_DIM), 0.02),
        "ffn2_w1": nrm(ks[20], (DEPTH, D_MODEL, D_FF), D_MODEL ** -0.5),
        "ffn2_w3": nrm(ks[21], (DEPTH, D_MODEL, D_FF), D_MODEL ** -0.5),
        "ffn2_w2": nrm(ks[22], (DEPTH, D_FF, D_MODEL), D_FF ** -0.5),
        "final_norm": 1.0 + nrm(ks[23], (D_MODEL,), 0.02),
    }


def reference(x, c, ctx, c_ctx, w_ada, b_ada, ffn1_w1, ffn1_w3, ffn1_w2, w_in, w_out, na_rpb,
              gm_ws, gm_bs, gm_norm, da_lq1, da_lk1, da_lq2, da_lk2, da_subln,
              ffn2_w1, ffn2_w3, ffn2_w2, final_norm):
    n_tokens = x.shape[1]
    rope_cos, rope_sin = axial_rope_tables(n_tokens)
    c_silu = jax.nn.silu(c)
    cc_silu = jax.nn.silu(c_ctx)
    xc = ctx
    for l in range(DEPTH):
        x, xc = hybrid_layer(
            x, xc, c_silu, cc_silu, rope_cos, rope_sin, l, l < DEPTH - 1,
            w_ada[l], b_ada[l], ffn1_w1[l], ffn1_w3[l], ffn1_w2[l], w_in[l], w_out[l], na_rpb[l],
            gm_ws[l], gm_bs[l], gm_norm[l], da_lq1[l], da_lk1[l], da_lq2[l], da_lk2[l], da_subln[l],
            ffn2_w1[l], ffn2_w3[l], ffn2_w2[l])
    return rms_norm(x, final_norm)
```

```python
import math
import contextlib
import numpy as np
import concourse.bass as bass
import concourse.mybir as mybir
from concourse.bass_utils import run_bass_kernel_spmd

F32 = mybir.dt.float32
BF16 = mybir.dt.bfloat16
AF = mybir.ActivationFunctionType
ALU = mybir.AluOpType
AX = mybir.AxisListType

D = 1024
KC = 8
TL = 8192
NCORES = 4
TC = 256
TT = TL + TC
FF = 2816
NJ = 22
DEPTH = 2
EPS = 1e-6
NEG = -30000.0
WEXT = 3840
C_QA, C_KA, C_U, C_QD, C_KD, C_QDS, C_KDS, C_VAB, C_VD = 0, 256, 512, 768, 1280, 1792, 2304, 2816, 3328
R_KD, R_VD, R_KA, R_VA, R_END = 0, 512, 1024, 1280, 1536

ENGS = ("pe", "act", "dve", "pool", "sp")
NDSEM = 56


class Tok:
    __slots__ = ("w", "rs")

    def __init__(self):
        self.w = None
        self.rs = {}


class TF(dict):
    def __missing__(self, k):
        t = Tok()
        self[k] = t
        return t


class Ins:
    __slots__ = ("eng", "fn", "deps", "idx", "need_inc", "val", "dsem", "dval", "is_dma")

    def __init__(self, eng, fn):
        self.eng = eng
        self.fn = fn
        self.deps = []
        self.idx = -1
        self.need_inc = False
        self.val = 0
        self.dsem = None
        self.dval = 0
        self.is_dma = False


class Prog:
    def __init__(self, nc, stack):
        self.nc = nc
        self.q = {e: [] for e in ENGS}
        self.seen = {e: {} for e in ENGS}
        self.nidx = {e: 0 for e in ENGS}
        self.ecount = {e: 0 for e in ENGS}
        self.esem = {e: stack.enter_context(nc.semaphore("s_" + e)) for e in ENGS}
        self.dsems = [stack.enter_context(nc.semaphore("d%d" % i)) for i in range(NDSEM)]
        self.dval = [0] * NDSEM
        self.dlast = [None] * NDSEM
        self.keymap = {}
        self.ninstr = 0

    def add(self, eng, fn, reads=(), writes=(), dma_key=None):
        ins = Ins(eng, fn)
        ins.idx = self.nidx[eng]
        self.nidx[eng] += 1
        cand = []
        for t in reads:
            if t.w is not None:
                cand.append(t.w)
        for t in writes:
            if t.w is not None:
                cand.append(t.w)
            cand.extend(t.rs.values())
        if dma_key is not None:
            ins.is_dma = True
            if dma_key not in self.keymap:
                assert len(self.keymap) < NDSEM, "out of dma sems"
                self.keymap[dma_key] = len(self.keymap)
            si = self.keymap[dma_key]
            if self.dlast[si] is not None:
                cand.append(self.dlast[si])
            self.dval[si] += 16
            self.dlast[si] = ins
            ins.dsem = si
            ins.dval = self.dval[si]
        seen = self.seen[eng]
        best = {}
        for p in cand:
            if p is ins:
                continue
            if p.is_dma:
                k = ("d", p.dsem)
                v = p.dval
            else:
                if p.eng == eng and eng == "pe":
                    continue
                k = p.eng
                v = p.idx
            if seen.get(k, -1) >= v:
                continue
            if k not in best or (best[k].dval if p.is_dma else best[k].idx) < v:
                best[k] = p
        for k, p in best.items():
            seen[k] = p.dval if p.is_dma else p.idx
            if not p.is_dma:
                p.need_inc = True
            ins.deps.append(p)
        self.q[eng].append(ins)
        rk = ("d", ins.dsem, ins.dval) if ins.is_dma else eng
        for t in reads:
            t.rs[rk] = ins
        for t in writes:
            t.w = ins
            t.rs = {}
        return ins

    def flush(self):
        nc = self.nc
        last = {}
        for e in ENGS:
            for ins in reversed(self.q[e]):
                if not ins.is_dma:
                    ins.need_inc = True
                    last[e] = ins
                    break
        for e in ENGS:
            c = self.ecount[e]
            for ins in self.q[e]:
                if ins.need_inc and not ins.is_dma:
                    c += 1
                    ins.val = c
            self.ecount[e] = c
        used_d = [i for i in range(NDSEM) if self.dlast[i] is not None]
        prog = self

        def run(engname, eh):
            for ins in prog.q[engname]:
                for p in ins.deps:
                    if p.is_dma:
                        eh.wait_ge(prog.dsems[p.dsem], p.dval)
                    else:
                        eh.wait_ge(prog.esem[p.eng], p.val)
                r = ins.fn(eh)
                if ins.is_dma:
                    r.then_inc(prog.dsems[ins.dsem], 16)
                elif ins.need_inc:
                    r.then_inc(prog.esem[engname], 1)
                prog.ninstr += 1
            for e2 in ENGS:
                if e2 != engname and prog.ecount[e2] > 0:
                    eh.wait_ge(prog.esem[e2], prog.ecount[e2])
            for i in used_d:
                eh.wait_ge(prog.dsems[i], prog.dval[i])

        with nc.Block() as block:
            @block.tensor
            def _(eh):
                run("pe", eh)

            @block.scalar
            def _(eh):
                run("act", eh)

            @block.vector
            def _(eh):
                run("dve", eh)

            @block.gpsimd
            def _(eh):
                run("pool", eh)

            @block.sync
            def _(eh):
                run("sp", eh)

        for e in ENGS:
            for e2 in ENGS:
                self.seen[e][e2] = self.nidx[e2] - 1
            for i in range(NDSEM):
                self.seen[e][("d", i)] = self.dval[i]
            self.q[e] = []
        self.keymap = {}
        self.dlast = [None] * NDSEM


def lambda_init(l):
    return 0.8 - 0.6 * math.exp(-0.3 * l)


def build_program(phases, ext_in, ext_out, fused=False):
    nc = bass.Bass("TRN2", target_bir_lowering=False)
    st = contextlib.ExitStack()
    with st:
        P = Prog(nc, st)

        def din(name, shape, dt=F32):
            return nc.dram_tensor(name, list(shape), dt, kind="ExternalInput").ap()

        SH_L = {"w_ada": [D, 9 * D], "b_adaT": [128, 72], "ffn1_w1": [D, FF], "ffn1_w3": [D, FF], "ffn2_w1": [D, FF],
                "ffn2_w3": [D, FF], "ffn1_w2": [FF, D], "ffn2_w2": [FF, D], "w_in_ext": [D, WEXT], "w_out": [D, D],
                "nat": [128, 4 * 5 * 768], "gm_wsT": [4, 128, 128], "gm_bs": [4, 128], "gm_norm": [256],
                "da_lq1": [64], "da_lk1": [64], "da_lq2": [64], "da_lk2": [64], "da_subln": [128]}
        SH_G = {"cT": [128, KC, 2], "fnT": [128, KC], "cosT": [128, TL], "sinT": [128, TL]}
        used = {}

        class _LW:
            def __init__(self, k):
                self.k = k

            def __getitem__(self, l):
                name = "%s_%d" % (self.k, l)
                if name not in used:
                    used[name] = din(name, SH_L[self.k])
                return used[name]

        class _WD:
            def __getitem__(self, k):
                if k in SH_L:
                    return _LW(k)
                if k not in used:
                    used[k] = din(k, SH_G[k])
                return used[k]

        W = _WD()

        ISHAPES = {
            "xs": ([D, TT], F32),
            "qaT": ([256, TT], BF16),
            "uT": ([256, TT], BF16),
            "qdT": ([512, TT], BF16),
            "vbn": ([TT, 256], BF16),
            "kv_own": ([R_END, TL], BF16),
            "kdTc": ([512, TC], BF16),
            "vdc": ([TC, 512], BF16),
            "kaTc": ([256, TC], BF16),
            "vac": ([TC, 256], BF16),
            "mixT": ([D, TT], BF16),
            "outT": ([D, TL], F32),
            "modD": ([128, DEPTH * 144], F32),
        }
        I = {}
        for k, (shp, dt) in ISHAPES.items():
            I[k] = nc.dram_tensor(k, shp, dt, kind="Internal").ap()
        EI = {k: nc.dram_tensor(k + "_i", ISHAPES[k][0], ISHAPES[k][1], kind="ExternalInput").ap() for k in ext_in}
        EO = {k: nc.dram_tensor(k + "_o", ISHAPES[k][0], ISHAPES[k][1], kind="ExternalOutput").ap() for k in ext_out}
        TD = TF()

        modT = st.enter_context(nc.sbuf_tensor("modT", [128, DEPTH, 72, 2], F32))
        sc1T = st.enter_context(nc.sbuf_tensor("sc1T", [128, DEPTH, 72, 2], F32))
        ghT = st.enter_context(nc.sbuf_tensor("ghT", [128, DEPTH, 72, 2], F32))
        ones_bf = st.enter_context(nc.sbuf_tensor("ones_bf", [128, 128], BF16))
        PSP = [st.enter_context(nc.psum_tensor("psp%d" % i, [128, 1024], F32)) for i in range(4)]
        PS = [PSP[i // 2][:, (i % 2) * 512:(i % 2 + 1) * 512] for i in range(8)]
        TP = [TF() for _ in range(8)]
        t_mod = Tok()
        t_ones = Tok()

        def MM(out, lhsT, rhs, s, e_, R, Wt, tp=None):
            if tp is None:
                P.add("pe", lambda e: e.matmul(out, lhsT=lhsT, rhs=rhs, start=s, stop=e_), R, Wt)
            else:
                P.add("pe", lambda e: e.matmul(out, lhsT=lhsT, rhs=rhs, start=s, stop=e_, tile_position=tp), R, Wt)

        def ACT(out, in_, func, R, Wt, **kw):
            P.add("act", lambda e: e.activation(out=out, in_=in_, func=func, **kw), R, Wt)

        def TTo(eng, out, a, b, op, R, Wt):
            P.add(eng, lambda e: e.tensor_tensor(out=out, in0=a, in1=b, op=op), R, Wt)

        def TS(eng, out, a, s1, s2, op0, op1, R, Wt):
            if op1 is None:
                P.add(eng, lambda e: e.tensor_scalar(out=out, in0=a, scalar1=s1, scalar2=None, op0=op0), R, Wt)
            else:
                P.add(eng, lambda e: e.tensor_scalar(out=out, in0=a, scalar1=s1, scalar2=s2, op0=op0, op1=op1), R, Wt)

        def STT(out, in0, scalar, in1, op0, op1, R, Wt):
            P.add("dve", lambda e: e.scalar_tensor_tensor(out=out, in0=in0, scalar=scalar, in1=in1, op0=op0, op1=op1), R, Wt)

        def RECIP(out, in_, R, Wt):
            P.add("dve", lambda e: e.reciprocal(out=out, in_=in_), R, Wt)

        def DMA(qn, out, in_, R, Wt, key):
            P.add(qn, lambda e: e.dma_start(out=out, in_=in_), R, Wt, dma_key=key)

        class Phase:
            cnt = [0]

            def __init__(self):
                self.st = contextlib.ExitStack()
                self.n = 0
                Phase.cnt[0] += 1
                self.pid = Phase.cnt[0]

            def __enter__(self):
                self.st.__enter__()
                return self

            def sb(self, shape, dt):
                self.n += 1
                return self.st.enter_context(nc.sbuf_tensor("t%d_%d" % (self.pid, self.n), list(shape), dt))

            def __exit__(self, *a):
                P.flush()
                return self.st.__exit__(*a)

        TILES = [(i * 1024, 1024, 0) for i in range(TL // 1024)] + [(TL, TC, 1)]

        with Phase() as ph:
            P.add("dve", lambda e: e.memset(ones_bf[:], 1.0), (), [t_ones])
            for k in ext_in:
                shp = ISHAPES[k][0]
                nr = shp[0]
                step = max(1, nr // 4)
                for r0 in range(0, nr, step):
                    r1 = min(nr, r0 + step)
                    DMA("sp", I[k][r0:r1, :], EI[k][r0:r1, :], (), [TD[k, "cp", r0]], "cp")

        def norm_mod(ph, bufs, xt, t_xt, h, t_h, s, cw, l, gS, gC, j):
            sqb, t_sqb, sd, t_sd, rstd, t_rstd, tmp, t_tmp = bufs
            c0 = s * 512
            for kc in range(KC):
                b = kc % 2
                ACT(sqb[b][:, :cw], xt[:, kc, c0:c0 + cw], AF.Square, [t_xt[kc, s]], [t_sqb[b]])
                MM(PS[0][:, :cw], ones_bf[:], sqb[b][:, :cw], kc == 0, kc == KC - 1, [t_sqb[b], t_ones], [TP[0][0]])
            ACT(sd[:, :cw], PS[0][:, :cw], AF.Sqrt, [TP[0][0]], [t_sd], scale=1.0 / D, bias=eps_ap[:, 0:1])
            RECIP(rstd[:, :cw], sd[:, :cw], [t_sd], [t_rstd])
            for kc in range(KC):
                b = kc % 2
                TTo("dve", tmp[b][:, :cw], xt[:, kc, c0:c0 + cw], rstd[:, :cw], ALU.mult, [t_xt[kc, s], t_rstd], [t_tmp[b]])
                ACT(h[:, kc, c0:c0 + cw], tmp[b][:, :cw], AF.Identity, [t_tmp[b], t_mod], [t_h[kc, s]],
                    scale=sc1T[:, l, gC * 8 + kc, j:j + 1], bias=modT[:, l, gS * 8 + kc, j:j + 1])

        def norm_bufs(ph):
            sqb = [ph.sb([128, 512], BF16) for _ in range(2)]
            sd = ph.sb([128, 512], F32)
            rstd = ph.sb([128, 512], F32)
            tmp = [ph.sb([128, 512], F32) for _ in range(2)]
            return (sqb, [Tok(), Tok()], sd, Tok(), rstd, Tok(), tmp, [Tok(), Tok()])

        xs_v = I["xs"].rearrange("(kc p) t -> p kc t", p=128)

        eps_ap = st.enter_context(nc.sbuf_tensor("eps_ap", [128, 1], F32))

        with Phase() as ph:
            P.add("dve", lambda e: e.memset(eps_ap[:], EPS), (), [Tok()])

        def phase_mod(l):
            with Phase() as ph:
                cs = ph.sb([128, KC, 2], F32)
                csl = ph.sb([128, KC, 2], F32)
                wa = [ph.sb([128, KC, 1024], F32) for _ in range(2)]
                bT = ph.sb([128, 72], F32)
                t_cs, t_csl, t_b = Tok(), Tok(), Tok()
                t_wa = [Tok(), Tok()]
                DMA("sp", cs[:], W["cT"], (), [t_cs], "cs")
                DMA("sp", bT[:], W["b_adaT"][l], (), [t_b], "bT")
                ACT(csl[:], cs[:], AF.Silu, [t_cs], [t_csl])
                pm = PS[1][:, 0:144].rearrange("p (g j) -> p g j", j=2)
                for g in range(9):
                    b = g % 2
                    DMA("sp", wa[b][:], W["w_ada"][l][:, g * 1024:(g + 1) * 1024].rearrange("(kc p) c -> p kc c", p=128),
                        (), [t_wa[b]], ("wa", b))
                    for m in range(8):
                        for kc in range(KC):
                            MM(pm[:, g * 8 + m, :], wa[b][:, kc, m * 128:(m + 1) * 128], csl[:, kc, :], kc == 0, kc == KC - 1,
                               [t_wa[b], t_csl], [TP[1][0]])
                for j in range(2):
                    TTo("dve", modT[:, l, :, j], pm[:, :, j], bT[:], ALU.add, [TP[1][0], t_b], [t_mod])
                TS("dve", sc1T[:, l, :, :], modT[:, l, :, :], 1.0, None, ALU.add, None, [t_mod], [t_mod])
                TS("dve", ghT[:, l, :, :], modT[:, l, :, :], 0.5, None, ALU.mult, None, [t_mod], [t_mod])
                DMA("sp", I["modD"][:, l * 144:(l + 1) * 144], modT[:, l, :, :].rearrange("p g j -> p (g j)"), [t_mod], [TD["modD", l]], "modst")

        def phase_modld():
            with Phase() as ph:
                DMA("sp", modT[:].rearrange("p l g j -> p (l g j)"), I["modD"], (), [t_mod], "modld")
                TS("dve", sc1T[:].rearrange("p l g j -> p (l g j)"), modT[:].rearrange("p l g j -> p (l g j)"), 1.0, None, ALU.add, None, [t_mod], [t_mod])
                TS("dve", ghT[:].rearrange("p l g j -> p (l g j)"), modT[:].rearrange("p l g j -> p (l g j)"), 0.5, None, ALU.mult, None, [t_mod], [t_mod])

        def phase_ffn(l, which, tiles):
            w1 = W["ffn%d_w1" % which][l]
            w3 = W["ffn%d_w3" % which][l]
            w2 = W["ffn%d_w2" % which][l]
            gS, gC, gG = (0, 1, 2) if which == 1 else (6, 7, 8)
            with Phase() as ph:
                xts = [ph.sb([128, KC, 1024], F32) for _ in range(2)]
                h = ph.sb([128, KC, 1024], BF16)
                a = ph.sb([128, NJ, 1024], BF16)
                w13 = [ph.sb([128, KC, 256], BF16) for _ in range(3)]
                w2c = [ph.sb([128, NJ, 128], BF16) for _ in range(3)]
                sil = [ph.sb([128, 512], F32) for _ in range(2)]
                nb = norm_bufs(ph)
                t_xts, t_h, t_a = [TF(), TF()], TF(), TF()
                t_w1, t_w3, t_w2, t_sil = TF(), TF(), TF(), TF()
                cnt = 0

                def geom(idx):
                    t0, T, j = tiles[idx]
                    return t0, T, j, (T + 511) // 512, min(T, 512)

                def load(idx):
                    t0, T, j, nsub, cw = geom(idx)
                    DMA("sp", xts[idx % 2][:, :, 0:T], xs_v[:, :, t0:t0 + T], [TD["xs", t0]],
                        [t_xts[idx % 2][kc, s] for kc in range(KC) for s in range(nsub)], ("xld", idx % 2))

                def norm(idx, s):
                    t0, T, j, nsub, cw = geom(idx)
                    if s < nsub:
                        norm_mod(ph, nb, xts[idx % 2], t_xts[idx % 2], h, t_h, s, cw, l, gS, gC, j)

                load(0)
                norm(0, 0)
                norm(0, 1)
                for idx in range(len(tiles)):
                    t0, T, j, nsub, cw = geom(idx)
                    xt, t_xt = xts[idx % 2], t_xts[idx % 2]
                    nxt = idx + 1 < len(tiles)
                    if nxt:
                        load(idx + 1)
                    for jj in range(NJ):
                        sl = jj % 3
                        DMA("pool", w13[sl][:, :, 0:128], w1[:, jj * 128:(jj + 1) * 128].rearrange("(kc p) c -> p kc c", p=128),
                            (), [t_w1[sl]], ("w1", sl))
                        DMA("pool", w13[sl][:, :, 128:256], w3[:, jj * 128:(jj + 1) * 128].rearrange("(kc p) c -> p kc c", p=128),
                            (), [t_w3[sl]], ("w3", sl))
                        for s in range(nsub):
                            b = cnt % 2
                            cnt += 1
                            c0 = s * 512
                            for kc in range(KC):
                                MM(PS[1 + b][:, :cw], w13[sl][:, kc, 0:128], h[:, kc, c0:c0 + cw], kc == 0, kc == KC - 1,
                                   [t_w1[sl], t_h[kc, s]], [TP[1 + b][0]])
                            for kc in range(KC):
                                MM(PS[3 + b][:, :cw], w13[sl][:, kc, 128:256], h[:, kc, c0:c0 + cw], kc == 0, kc == KC - 1,
                                   [t_w3[sl], t_h[kc, s]], [TP[3 + b][0]])
                            ACT(sil[b][:, :cw], PS[1 + b][:, :cw], AF.Silu, [TP[1 + b][0]], [t_sil[b]])
                            TTo("dve", a[:, jj, c0:c0 + cw], sil[b][:, :cw], PS[3 + b][:, :cw], ALU.mult,
                                [t_sil[b], TP[3 + b][0]], [t_a[jj, s]])
                    for m in range(KC):
                        sl = m % 3
                        DMA("pool", w2c[sl][:], w2[:, m * 128:(m + 1) * 128].rearrange("(j p) c -> p j c", p=128),
                            (), [t_w2[sl]], ("w2", sl))
                        if nxt and m == 2:
                            norm(idx + 1, 0)
                        if nxt and m == 5:
                            norm(idx + 1, 1)
                        for s in range(nsub):
                            b = cnt % 2
                            cnt += 1
                            c0 = s * 512
                            for jj in range(NJ):
                                MM(PS[5 + b][:, :cw], w2c[sl][:, jj, :], a[:, jj, c0:c0 + cw], jj == 0, jj == NJ - 1,
                                   [t_w2[sl], t_a[jj, s]], [TP[5 + b][0]])
                            STT(xt[:, m, c0:c0 + cw], PS[5 + b][:, :cw], ghT[:, l, gG * 8 + m, j:j + 1], xt[:, m, c0:c0 + cw],
                                ALU.mult, ALU.add, [TP[5 + b][0], t_mod, t_xt[m, s]], [t_xt[m, s]])
                    DMA("sp", xs_v[:, :, t0:t0 + T], xt[:, :, 0:T],
                        [t_xt[kc, s] for kc in range(KC) for s in range(nsub)], [TD["xs", t0]], ("xst", idx % 2))

        def phase_proj(l):
            win = W["w_in_ext"][l]
            kvo = I["kv_own"]
            vd_lat = kvo[R_VD:R_KA, :].rearrange("r (a f) -> (r a) f", f=512)
            va_lat = kvo[R_VA:R_END, :].rearrange("r (a f) -> (r a) f", f=256)
            with Phase() as ph:
                wsb = ph.sb([128, KC, WEXT], BF16)
                xt = ph.sb([128, KC, 1024], F32)
                h = ph.sb([128, KC, 1024], BF16)
                ct = ph.sb([128, 1024], F32)
                sn = ph.sb([128, 1024], F32)
                stg = [ph.sb([128, 1024], BF16) for _ in range(4)]
                r1 = [ph.sb([128, 512], F32) for _ in range(2)]
                r2 = [ph.sb([128, 512], F32) for _ in range(2)]
                gnb = ph.sb([128, 256], F32)
                gl = [ph.sb([128, 256], F32) for _ in range(2)]
                sq = [ph.sb([128, 256], F32) for _ in range(2)]
                gn = [ph.sb([128, 256], F32) for _ in range(2)]
                ss4 = [ph.sb([128, 4], F32) for _ in range(2)]
                sd4 = [ph.sb([128, 4], F32) for _ in range(2)]
                r4 = [ph.sb([128, 4], F32) for _ in range(2)]
                sva = [ph.sb([128, 256], BF16) for _ in range(2)]
                svb = [ph.sb([128, 256], BF16) for _ in range(2)]
                svd = [ph.sb([128, 512], BF16) for _ in range(2)]
                nb = norm_bufs(ph)
                t_w, t_xt, t_h, t_stg, t_r1, t_r2 = TF(), TF(), TF(), TF(), TF(), TF()
                t_ct, t_sn, t_gnb = Tok(), Tok(), Tok()
                tv = TF()
                NWP = 8
                wp = WEXT // NWP
                for i in range(NWP):
                    DMA("pool", wsb[:, :, i * wp:(i + 1) * wp], win[:, i * wp:(i + 1) * wp].rearrange("(kc p) c -> p kc c", p=128),
                        (), [t_w[i]], ("win", i))
                tw_all = [t_w[i] for i in range(NWP)]

                def twc(col0, ncols):
                    return [t_w[i] for i in range(col0 // wp, (col0 + ncols - 1) // wp + 1)]

                DMA("sp", gnb[:], W["gm_norm"][l].partition_broadcast(128), (), [t_gnb], "gnb")
                cnt = 0
                sgi = 0
                vcnt = 0
                for (t0, T, j) in TILES:
                    nsub = (T + 511) // 512
                    cw = min(T, 512)
                    DMA("sp", xt[:, :, 0:T], xs_v[:, :, t0:t0 + T], [TD["xs", t0]],
                        [t_xt[kc, s] for kc in range(KC) for s in range(nsub)], "xld")
                    if j == 0:
                        DMA("sp", ct[:, 0:T], W["cosT"][:, t0:t0 + T], (), [t_ct], "ct")
                        DMA("sp", sn[:, 0:T], W["sinT"][:, t0:t0 + T], (), [t_sn], "sn")
                    for s in range(nsub):
                        norm_mod(ph, nb, xt, t_xt, h, t_h, s, cw, l, 3, 4, j)
                    fm = []
                    for c in range(2):
                        fm.append(("copy", C_QA + c * 128, None, I["qaT"][c * 128:(c + 1) * 128, t0:t0 + T], ("qaT", t0)))
                    for c in range(2):
                        dst = kvo[R_KA + c * 128:R_KA + (c + 1) * 128, t0:t0 + T] if j == 0 else I["kaTc"][c * 128:(c + 1) * 128, :]
                        fm.append(("copy", C_KA + c * 128, None, dst, ("ka", t0)))
                    for c in range(2):
                        fm.append(("gelu", C_U + c * 128, None, I["uT"][c * 128:(c + 1) * 128, t0:t0 + T], ("uT", t0)))
                    for c in range(4):
                        fm.append(("rope" if j == 0 else "copy", C_QD + c * 128, C_QDS + c * 128,
                                   I["qdT"][c * 128:(c + 1) * 128, t0:t0 + T], ("qdT", t0)))
                    for c in range(4):
                        dst = kvo[R_KD + c * 128:R_KD + (c + 1) * 128, t0:t0 + T] if j == 0 else I["kdTc"][c * 128:(c + 1) * 128, :]
                        fm.append(("rope" if j == 0 else "copy", C_KD + c * 128, C_KDS + c * 128, dst, ("kd", t0)))
                    for (kind, col, cols, dst, dtk) in fm:
                        sg = stg[sgi % 4]
                        tsg = t_stg[sgi % 4]
                        sgi += 1
                        for s in range(nsub):
                            b = cnt % 2
                            cnt += 1
                            c0 = s * 512
                            for kc in range(KC):
                                MM(PS[1 + b][:, :cw], wsb[:, kc, col:col + 128], h[:, kc, c0:c0 + cw], kc == 0, kc == KC - 1,
                                   twc(col, 128) + [t_h[kc, s]], [TP[1 + b][0]])
                            if kind == "copy":
                                ACT(sg[:, c0:c0 + cw], PS[1 + b][:, :cw], AF.Copy, [TP[1 + b][0]], [tsg])
                            elif kind == "gelu":
                                ACT(sg[:, c0:c0 + cw], PS[1 + b][:, :cw], AF.Gelu, [TP[1 + b][0]], [tsg])
                            else:
                                for kc in range(KC):
                                    MM(PS[3 + b][:, :cw], wsb[:, kc, cols:cols + 128], h[:, kc, c0:c0 + cw], kc == 0, kc == KC - 1,
                                       twc(cols, 128) + [t_h[kc, s]], [TP[3 + b][0]])
                                TTo("dve", r1[b][:, :cw], PS[1 + b][:, :cw], ct[:, c0:c0 + cw], ALU.mult, [TP[1 + b][0], t_ct], [t_r1[b]])
                                TTo("dve", r2[b][:, :cw], PS[3 + b][:, :cw], sn[:, c0:c0 + cw], ALU.mult, [TP[3 + b][0], t_sn], [t_r2[b]])
                                TTo("pool", sg[:, c0:c0 + cw], r1[b][:, :cw], r2[b][:, :cw], ALU.add, [t_r1[b], t_r2[b]], [tsg])
                        DMA("sp", dst, sg[:, 0:T], [tsg], [TD[dtk]], ("stg", (sgi - 1) % 4))
                    for tb in range(T // 128):
                        b = vcnt % 2
                        vcnt += 1
                        tk0 = tb * 128
                        hs = [t_h[kc, tk0 // 512] for kc in range(KC)]
                        for kc in range(KC):
                            MM(PS[5 + b][:, :], h[:, kc, tk0:tk0 + 128], wsb[:, kc, C_VAB:C_VAB + 512], kc == 0, kc == KC - 1,
                               twc(C_VAB, 512) + [hs[kc]], [TP[5 + b][0]])
                        ACT(sva[b][:], PS[5 + b][:, 0:256], AF.Copy, [TP[5 + b][0]], [tv["sva", b]])
                        dst = va_lat[t0 + tk0:t0 + tk0 + 128, :] if j == 0 else I["vac"][tk0:tk0 + 128, :]
                        DMA("sp", dst, sva[b][:], [tv["sva", b]], [TD["va", t0]], ("sva", b))
                        ACT(gl[b][:], PS[5 + b][:, 256:512], AF.Gelu, [TP[5 + b][0]], [tv["gl", b]])
                        TTo("dve", sq[b][:], gl[b][:], gl[b][:], ALU.mult, [tv["gl", b]], [tv["sq", b]])
                        P.add("dve", lambda e, b=b: e.reduce_sum(out=ss4[b][:], in_=sq[b][:].rearrange("p (g w) -> p g w", w=64), axis=AX.X),
                              [tv["sq", b]], [tv["ss4", b]])
                        ACT(sd4[b][:], ss4[b][:], AF.Sqrt, [tv["ss4", b]], [tv["sd4", b]], scale=1.0 / 64, bias=eps_ap[:, 0:1])
                        RECIP(r4[b][:], sd4[b][:], [tv["sd4", b]], [tv["r4", b]])
                        for g in range(4):
                            TS("dve", gn[b][:, g * 64:(g + 1) * 64], gl[b][:, g * 64:(g + 1) * 64], r4[b][:, g:g + 1], None, ALU.mult, None,
                               [tv["gl", b], tv["r4", b]], [tv["gn", b, g]])
                        TTo("dve", svb[b][:], gn[b][:], gnb[:], ALU.mult, [tv["gn", b, g] for g in range(4)] + [t_gnb], [tv["svb", b]])
                        DMA("sp", I["vbn"][t0 + tk0:t0 + tk0 + 128, :], svb[b][:], [tv["svb", b]], [TD["vbn", t0]], ("svb", b))
                        for kc in range(KC):
                            MM(PS[3 + b][:, :], h[:, kc, tk0:tk0 + 128], wsb[:, kc, C_VD:C_VD + 512], kc == 0, kc == KC - 1,
                               twc(C_VD, 512) + [hs[kc]], [TP[3 + b][0]])
                        ACT(svd[b][:], PS[3 + b][:, :], AF.Copy, [TP[3 + b][0]], [tv["svd", b]])
                        dst = vd_lat[t0 + tk0:t0 + tk0 + 128, :] if j == 0 else I["vdc"][tk0:tk0 + 128, :]
                        DMA("sp", dst, svd[b][:], [tv["svd", b]], [TD["vd", t0]], ("svd", b))

        def phase_exchange():
            if not fused:
                return
            with Phase() as ph:
                P.add("pool", lambda e: e.collective_compute(
                    "AllGather", ALU.bypass, replica_groups=[[0, 1], [2, 3], [4, 5], [6, 7]],
                    ins=[I["kv_own"]], outs=[I["kv_all"]]), (), [TD["kv_all"]], dma_key="cc")

        def phase_gm(l, with_ctx):
            with Phase() as ph:
                wsT = ph.sb([128, 4, 128], BF16)
                bsb = ph.sb([64, 4, 4, 128], F32)
                vb = [ph.sb([128, 4, 256], BF16) for _ in range(2)]
                us = [ph.sb([64, 4, 512], BF16) for _ in range(2)]
                tq = [ph.sb([64, 512], F32) for _ in range(2)]
                ob = [ph.sb([64, 4, 512], BF16) for _ in range(2)]
                t_ws, t_bs = Tok(), TF()
                tg = TF()
                DMA("pool", wsT[:], W["gm_wsT"][l].rearrange("g q p -> q g p"), (), [t_ws], "wsT")
                for g in range(4):
                    for c in range(4):
                        DMA("sp", bsb[:, g, c, :], W["gm_bs"][l][g].partition_broadcast(64), (), [t_bs[g, c]], ("bsb", (g * 4 + c) % 4))
                tbs = [t_bs[g, c] for g in range(4) for c in range(4)]
                uT_v = I["uT"].rearrange("(g w) t -> w g t", w=64)
                mx_v = I["mixT"][256:512, :].rearrange("(g w) t -> w g t", w=64)
                tiles = [(i * 512, 512) for i in range(TL // 512)] + ([(TL, TC)] if with_ctx else [])
                cnt = 0
                for ti, (t0, T) in enumerate(tiles):
                    bb = ti % 2
                    nck = T // 128
                    DMA("sp", vb[bb][:, 0:nck, :], I["vbn"][t0:t0 + T, :].rearrange("(c p) f -> p c f", p=128), (), [tg["vb", bb]], ("vb", bb))
                    DMA("sp", us[bb][:, :, 0:T], uT_v[:, :, t0:t0 + T], (), [tg["us", bb]], ("us", bb))
                    for g in range(4):
                        b = cnt % 2
                        cnt += 1
                        for c in range(nck):
                            MM(PS[1 + b][0:64, c * 128:(c + 1) * 128], vb[bb][:, c, g * 64:(g + 1) * 64], wsT[:, g, :], True, True,
                               [tg["vb", bb], t_ws], [TP[1 + b][0]])
                        TTo("dve", tq[b][:, 0:T], PS[1 + b][0:64, 0:T], bsb[:, g, 0:nck, :].rearrange("p c q -> p (c q)"), ALU.add,
                            [TP[1 + b][0]] + tbs, [tg["tq", b]])
                        TTo("dve", ob[bb][:, g, 0:T], tq[b][:, 0:T], us[bb][:, g, 0:T], ALU.mult, [tg["tq", b], tg["us", bb]], [tg["ob", bb, g]])
                    DMA("sp", mx_v[:, :, t0:t0 + T], ob[bb][:, :, 0:T], [tg["ob", bb, g] for g in range(4)], [TD["mixT", "b", t0]], ("ob", bb))

        def phase_na(l, with_ctx):
            kvo = I["kv_own"]
            NP = TL // 128
            NK = NP + 4
            with Phase() as ph:
                kT = ph.sb([64, NK * 128], BF16)
                vA = ph.sb([128, NK, 64], BF16)
                kTc = ph.sb([64, 256], BF16)
                vAc = ph.sb([128, 2, 64], BF16)
                tab = ph.sb([128, 5, 768], F32)
                qT = ph.sb([64, TT], BF16)
                ones64 = ph.sb([128, 64], BF16)
                sc = [ph.sb([128, 768], F32) for _ in range(2)]
                pr = [ph.sb([128, 1024], BF16) for _ in range(2)]
                rz = [ph.sb([64, 256], F32) for _ in range(2)]
                oa = [ph.sb([64, 512], BF16) for _ in range(2)]
                tn = TF()
                P.add("pool", lambda e: e.memset(ones64[:], 1.0), (), [tn["ones"]])
                P.add("pool", lambda e: e.memset(kT[:, 0:256], 0.0), (), [tn["kpad", 0]])
                P.add("pool", lambda e: e.memset(kT[:, (NK - 2) * 128:NK * 128], 0.0), (), [tn["kpad", 1]])
                P.add("pool", lambda e: e.memset(vA[:, 0:2, :], 0.0), (), [tn["vpad", 0]])
                P.add("pool", lambda e: e.memset(vA[:, NK - 2:NK, :], 0.0), (), [tn["vpad", 1]])
                va_v = kvo[R_VA:R_END, :].rearrange("r (a f) -> (r a) f", f=256)
                tk = [tn["k"], tn["kpad", 0], tn["kpad", 1]]
                tvv = [tn["v"], tn["vpad", 0], tn["vpad", 1]]
                cnt = 0
                ocnt = 0
                for hh in range(4):
                    hr = slice(hh * 64, (hh + 1) * 64)
                    DMA("sp", kT[:, 256:(NK - 2) * 128], kvo[R_KA + hh * 64:R_KA + (hh + 1) * 64, :], (), [tn["k"]], "k1")
                    DMA("sp", vA[:, 2:NK - 2, :], va_v[:, hr].rearrange("(c p) f -> p c f", p=128), (), [tn["v"]], "v1")
                    DMA("sp", kTc[:], I["kaTc"][hr, :], (), [tn["kc"]], "kc")
                    DMA("sp", vAc[:], I["vac"][:, hr].rearrange("(c p) f -> p c f", p=128), (), [tn["vc"]], "vc")
                    DMA("sp", tab[:].rearrange("p c n -> p (c n)"), W["nat"][l][:, hh * 3840:(hh + 1) * 3840], (), [tn["tab"]], "tab")
                    DMA("sp", qT[:], I["qaT"][hr, :], (), [tn["q"]], "q")
                    pend = None
                    for i in range(NP):
                        cls = 0 if i == 0 else 1 if i == 1 else 3 if i == NP - 2 else 4 if i == NP - 1 else 2
                        s0 = min(i, NP - 2)
                        if i % 4 == 0:
                            oi = ocnt % 2
                            ocnt += 1
                        ob = oa[oi]
                        b = cnt % 2
                        cnt += 1
                        A, B_ = PS[0 + b], PS[2 + b]
                        q_ap = qT[:, i * 128:(i + 1) * 128]
                        for c in range(6):
                            dst = A[:, c * 128:(c + 1) * 128] if c < 4 else B_[:, (c - 4) * 128:(c - 3) * 128]
                            MM(dst, kT[:, (s0 + c) * 128:(s0 + c + 1) * 128], q_ap, True, True, tk + [tn["q"]],
                               [TP[0 + b][0] if c < 4 else TP[2 + b][0]])
                        for c in range(2):
                            MM(B_[:, (2 + c) * 128:(3 + c) * 128], kTc[:, c * 128:(c + 1) * 128], q_ap, True, True,
                               [tn["kc"], tn["q"]], [TP[2 + b][0]])
                        STT(sc[b][:, 0:512], A[:, :], 0.125, tab[:, cls, 0:512], ALU.mult, ALU.add, [TP[0 + b][0], tn["tab"]], [tn["sc", b, 0]])
                        STT(sc[b][:, 512:768], B_[:, 0:256], 0.125, tab[:, cls, 512:768], ALU.mult, ALU.add, [TP[2 + b][0], tn["tab"]], [tn["sc", b, 1]])
                        ACT(pr[b][:, 0:768], sc[b][:, :], AF.Exp, [tn["sc", b, 0], tn["sc", b, 1]], [tn["pr", b, 0]])
                        ACT(pr[b][:, 768:1024], B_[:, 256:512], AF.Exp, [TP[2 + b][0]], [tn["pr", b, 1]], scale=0.125)
                        def st2(i=i, b=b, ob=ob, oi=oi, s0=s0, hr=hr, hh=hh):
                            O, Z = PS[4 + b], PS[6 + b]
                            for c in range(8):
                                vch = vA[:, s0 + c, :] if c < 6 else vAc[:, c - 6, :]
                                MM(O[0:64, 0:128], vch, pr[b][:, c * 128:(c + 1) * 128], c == 0, c == 7,
                                   tvv + [tn["vc"], tn["pr", b, 0], tn["pr", b, 1]], [TP[4 + b][0]])
                            for c in range(8):
                                MM(Z[0:64, 0:128], ones64[:], pr[b][:, c * 128:(c + 1) * 128], c == 0, c == 7,
                                   [tn["ones"], tn["pr", b, 0], tn["pr", b, 1]], [TP[6 + b][0]])
                            RECIP(rz[b][:, 0:128], Z[0:64, 0:128], [TP[6 + b][0]], [tn["rz", b]])
                            TTo("dve", ob[:, (i % 4) * 128:(i % 4 + 1) * 128], O[0:64, 0:128], rz[b][:, 0:128], ALU.mult,
                                [TP[4 + b][0], tn["rz", b]], [tn["oa", oi, i % 4]])
                            if i % 4 == 3:
                                t0 = (i // 4) * 512
                                DMA("sp", I["mixT"][hr, t0:t0 + 512], ob[:, :], [tn["oa", oi, k] for k in range(4)],
                                    [TD["mixT", "a", hh, t0]], ("oa", oi))

                        if pend is not None:
                            pend()
                        pend = st2
                    pend()
                    pend = None
                    if with_ctx:
                        oi = ocnt % 2
                        ocnt += 1
                        ob = oa[oi]
                        b = cnt % 2
                        cnt += 1
                        A = PS[0 + b]
                        q_ap = qT[:, TL:TL + 256]
                        for c in range(2):
                            MM(A[:, c * 256:(c + 1) * 256], kTc[:, c * 128:(c + 1) * 128], q_ap, True, True, [tn["kc"], tn["q"]], [TP[0 + b][0]])
                        ACT(pr[b][:, 0:512], A[:, :], AF.Exp, [TP[0 + b][0]], [tn["pr", b, 0]], scale=0.125)
                        O, Z = PS[4 + b], PS[6 + b]
                        for c in range(2):
                            MM(O[0:64, 0:256], vAc[:, c, :], pr[b][:, c * 256:(c + 1) * 256], c == 0, c == 1,
                               [tn["vc"], tn["pr", b, 0]], [TP[4 + b][0]])
                        for c in range(2):
                            MM(Z[0:64, 0:256], ones64[:], pr[b][:, c * 256:(c + 1) * 256], c == 0, c == 1,
                               [tn["ones"], tn["pr", b, 0]], [TP[6 + b][0]])
                        RECIP(rz[b][:, 0:256], Z[0:64, 0:256], [TP[6 + b][0]], [tn["rz", b]])
                        TTo("dve", ob[:, 0:256], O[0:64, 0:256], rz[b][:, 0:256], ALU.mult, [TP[4 + b][0], tn["rz", b]],
                            [tn["oa", oi, k] for k in range(4)])
                        DMA("sp", I["mixT"][hr, TL:TL + 256], ob[:, 0:256], [tn["oa", oi, k] for k in range(4)],
                            [TD["mixT", "a", hh, TL]], ("oa", oi))

        def phase_da(l, with_ctx):
            kvo = I["kv_own"]
            li = lambda_init(l)
            with Phase() as ph:
                KT = [ph.sb([128, TL + TC], BF16) for _ in range(2)]
                VV = [ph.sb([128, TL // 128 + 2, 128], BF16) for _ in range(2)]
                QT = [ph.sb([128, TT], BF16) for _ in range(2)]
                pt = [ph.sb([128, 1024], BF16) for _ in range(6)]
                sel = ph.sb([64, 256], F32)
                rzs = ph.sb([64, 512], F32)
                lv = [ph.sb([128, 64], F32) for _ in range(4)]
                lp = [ph.sb([128, 64], F32) for _ in range(2)]
                ls = ph.sb([128, 2], F32)
                le = ph.sb([128, 2], F32)
                nlam = ph.sb([128, 1], F32)
                gsub = ph.sb([128, 1], F32)
                gs0 = ph.sb([128, 1], F32)
                e1 = [ph.sb([128, 512], F32) for _ in range(2)]
                e2 = [ph.sb([128, 512], F32) for _ in range(2)]
                eo = ph.sb([128, 512], F32)
                esq = ph.sb([128, 512], BF16)
                esd = ph.sb([128, 512], F32)
                ers = ph.sb([128, 512], F32)
                eout = [ph.sb([128, 512], BF16) for _ in range(2)]
                td = TF()
                for i, k in enumerate(("da_lq1", "da_lk1", "da_lq2", "da_lk2")):
                    DMA("sp", lv[i][:], W[k][l].partition_broadcast(128), (), [td["lv", i]], ("lv", i))
                for c in range(2):
                    TTo("dve", lp[c][:], lv[2 * c][:], lv[2 * c + 1][:], ALU.mult, [td["lv", 2 * c], td["lv", 2 * c + 1]], [td["lp", c]])
                    P.add("dve", lambda e, c=c: e.reduce_sum(out=ls[:, c:c + 1], in_=lp[c][:], axis=AX.X), [td["lp", c]], [td["ls", c]])
                ACT(le[:], ls[:], AF.Exp, [td["ls", 0], td["ls", 1]], [td["le"]])
                TTo("dve", nlam[:], le[:, 1:2], le[:, 0:1], ALU.subtract, [td["le"]], [td["nl0"]])
                TS("dve", nlam[:], nlam[:], -li, None, ALU.add, None, [td["nl0"]], [td["nlam"]])
                DMA("sp", gs0[:], W["da_subln"][l].rearrange("(p o) -> p o", o=1), (), [td["gs0"]], "gs0")
                TS("dve", gsub[:], gs0[:], 1.0 - li, None, ALU.mult, None, [td["gs0"]], [td["gsub"]])

                vsrc = kvo[R_VD:R_KA, :].rearrange("r (a f) -> (r a) f", f=512)
                NCH = TL // 128
                HC = NCH // 2

                P.add("dve", lambda e: e.memset(sel[:], 0.0), (), [td["sel"]])
                P.add("dve", lambda e: e.memset(sel[0:32, 0:128], 1.0 / 32), [td["sel"]], [td["sel"]])
                P.add("dve", lambda e: e.memset(sel[32:64, 128:256], 1.0 / 32), [td["sel"]], [td["sel"]])
                ecnt = 0
                pcnt = 0
                for hh in range(4):
                    hb = hh % 2
                    K_, V_, Q_ = KT[hb], VV[hb], QT[hb]
                    rows = slice(hh * 128, (hh + 1) * 128)
                    for r in range(2):
                        DMA("sp", K_[:, r * (TL // 2):(r + 1) * (TL // 2)], kvo[R_KD + hh * 128:R_KD + (hh + 1) * 128, r * (TL // 2):(r + 1) * (TL // 2)],
                            (), [td["K", hb, r]], ("K", hb, r))
                    DMA("sp", K_[:, TL:], I["kdTc"][rows, :], (), [td["K", hb, 2]], ("K", hb, 2))
                    for r in range(2):
                        DMA("sp", V_[:, r * HC:(r + 1) * HC, :],
                            vsrc[r * (TL // 2):(r + 1) * (TL // 2), hh * 128:(hh + 1) * 128].rearrange("(c p) f -> p c f", p=128),
                            (), [td["V", hb, r]], ("V", hb, r))
                    DMA("sp", V_[:, NCH:NCH + 2, :], I["vdc"][:, hh * 128:(hh + 1) * 128].rearrange("(c p) f -> p c f", p=128), (), [td["V", hb, 2]], ("V", hb, 2))
                    DMA("sp", Q_[:], I["qdT"][rows, :], (), [td["Q", hb]], ("Q", hb))
                    tK = [td["K", hb, i] for i in range(3)]
                    tV = [td["V", hb, i] for i in range(3)]
                    qtiles = [(i * 512, 512, list(range(NCH + 2))) for i in range(TL // 512)]
                    if with_ctx:
                        qtiles.append((TL, 256, [NCH, NCH + 1]))
                    for (q0, N, chunks) in qtiles:
                        nch = len(chunks)
                        SK = 2
                        pslots = {}

                        def emit_s(ci):
                            kc = chunks[ci]
                            sb_ = ci % 2
                            for c in range(2):
                                MM(PS[2 * sb_ + c][:, :N], K_[c * 64:(c + 1) * 64, kc * 128:(kc + 1) * 128], Q_[c * 64:(c + 1) * 64, q0:q0 + N],
                                   True, True, tK + [td["Q", hb]], [TP[2 * sb_ + c][0]])

                        def emit_e(ci):
                            nonlocal pcnt
                            sb_ = ci % 2
                            sl = pcnt % 6
                            pcnt += 1
                            pslots[ci] = sl
                            ACT(pt[sl][:, :].rearrange("p (c n) -> p c n", c=2)[:, :, 0:N],
                                PSP[sb_][:, :].rearrange("p (c n) -> p c n", c=2)[:, :, 0:N], AF.Exp,
                                [TP[2 * sb_][0], TP[2 * sb_ + 1][0]], [td["pt", sl]], scale=0.125)

                        def emit_pv(ci):
                            kc = chunks[ci]
                            sl = pslots[ci]
                            for c in range(2):
                                MM(PS[4 + c][:, :N], V_[:, kc, :], pt[sl][:, c * 512:c * 512 + N], ci == 0, ci == nch - 1, tV + [td["pt", sl]], [TP[4 + c][0]])

                        def emit_z(ci):
                            sl = pslots[ci]
                            for c in range(2):
                                MM(PS[6][c * 32:(c + 1) * 32, :N], ones_bf[:, 0:32], pt[sl][:, c * 512:c * 512 + N], ci == 0, ci == nch - 1,
                                   [t_ones, td["pt", sl]], [TP[6][c]], tp=(0, 32 * c))

                        npair = (nch + 1) // 2
                        for p_ in range(npair + 1):
                            if p_ < npair:
                                cur = [ci for ci in (2 * p_, 2 * p_ + 1) if ci < nch]
                                for ci in cur:
                                    emit_s(ci)
                                for ci in cur:
                                    emit_e(ci)
                            if p_ >= 1:
                                prv = [ci for ci in (2 * p_ - 2, 2 * p_ - 1) if ci < nch]
                                for ci in prv:
                                    emit_pv(ci)
                                for ci in prv:
                                    emit_z(ci)
                        eb = ecnt % 2
                        ecnt += 1
                        RECIP(rzs[:, :N], PS[6][0:64, :N], [TP[6][0], TP[6][1]], [td["rzs"]])
                        for c in range(2):
                            ebuf = e1 if c == 0 else e2
                            MM(PS[7][:, :N], sel[:, c * 128:(c + 1) * 128], rzs[:, :N], True, True, [td["sel"], td["rzs"]], [TP[7][0]])
                            ACT(ebuf[eb][:, :N], PS[7][:, :N], AF.Copy, [TP[7][0]], [td["e", c, eb]])
                            TTo("dve", ebuf[eb][:, :N], PS[4 + c][:, :N], ebuf[eb][:, :N], ALU.mult, [TP[4 + c][0], td["e", c, eb]], [td["e", c, eb]])
                        STT(eo[:, :N], e2[eb][:, :N], nlam[:, 0:1], e1[eb][:, :N], ALU.mult, ALU.add,
                            [td["e", 0, eb], td["e", 1, eb], td["nlam"]], [td["eo"]])
                        ACT(esq[:, :N], eo[:, :N], AF.Square, [td["eo"]], [td["esq"]])
                        MM(PS[0][:, :N], ones_bf[:], esq[:, :N], True, True, [t_ones, td["esq"]], [TP[0][0]])
                        ACT(esd[:, :N], PS[0][:, :N], AF.Sqrt, [TP[0][0]], [td["esd"]], scale=1.0 / 128, bias=eps_ap[:, 0:1])
                        RECIP(ers[:, :N], esd[:, :N], [td["esd"]], [td["ers"]])
                        STT(eout[eb][:, :N], eo[:, :N], gsub[:, 0:1], ers[:, :N], ALU.mult, ALU.mult, [td["eo"], td["ers"], td["gsub"]], [td["eout", eb]])
                        DMA("sp", I["mixT"][512 + hh * 128:512 + (hh + 1) * 128, q0:q0 + N], eout[eb][:, :N], [td["eout", eb]],
                            [TD["mixT", "c", hh, q0]], ("eout", eb))

        def phase_wout(l, tiles):
            with Phase() as ph:
                wo = ph.sb([128, KC, D], BF16)
                xt = ph.sb([128, KC, 1024], F32)
                mx = ph.sb([128, KC, 1024], BF16)
                t_wo, t_xt, t_mx = Tok(), TF(), Tok()
                DMA("pool", wo[:], W["w_out"][l].rearrange("(kc p) c -> p kc c", p=128), (), [t_wo], "wo")
                mx_v = I["mixT"].rearrange("(kc p) t -> p kc t", p=128)
                cnt = 0
                for (t0, T, j) in tiles:
                    nsub = (T + 511) // 512
                    cw = min(T, 512)
                    DMA("sp", xt[:, :, 0:T], xs_v[:, :, t0:t0 + T], [TD["xs", t0]], [t_xt[kc, s] for kc in range(KC) for s in range(nsub)], "xld")
                    DMA("sp", mx[:, :, 0:T], mx_v[:, :, t0:t0 + T], (), [t_mx], "mxld")
                    for m in range(KC):
                        for s in range(nsub):
                            b = cnt % 2
                            cnt += 1
                            c0 = s * 512
                            for kc in range(KC):
                                MM(PS[1 + b][:, :cw], wo[:, kc, m * 128:(m + 1) * 128], mx[:, kc, c0:c0 + cw], kc == 0, kc == KC - 1,
                                   [t_wo, t_mx], [TP[1 + b][0]])
                            STT(xt[:, m, c0:c0 + cw], PS[1 + b][:, :cw], modT[:, l, 5 * 8 + m, j:j + 1], xt[:, m, c0:c0 + cw],
                                ALU.mult, ALU.add, [TP[1 + b][0], t_mod, t_xt[m, s]], [t_xt[m, s]])
                    DMA("sp", xs_v[:, :, t0:t0 + T], xt[:, :, 0:T], [t_xt[kc, s] for kc in range(KC) for s in range(nsub)], [TD["xs", t0]], "xst")

        def phase_final():
            with Phase() as ph:
                xt = ph.sb([128, KC, 1024], F32)
                fn = ph.sb([128, KC], F32)
                nb = norm_bufs(ph)
                sqb, t_sqb, sd, t_sd, rstd, t_rstd, tmp, t_tmp = nb
                t_xt, t_fn = TF(), Tok()
                DMA("sp", fn[:], W["fnT"], (), [t_fn], "fn")
                o_v = I["outT"].rearrange("(kc p) t -> p kc t", p=128)
                for (t0, T, j) in TILES[:-1]:
                    DMA("sp", xt[:, :, 0:T], xs_v[:, :, t0:t0 + T], [TD["xs", t0]], [t_xt[kc, s] for kc in range(KC) for s in range(2)], "xld")
                    for s in range(2):
                        c0 = s * 512
                        cw = 512
                        for kc in range(KC):
                            b = kc % 2
                            ACT(sqb[b][:, :cw], xt[:, kc, c0:c0 + cw], AF.Square, [t_xt[kc, s]], [t_sqb[b]])
                            MM(PS[0][:, :cw], ones_bf[:], sqb[b][:, :cw], kc == 0, kc == KC - 1, [t_sqb[b], t_ones], [TP[0][0]])
                        ACT(sd[:, :cw], PS[0][:, :cw], AF.Sqrt, [TP[0][0]], [t_sd], scale=1.0 / D, bias=eps_ap[:, 0:1])
                        RECIP(rstd[:, :cw], sd[:, :cw], [t_sd], [t_rstd])
                        for kc in range(KC):
                            STT(xt[:, kc, c0:c0 + cw], xt[:, kc, c0:c0 + cw], fn[:, kc:kc + 1], rstd[:, :cw], ALU.mult, ALU.mult,
                                [t_xt[kc, s], t_fn, t_rstd], [t_xt[kc, s]])
                    DMA("sp", o_v[:, :, t0:t0 + T], xt[:, :, 0:T], [t_xt[kc, s] for kc in range(KC) for s in range(2)], [TD["outT", t0]], "ost")

        for (name, l) in phases:
            last = (l == DEPTH - 1)
            if name == "mod":
                phase_mod(l)
            elif name == "modld":
                phase_modld()
            elif name == "ffn1":
                phase_ffn(l, 1, TILES)
            elif name == "proj":
                phase_proj(l)
            elif name == "xchg":
                phase_exchange()
            elif name == "gm":
                phase_gm(l, not last)
            elif name == "na":
                phase_na(l, not last)
            elif name == "da":
                phase_da(l, not last)
            elif name == "wout":
                phase_wout(l, TILES[:-1] if last else TILES)
            elif name == "ffn2":
                phase_ffn(l, 2, TILES[:-1] if last else TILES)
            elif name == "final":
                phase_final()
            else:
                raise ValueError(name)

        with Phase() as ph:
            for k in ext_out:
                shp = ISHAPES[k][0]
                nr = shp[0]
                step = max(1, nr // 4)
                for r0 in range(0, nr, step):
                    r1 = min(nr, r0 + step)
                    DMA("sp", EO[k][r0:r1, :], I[k][r0:r1, :], (), [TD[k, "cpo", r0]], "cpo")
        build_program.ninstr = P.ninstr
        build_program.used = list(used.keys())
    return nc


GRID_W = 64


def _rope_tables():
    t = np.arange(TL)
    row = (t // GRID_W).astype(np.float32)
    col = (t % GRID_W).astype(np.float32)
    nf = 16
    freqs = (np.float32(10000.0) ** (-(np.arange(nf, dtype=np.float32) / np.float32(nf)))).astype(np.float32)
    ang = np.concatenate([row[:, None] * freqs[None, :], col[:, None] * freqs[None, :]], axis=-1).astype(np.float32)
    cos = np.cos(ang).astype(np.float32)
    sin = np.sin(ang).astype(np.float32)
    cosT = np.empty((128, TL), np.float32)
    sinT = np.empty((128, TL), np.float32)
    for p in range(128):
        d = p % 64
        jx = d % 32
        cosT[p] = cos[:, jx]
        sinT[p] = -sin[:, jx] if d < 32 else sin[:, jx]
    return cosT, sinT


def _na_tables(rpb):
    NP = TL // 128
    out = np.full((128, 4, 5, 6, 128), NEG, np.float32)
    kl = np.arange(128) // 64
    kcol = np.arange(128) % 64
    rr = np.arange(128) // 64
    qcol = np.arange(128) % 64
    w0 = np.clip(qcol - 8, 0, 48)
    colok = (kcol[:, None] >= w0[None, :]) & (kcol[:, None] < w0[None, :] + 16)
    dc = np.clip(kcol[:, None] - qcol[None, :] + 15, 0, 30)
    for cls, i in enumerate((0, 1, 5, NP - 2, NP - 1)):
        s0 = min(i, NP - 2)
        for c in range(6):
            li = s0 + c
            gp = li - 2
            if gp < 0 or gp >= NP:
                continue
            krow = 2 * gp + kl
            qrow = 2 * i + rr
            kstart = np.clip(qrow - 4, 0, 2 * NP - 8)
            rowok = (krow[:, None] >= kstart[None, :]) & (krow[:, None] < kstart[None, :] + 8)
            dr = krow[:, None] - qrow[None, :] + 7
            ok = rowok & colok
            drc = np.clip(dr, 0, 14)
            for hh in range(4):
                vals = rpb[hh][drc, dc]
                out[:, hh, cls, c, :] = np.where(ok, vals, np.float32(NEG))
    return np.ascontiguousarray(out.reshape(128, 4 * 5 * 768))


def _swap_cols(wcols):
    n = wcols.shape[-1]
    idx = np.arange(n).reshape(-1, 2, 32)[:, ::-1, :].reshape(-1)
    return wcols[..., idx]


def prepare_inputs(inp):
    f = lambda a: np.ascontiguousarray(np.asarray(a, dtype=np.float32))
    x, c, ctx, c_ctx = f(inp["x"]), f(inp["c"]), f(inp["ctx"]), f(inp["c_ctx"])
    w_in = f(inp["w_in"])
    qa, ka, va, u, v, qd, kd, vd = (w_in[:, :, 0:256], w_in[:, :, 256:512], w_in[:, :, 512:768], w_in[:, :, 768:1024],
                                    w_in[:, :, 1024:1280], w_in[:, :, 1280:1792], w_in[:, :, 1792:2304], w_in[:, :, 2304:2816])
    w_in_ext = np.ascontiguousarray(np.concatenate([qa, ka, u, qd, kd, _swap_cols(qd), _swap_cols(kd), va, v, vd], axis=-1))
    shared = {
        "w_ada": f(inp["w_ada"]),
        "b_adaT": np.ascontiguousarray(f(inp["b_ada"]).reshape(DEPTH, 72, 128).transpose(0, 2, 1)),
        "w_in_ext": w_in_ext,
        "w_out": f(inp["w_out"]),
        "gm_wsT": np.ascontiguousarray(f(inp["gm_ws"]).transpose(0, 1, 3, 2)),
        "gm_bs": f(inp["gm_bs"]),
        "gm_norm": f(inp["gm_norm"]).reshape(DEPTH, 256),
        "da_subln": f(inp["da_subln"]),
        "fnT": np.ascontiguousarray(f(inp["final_norm"]).reshape(KC, 128).T),
    }
    for k in ("ffn1_w1", "ffn1_w3", "ffn1_w2", "ffn2_w1", "ffn2_w3", "ffn2_w2", "da_lq1", "da_lk1", "da_lq2", "da_lk2"):
        shared[k] = f(inp[k])
    rpb = f(inp["na_rpb"])
    cosT, sinT = _rope_tables()
    shared["cosT"] = cosT
    shared["sinT"] = sinT
    shared["nat"] = np.stack([_na_tables(rpb[l]) for l in range(DEPTH)], axis=0)
    cores = []
    xs0 = []
    for b in range(NCORES):
        m = dict(shared)
        cT = np.stack([c[b].reshape(KC, 128).T, c_ctx.reshape(KC, 128).T], axis=-1)
        m["cT"] = np.ascontiguousarray(cT)
        cores.append(m)
        xT = np.concatenate([x[b].T, ctx[b].T], axis=1)
        xs0.append(np.ascontiguousarray(xT))
    return cores, xs0


FUSED = dict(
    phases=[("mod", 0), ("mod", 1),
            ("ffn1", 0), ("proj", 0), ("gm", 0), ("na", 0), ("da", 0), ("wout", 0), ("ffn2", 0),
            ("ffn1", 1), ("proj", 1), ("gm", 1), ("na", 1), ("da", 1), ("wout", 1), ("ffn2", 1), ("final", 1)],
    ins=["xs"], outs=["outT"])


def run_launch(cfg, cores, state):
    nc = build_program(cfg["phases"], cfg["ins"], cfg["outs"], fused=False)
    in_maps = []
    ncores = len(state)
    for core in range(ncores):
        m = {}
        for name in build_program.used:
            if name in cores[core]:
                m[name] = cores[core][name]
            else:
                k, l = name.rsplit("_", 1)
                m[name] = np.ascontiguousarray(cores[core][k][int(l)])
        for k in cfg["ins"]:
            m[k + "_i"] = state[core][k]
        in_maps.append(m)
    res = run_bass_kernel_spmd(nc, in_maps, core_ids=list(range(ncores)))
    for core in range(ncores):
        for k in cfg["outs"]:
            state[core][k] = res.results[core][k + "_o"]
    return state


def kernel(**inputs):
    cores, xs0 = prepare_inputs(inputs)
    state = [{"xs": xs0[i]} for i in range(NCORES)]
    state = run_launch(FUSED, cores, state)
    out = np.empty((NCORES, TL, D), np.float32)
    for b in range(NCORES):
        out[b] = state[b]["outT"].T
    return out
```

```python
import math
import contextlib
import numpy as np
import concourse.bass as bass
import concourse.mybir as mybir
from concourse.bass_utils import run_bass_kernel_spmd

F32 = mybir.dt.float32
BF16 = mybir.dt.bfloat16
AF = mybir.ActivationFunctionType
ALU = mybir.AluOpType
AX = mybir.AxisListType

D = 1024
KC = 8
TL = 8192
NCORES = 4
TC = 256
TT = TL + TC
FF = 2816
NJ = 22
DEPTH = 2
EPS = 1e-6
NEG = -30000.0
WEXT = 3840
C_QA, C_KA, C_U, C_QD, C_KD, C_QDS, C_KDS, C_VAB, C_VD = 0, 256, 512, 768, 1280, 1792, 2304, 2816, 3328
R_KD, R_VD, R_KA, R_VA, R_END = 0, 512, 1024, 1280, 1536

ENGS = ("pe", "act", "dve", "pool", "sp")
NDSEM = 56


class Tok:
    __slots__ = ("w", "rs")

    def __init__(self):
        self.w = None
        self.rs = {}


class TF(dict):
    def __missing__(self, k):
        t = Tok()
        self[k] = t
        return t


class Ins:
    __slots__ = ("eng", "fn", "deps", "idx", "need_inc", "val", "dsem", "dval", "is_dma")

    def __init__(self, eng, fn):
        self.eng = eng
        self.fn = fn
        self.deps = []
        self.idx = -1
        self.need_inc = False
        self.val = 0
        self.dsem = None
        self.dval = 0
        self.is_dma = False


class Prog:
    def __init__(self, nc, stack):
        self.nc = nc
        self.q = {e: [] for e in ENGS}
        self.seen = {e: {} for e in ENGS}
        self.nidx = {e: 0 for e in ENGS}
        self.ecount = {e: 0 for e in ENGS}
        self.esem = {e: stack.enter_context(nc.semaphore("s_" + e)) for e in ENGS}
        self.dsems = [stack.enter_context(nc.semaphore("d%d" % i)) for i in range(NDSEM)]
        self.dval = [0] * NDSEM
        self.dlast = [None] * NDSEM
        self.keymap = {}
        self.ninstr = 0

    def add(self, eng, fn, reads=(), writes=(), dma_key=None):
        ins = Ins(eng, fn)
        ins.idx = self.nidx[eng]
        self.nidx[eng] += 1
        cand = []
        for t in reads:
            if t.w is not None:
                cand.append(t.w)
        for t in writes:
            if t.w is not None:
                cand.append(t.w)
            cand.extend(t.rs.values())
        if dma_key is not None:
            ins.is_dma = True
            if dma_key not in self.keymap:
                assert len(self.keymap) < NDSEM, "out of dma sems"
                self.keymap[dma_key] = len(self.keymap)
            si = self.keymap[dma_key]
            if self.dlast[si] is not None:
                cand.append(self.dlast[si])
            self.dval[si] += 16
            self.dlast[si] = ins
            ins.dsem = si
            ins.dval = self.dval[si]
        seen = self.seen[eng]
        best = {}
        for p in cand:
            if p is ins:
                continue
            if p.is_dma:
                k = ("d", p.dsem)
                v = p.dval
            else:
                if p.eng == eng and eng == "pe":
                    continue
                k = p.eng
                v = p.idx
            if seen.get(k, -1) >= v:
                continue
            if k not in best or (best[k].dval if p.is_dma else best[k].idx) < v:
                best[k] = p
        for k, p in best.items():
            seen[k] = p.dval if p.is_dma else p.idx
            if not p.is_dma:
                p.need_inc = True
            ins.deps.append(p)
        self.q[eng].append(ins)
        rk = ("d", ins.dsem, ins.dval) if ins.is_dma else eng
        for t in reads:
            t.rs[rk] = ins
        for t in writes:
            t.w = ins
            t.rs = {}
        return ins

    def flush(self):
        nc = self.nc
        last = {}
        for e in ENGS:
            for ins in reversed(self.q[e]):
                if not ins.is_dma:
                    ins.need_inc = True
                    last[e] = ins
                    break
        for e in ENGS:
            c = self.ecount[e]
            for ins in self.q[e]:
                if ins.need_inc and not ins.is_dma:
                    c += 1
                    ins.val = c
            self.ecount[e] = c
        used_d = [i for i in range(NDSEM) if self.dlast[i] is not None]
        prog = self

        def run(engname, eh):
            for ins in prog.q[engname]:
                for p in ins.deps:
                    if p.is_dma:
                        eh.wait_ge(prog.dsems[p.dsem], p.dval)
                    else:
                        eh.wait_ge(prog.esem[p.eng], p.val)
                r = ins.fn(eh)
                if ins.is_dma:
                    r.then_inc(prog.dsems[ins.dsem], 16)
                elif ins.need_inc:
                    r.then_inc(prog.esem[engname], 1)
                prog.ninstr += 1
            for e2 in ENGS:
                if e2 != engname and prog.ecount[e2] > 0:
                    eh.wait_ge(prog.esem[e2], prog.ecount[e2])
            for i in used_d:
                eh.wait_ge(prog.dsems[i], prog.dval[i])

        with nc.Block() as block:
            @block.tensor
            def _(eh):
                run("pe", eh)

            @block.scalar
            def _(eh):
                run("act", eh)

            @block.vector
            def _(eh):
                run("dve", eh)

            @block.gpsimd
            def _(eh):
                run("pool", eh)

            @block.sync
            def _(eh):
                run("sp", eh)

        for e in ENGS:
            for e2 in ENGS:
                self.seen[e][e2] = self.nidx[e2] - 1
            for i in range(NDSEM):
                self.seen[e][("d", i)] = self.dval[i]
            self.q[e] = []
        self.keymap = {}
        self.dlast = [None] * NDSEM


def lambda_init(l):
    return 0.8 - 0.6 * math.exp(-0.3 * l)


def build_program(phases, ext_in, ext_out, fused=False):
    nc = bass.Bass("TRN2", target_bir_lowering=False)
    st = contextlib.ExitStack()
    with st:
        P = Prog(nc, st)

        def din(name, shape, dt=F32):
            return nc.dram_tensor(name, list(shape), dt, kind="ExternalInput").ap()

        SH_L = {"w_ada": [D, 9 * D], "b_adaT": [128, 72], "ffn1_w1": [D, FF], "ffn1_w3": [D, FF], "ffn2_w1": [D, FF],
                "ffn2_w3": [D, FF], "ffn1_w2": [FF, D], "ffn2_w2": [FF, D], "w_in_ext": [D, WEXT], "w_out": [D, D],
                "nat": [128, 4 * 5 * 768], "gm_wsT": [4, 128, 128], "gm_bs": [4, 128], "gm_norm": [256],
                "da_lq1": [64], "da_lk1": [64], "da_lq2": [64], "da_lk2": [64], "da_subln": [128]}
        SH_G = {"cT": [128, KC, 2], "fnT": [128, KC], "cosT": [128, TL], "sinT": [128, TL]}
        used = {}

        class _LW:
            def __init__(self, k):
                self.k = k

            def __getitem__(self, l):
                name = "%s_%d" % (self.k, l)
                if name not in used:
                    used[name] = din(name, SH_L[self.k])
                return used[name]

        class _WD:
            def __getitem__(self, k):
                if k in SH_L:
                    return _LW(k)
                if k not in used:
                    used[k] = din(k, SH_G[k])
                return used[k]

        W = _WD()

        ISHAPES = {
            "xs": ([D, TT], F32),
            "qaT": ([256, TT], BF16),
            "uT": ([256, TT], BF16),
            "qdT": ([512, TT], BF16),
            "vbn": ([TT, 256], BF16),
            "kv_own": ([R_END, TL], BF16),
            "kdTc": ([512, TC], BF16),
            "vdc": ([TC, 512], BF16),
            "kaTc": ([256, TC], BF16),
            "vac": ([TC, 256], BF16),
            "mixT": ([D, TT], BF16),
            "outT": ([D, TL], F32),
            "modD": ([128, DEPTH * 144], F32),
        }
        I = {}
        for k, (shp, dt) in ISHAPES.items():
            I[k] = nc.dram_tensor(k, shp, dt, kind="Internal").ap()
        EI = {k: nc.dram_tensor(k + "_i", ISHAPES[k][0], ISHAPES[k][1], kind="ExternalInput").ap() for k in ext_in}
        EO = {k: nc.dram_tensor(k + "_o", ISHAPES[k][0], ISHAPES[k][1], kind="ExternalOutput").ap() for k in ext_out}
        TD = TF()

        modT = st.enter_context(nc.sbuf_tensor("modT", [128, DEPTH, 72, 2], F32))
        sc1T = st.enter_context(nc.sbuf_tensor("sc1T", [128, DEPTH, 72, 2], F32))
        ghT = st.enter_context(nc.sbuf_tensor("ghT", [128, DEPTH, 72, 2], F32))
        ones_bf = st.enter_context(nc.sbuf_tensor("ones_bf", [128, 128], BF16))
        PSP = [st.enter_context(nc.psum_tensor("psp%d" % i, [128, 1024], F32)) for i in range(4)]
        PS = [PSP[i // 2][:, (i % 2) * 512:(i % 2 + 1) * 512] for i in range(8)]
        TP = [TF() for _ in range(8)]
        t_mod = Tok()
        t_ones = Tok()

        def MM(out, lhsT, rhs, s, e_, R, Wt, tp=None):
            if tp is None:
                P.add("pe", lambda e: e.matmul(out, lhsT=lhsT, rhs=rhs, start=s, stop=e_), R, Wt)
            else:
                P.add("pe", lambda e: e.matmul(out, lhsT=lhsT, rhs=rhs, start=s, stop=e_, tile_position=tp), R, Wt)

        def ACT(out, in_, func, R, Wt, **kw):
            P.add("act", lambda e: e.activation(out=out, in_=in_, func=func, **kw), R, Wt)

        def TTo(eng, out, a, b, op, R, Wt):
            P.add(eng, lambda e: e.tensor_tensor(out=out, in0=a, in1=b, op=op), R, Wt)

        def TS(eng, out, a, s1, s2, op0, op1, R, Wt):
            if op1 is None:
                P.add(eng, lambda e: e.tensor_scalar(out=out, in0=a, scalar1=s1, scalar2=None, op0=op0), R, Wt)
            else:
                P.add(eng, lambda e: e.tensor_scalar(out=out, in0=a, scalar1=s1, scalar2=s2, op0=op0, op1=op1), R, Wt)

        def STT(out, in0, scalar, in1, op0, op1, R, Wt):
            P.add("dve", lambda e: e.scalar_tensor_tensor(out=out, in0=in0, scalar=scalar, in1=in1, op0=op0, op1=op1), R, Wt)

        def RECIP(out, in_, R, Wt):
            P.add("dve", lambda e: e.reciprocal(out=out, in_=in_), R, Wt)

        def DMA(qn, out, in_, R, Wt, key):
            P.add(qn, lambda e: e.dma_start(out=out, in_=in_), R, Wt, dma_key=key)

        class Phase:
            cnt = [0]

            def __init__(self):
                self.st = contextlib.ExitStack()
                self.n = 0
                Phase.cnt[0] += 1
                self.pid = Phase.cnt[0]

            def __enter__(self):
                self.st.__enter__()
                return self

            def sb(self, shape, dt):
                self.n += 1
                return self.st.enter_context(nc.sbuf_tensor("t%d_%d" % (self.pid, self.n), list(shape), dt))

            def __exit__(self, *a):
                P.flush()
                return self.st.__exit__(*a)

        TILES = [(i * 1024, 1024, 0) for i in range(TL // 1024)] + [(TL, TC, 1)]

        with Phase() as ph:
            P.add("dve", lambda e: e.memset(ones_bf[:], 1.0), (), [t_ones])
            for k in ext_in:
                shp = ISHAPES[k][0]
                nr = shp[0]
                step = max(1, nr // 4)
                for r0 in range(0, nr, step):
                    r1 = min(nr, r0 + step)
                    DMA("sp", I[k][r0:r1, :], EI[k][r0:r1, :], (), [TD[k, "cp", r0]], "cp")

        def norm_mod(ph, bufs, xt, t_xt, h, t_h, s, cw, l, gS, gC, j):
            sqb, t_sqb, sd, t_sd, rstd, t_rstd, tmp, t_tmp = bufs
            c0 = s * 512
            for kc in range(KC):
                b = kc % 2
                ACT(sqb[b][:, :cw], xt[:, kc, c0:c0 + cw], AF.Square, [t_xt[kc, s]], [t_sqb[b]])
                MM(PS[0][:, :cw], ones_bf[:], sqb[b][:, :cw], kc == 0, kc == KC - 1, [t_sqb[b], t_ones], [TP[0][0]])
            ACT(sd[:, :cw], PS[0][:, :cw], AF.Sqrt, [TP[0][0]], [t_sd], scale=1.0 / D, bias=eps_ap[:, 0:1])
            RECIP(rstd[:, :cw], sd[:, :cw], [t_sd], [t_rstd])
            for kc in range(KC):
                b = kc % 2
                TTo("dve", tmp[b][:, :cw], xt[:, kc, c0:c0 + cw], rstd[:, :cw], ALU.mult, [t_xt[kc, s], t_rstd], [t_tmp[b]])
                ACT(h[:, kc, c0:c0 + cw], tmp[b][:, :cw], AF.Identity, [t_tmp[b], t_mod], [t_h[kc, s]],
                    scale=sc1T[:, l, gC * 8 + kc, j:j + 1], bias=modT[:, l, gS * 8 + kc, j:j + 1])

        def norm_bufs(ph):
            sqb = [ph.sb([128, 512], BF16) for _ in range(2)]
            sd = ph.sb([128, 512], F32)
            rstd = ph.sb([128, 512], F32)
            tmp = [ph.sb([128, 512], F32) for _ in range(2)]
            return (sqb, [Tok(), Tok()], sd, Tok(), rstd, Tok(), tmp, [Tok(), Tok()])

        xs_v = I["xs"].rearrange("(kc p) t -> p kc t", p=128)

        eps_ap = st.enter_context(nc.sbuf_tensor("eps_ap", [128, 1], F32))

        with Phase() as ph:
            P.add("dve", lambda e: e.memset(eps_ap[:], EPS), (), [Tok()])

        def phase_mod(l):
            with Phase() as ph:
                cs = ph.sb([128, KC, 2], F32)
                csl = ph.sb([128, KC, 2], F32)
                wa = [ph.sb([128, KC, 1024], F32) for _ in range(2)]
                bT = ph.sb([128, 72], F32)
                t_cs, t_csl, t_b = Tok(), Tok(), Tok()
                t_wa = [Tok(), Tok()]
                DMA("sp", cs[:], W["cT"], (), [t_cs], "cs")
                DMA("sp", bT[:], W["b_adaT"][l], (), [t_b], "bT")
                ACT(csl[:], cs[:], AF.Silu, [t_cs], [t_csl])
                pm = PS[1][:, 0:144].rearrange("p (g j) -> p g j", j=2)
                for g in range(9):
                    b = g % 2
                    DMA("sp", wa[b][:], W["w_ada"][l][:, g * 1024:(g + 1) * 1024].rearrange("(kc p) c -> p kc c", p=128),
                        (), [t_wa[b]], ("wa", b))
                    for m in range(8):
                        for kc in range(KC):
                            MM(pm[:, g * 8 + m, :], wa[b][:, kc, m * 128:(m + 1) * 128], csl[:, kc, :], kc == 0, kc == KC - 1,
                               [t_wa[b], t_csl], [TP[1][0]])
                for j in range(2):
                    TTo("dve", modT[:, l, :, j], pm[:, :, j], bT[:], ALU.add, [TP[1][0], t_b], [t_mod])
                TS("dve", sc1T[:, l, :, :], modT[:, l, :, :], 1.0, None, ALU.add, None, [t_mod], [t_mod])
                TS("dve", ghT[:, l, :, :], modT[:, l, :, :], 0.5, None, ALU.mult, None, [t_mod], [t_mod])
                DMA("sp", I["modD"][:, l * 144:(l + 1) * 144], modT[:, l, :, :].rearrange("p g j -> p (g j)"), [t_mod], [TD["modD", l]], "modst")

        def phase_modld():
            with Phase() as ph:
                DMA("sp", modT[:].rearrange("p l g j -> p (l g j)"), I["modD"], (), [t_mod], "modld")
                TS("dve", sc1T[:].rearrange("p l g j -> p (l g j)"), modT[:].rearrange("p l g j -> p (l g j)"), 1.0, None, ALU.add, None, [t_mod], [t_mod])
                TS("dve", ghT[:].rearrange("p l g j -> p (l g j)"), modT[:].rearrange("p l g j -> p (l g j)"), 0.5, None, ALU.mult, None, [t_mod], [t_mod])

        def phase_ffn(l, which, tiles):
            w1 = W["ffn%d_w1" % which][l]
            w3 = W["ffn%d_w3" % which][l]
            w2 = W["ffn%d_w2" % which][l]
            gS, gC, gG = (0, 1, 2) if which == 1 else (6, 7, 8)
            with Phase() as ph:
                xts = [ph.sb([128, KC, 1024], F32) for _ in range(2)]
                h = ph.sb([128, KC, 1024], BF16)
                a = ph.sb([128, NJ, 1024], BF16)
                w13 = [ph.sb([128, KC, 256], BF16) for _ in range(3)]
                w2c = [ph.sb([128, NJ, 128], BF16) for _ in range(3)]
                sil = [ph.sb([128, 512], F32) for _ in range(2)]
                nb = norm_bufs(ph)
                t_xts, t_h, t_a = [TF(), TF()], TF(), TF()
                t_w1, t_w3, t_w2, t_sil = TF(), TF(), TF(), TF()
                cnt = 0

                def geom(idx):
                    t0, T, j = tiles[idx]
                    return t0, T, j, (T + 511) // 512, min(T, 512)

                def load(idx):
                    t0, T, j, nsub, cw = geom(idx)
                    DMA("sp", xts[idx % 2][:, :, 0:T], xs_v[:, :, t0:t0 + T], [TD["xs", t0]],
                        [t_xts[idx % 2][kc, s] for kc in range(KC) for s in range(nsub)], ("xld", idx % 2))

                def norm(idx, s):
                    t0, T, j, nsub, cw = geom(idx)
                    if s < nsub:
                        norm_mod(ph, nb, xts[idx % 2], t_xts[idx % 2], h, t_h, s, cw, l, gS, gC, j)

                load(0)
                norm(0, 0)
                norm(0, 1)
                for idx in range(len(tiles)):
                    t0, T, j, nsub, cw = geom(idx)
                    xt, t_xt = xts[idx % 2], t_xts[idx % 2]
                    nxt = idx + 1 < len(tiles)
                    if nxt:
                        load(idx + 1)
                    for jj in range(NJ):
                        sl = jj % 3
                        DMA("pool", w13[sl][:, :, 0:128], w1[:, jj * 128:(jj + 1) * 128].rearrange("(kc p) c -> p kc c", p=128),
                            (), [t_w1[sl]], ("w1", sl))
                        DMA("pool", w13[sl][:, :, 128:256], w3[:, jj * 128:(jj + 1) * 128].rearrange("(kc p) c -> p kc c", p=128),
                            (), [t_w3[sl]], ("w3", sl))
                        for s in range(nsub):
                            b = cnt % 2
                            cnt += 1
                            c0 = s * 512
                            for kc in range(KC):
                                MM(PS[1 + b][:, :cw], w13[sl][:, kc, 0:128], h[:, kc, c0:c0 + cw], kc == 0, kc == KC - 1,
                                   [t_w1[sl], t_h[kc, s]], [TP[1 + b][0]])
                            for kc in range(KC):
                                MM(PS[3 + b][:, :cw], w13[sl][:, kc, 128:256], h[:, kc, c0:c0 + cw], kc == 0, kc == KC - 1,
                                   [t_w3[sl], t_h[kc, s]], [TP[3 + b][0]])
                            ACT(sil[b][:, :cw], PS[1 + b][:, :cw], AF.Silu, [TP[1 + b][0]], [t_sil[b]])
                            TTo("dve", a[:, jj, c0:c0 + cw], sil[b][:, :cw], PS[3 + b][:, :cw], ALU.mult,
                                [t_sil[b], TP[3 + b][0]], [t_a[jj, s]])
                    for m in range(KC):
                        sl = m % 3
                        DMA("pool", w2c[sl][:], w2[:, m * 128:(m + 1) * 128].rearrange("(j p) c -> p j c", p=128),
                            (), [t_w2[sl]], ("w2", sl))
                        if nxt and m == 2:
                            norm(idx + 1, 0)
                        if nxt and m == 5:
                            norm(idx + 1, 1)
                        for s in range(nsub):
                            b = cnt % 2
                            cnt += 1
                            c0 = s * 512
                            for jj in range(NJ):
                                MM(PS[5 + b][:, :cw], w2c[sl][:, jj, :], a[:, jj, c0:c0 + cw], jj == 0, jj == NJ - 1,
                                   [t_w2[sl], t_a[jj, s]], [TP[5 + b][0]])
                            STT(xt[:, m, c0:c0 + cw], PS[5 + b][:, :cw], ghT[:, l, gG * 8 + m, j:j + 1], xt[:, m, c0:c0 + cw],
                                ALU.mult, ALU.add, [TP[5 + b][0], t_mod, t_xt[m, s]], [t_xt[m, s]])
                    DMA("sp", xs_v[:, :, t0:t0 + T], xt[:, :, 0:T],
                        [t_xt[kc, s] for kc in range(KC) for s in range(nsub)], [TD["xs", t0]], ("xst", idx % 2))

        def phase_proj(l):
            win = W["w_in_ext"][l]
            kvo = I["kv_own"]
            vd_lat = kvo[R_VD:R_KA, :].rearrange("r (a f) -> (r a) f", f=512)
            va_lat = kvo[R_VA:R_END, :].rearrange("r (a f) -> (r a) f", f=256)
            with Phase() as ph:
                wsb = ph.sb([128, KC, WEXT], BF16)
                xt = ph.sb([128, KC, 1024], F32)
                h = ph.sb([128, KC, 1024], BF16)
                ct = ph.sb([128, 1024], F32)
                sn = ph.sb([128, 1024], F32)
                stg = [ph.sb([128, 1024], BF16) for _ in range(4)]
                r1 = [ph.sb([128, 512], F32) for _ in range(2)]
                r2 = [ph.sb([128, 512], F32) for _ in range(2)]
                gnb = ph.sb([128, 256], F32)
                gl = [ph.sb([128, 256], F32) for _ in range(2)]
                sq = [ph.sb([128, 256], F32) for _ in range(2)]
                gn = [ph.sb([128, 256], F32) for _ in range(2)]
                ss4 = [ph.sb([128, 4], F32) for _ in range(2)]
                sd4 = [ph.sb([128, 4], F32) for _ in range(2)]
                r4 = [ph.sb([128, 4], F32) for _ in range(2)]
                sva = [ph.sb([128, 256], BF16) for _ in range(2)]
                svb = [ph.sb([128, 256], BF16) for _ in range(2)]
                svd = [ph.sb([128, 512], BF16) for _ in range(2)]
                nb = norm_bufs(ph)
                t_w, t_xt, t_h, t_stg, t_r1, t_r2 = TF(), TF(), TF(), TF(), TF(), TF()
                t_ct, t_sn, t_gnb = Tok(), Tok(), Tok()
                tv = TF()
                NWP = 8
                wp = WEXT // NWP
                for i in range(NWP):
                    DMA("pool", wsb[:, :, i * wp:(i + 1) * wp], win[:, i * wp:(i + 1) * wp].rearrange("(kc p) c -> p kc c", p=128),
                        (), [t_w[i]], ("win", i))
                tw_all = [t_w[i] for i in range(NWP)]

                def twc(col0, ncols):
                    return [t_w[i] for i in range(col0 // wp, (col0 + ncols - 1) // wp + 1)]

                DMA("sp", gnb[:], W["gm_norm"][l].partition_broadcast(128), (), [t_gnb], "gnb")
                cnt = 0
                sgi = 0
                vcnt = 0
                for (t0, T, j) in TILES:
                    nsub = (T + 511) // 512
                    cw = min(T, 512)
                    DMA("sp", xt[:, :, 0:T], xs_v[:, :, t0:t0 + T], [TD["xs", t0]],
                        [t_xt[kc, s] for kc in range(KC) for s in range(nsub)], "xld")
                    if j == 0:
                        DMA("sp", ct[:, 0:T], W["cosT"][:, t0:t0 + T], (), [t_ct], "ct")
                        DMA("sp", sn[:, 0:T], W["sinT"][:, t0:t0 + T], (), [t_sn], "sn")
                    for s in range(nsub):
                        norm_mod(ph, nb, xt, t_xt, h, t_h, s, cw, l, 3, 4, j)
                    fm = []
                    for c in range(2):
                        fm.append(("copy", C_QA + c * 128, None, I["qaT"][c * 128:(c + 1) * 128, t0:t0 + T], ("qaT", t0)))
                    for c in range(2):
                        dst = kvo[R_KA + c * 128:R_KA + (c + 1) * 128, t0:t0 + T] if j == 0 else I["kaTc"][c * 128:(c + 1) * 128, :]
                        fm.append(("copy", C_KA + c * 128, None, dst, ("ka", t0)))
                    for c in range(2):
                        fm.append(("gelu", C_U + c * 128, None, I["uT"][c * 128:(c + 1) * 128, t0:t0 + T], ("uT", t0)))
                    for c in range(4):
                        fm.append(("rope" if j == 0 else "copy", C_QD + c * 128, C_QDS + c * 128,
                                   I["qdT"][c * 128:(c + 1) * 128, t0:t0 + T], ("qdT", t0)))
                    for c in range(4):
                        dst = kvo[R_KD + c * 128:R_KD + (c + 1) * 128, t0:t0 + T] if j == 0 else I["kdTc"][c * 128:(c + 1) * 128, :]
                        fm.append(("rope" if j == 0 else "copy", C_KD + c * 128, C_KDS + c * 128, dst, ("kd", t0)))
                    for (kind, col, cols, dst, dtk) in fm:
                        sg = stg[sgi % 4]
                        tsg = t_stg[sgi % 4]
                        sgi += 1
                        for s in range(nsub):
                            b = cnt % 2
                            cnt += 1
                            c0 = s * 512
                            for kc in range(KC):
                                MM(PS[1 + b][:, :cw], wsb[:, kc, col:col + 128], h[:, kc, c0:c0 + cw], kc == 0, kc == KC - 1,
                                   twc(col, 128) + [t_h[kc, s]], [TP[1 + b][0]])
                            if kind == "copy":
                                ACT(sg[:, c0:c0 + cw], PS[1 + b][:, :cw], AF.Copy, [TP[1 + b][0]], [tsg])
                            elif kind == "gelu":
                                ACT(sg[:, c0:c0 + cw], PS[1 + b][:, :cw], AF.Gelu, [TP[1 + b][0]], [tsg])
                            else:
                                for kc in range(KC):
                                    MM(PS[3 + b][:, :cw], wsb[:, kc, cols:cols + 128], h[:, kc, c0:c0 + cw], kc == 0, kc == KC - 1,
                                       twc(cols, 128) + [t_h[kc, s]], [TP[3 + b][0]])
                                TTo("dve", r1[b][:, :cw], PS[1 + b][:, :cw], ct[:, c0:c0 + cw], ALU.mult, [TP[1 + b][0], t_ct], [t_r1[b]])
                                TTo("dve", r2[b][:, :cw], PS[3 + b][:, :cw], sn[:, c0:c0 + cw], ALU.mult, [TP[3 + b][0], t_sn], [t_r2[b]])
                                TTo("pool", sg[:, c0:c0 + cw], r1[b][:, :cw], r2[b][:, :cw], ALU.add, [t_r1[b], t_r2[b]], [tsg])
                        DMA("sp", dst, sg[:, 0:T], [tsg], [TD[dtk]], ("stg", (sgi - 1) % 4))
                    for tb in range(T // 128):
                        b = vcnt % 2
                        vcnt += 1
                        tk0 = tb * 128
                        hs = [t_h[kc, tk0 // 512] for kc in range(KC)]
                        for kc in range(KC):
                            MM(PS[5 + b][:, :], h[:, kc, tk0:tk0 + 128], wsb[:, kc, C_VAB:C_VAB + 512], kc == 0, kc == KC - 1,
                               twc(C_VAB, 512) + [hs[kc]], [TP[5 + b][0]])
                        ACT(sva[b][:], PS[5 + b][:, 0:256], AF.Copy, [TP[5 + b][0]], [tv["sva", b]])
                        dst = va_lat[t0 + tk0:t0 + tk0 + 128, :] if j == 0 else I["vac"][tk0:tk0 + 128, :]
                        DMA("sp", dst, sva[b][:], [tv["sva", b]], [TD["va", t0]], ("sva", b))
                        ACT(gl[b][:], PS[5 + b][:, 256:512], AF.Gelu, [TP[5 + b][0]], [tv["gl", b]])
                        TTo("dve", sq[b][:], gl[b][:], gl[b][:], ALU.mult, [tv["gl", b]], [tv["sq", b]])
                        P.add("dve", lambda e, b=b: e.reduce_sum(out=ss4[b][:], in_=sq[b][:].rearrange("p (g w) -> p g w", w=64), axis=AX.X),
                              [tv["sq", b]], [tv["ss4", b]])
                        ACT(sd4[b][:], ss4[b][:], AF.Sqrt, [tv["ss4", b]], [tv["sd4", b]], scale=1.0 / 64, bias=eps_ap[:, 0:1])
                        RECIP(r4[b][:], sd4[b][:], [tv["sd4", b]], [tv["r4", b]])
                        for g in range(4):
                            TS("dve", gn[b][:, g * 64:(g + 1) * 64], gl[b][:, g * 64:(g + 1) * 64], r4[b][:, g:g + 1], None, ALU.mult, None,
                               [tv["gl", b], tv["r4", b]], [tv["gn", b, g]])
                        TTo("dve", svb[b][:], gn[b][:], gnb[:], ALU.mult, [tv["gn", b, g] for g in range(4)] + [t_gnb], [tv["svb", b]])
                        DMA("sp", I["vbn"][t0 + tk0:t0 + tk0 + 128, :], svb[b][:], [tv["svb", b]], [TD["vbn", t0]], ("svb", b))
                        for kc in range(KC):
                            MM(PS[3 + b][:, :], h[:, kc, tk0:tk0 + 128], wsb[:, kc, C_VD:C_VD + 512], kc == 0, kc == KC - 1,
                               twc(C_VD, 512) + [hs[kc]], [TP[3 + b][0]])
                        ACT(svd[b][:], PS[3 + b][:, :], AF.Copy, [TP[3 + b][0]], [tv["svd", b]])
                        dst = vd_lat[t0 + tk0:t0 + tk0 + 128, :] if j == 0 else I["vdc"][tk0:tk0 + 128, :]
                        DMA("sp", dst, svd[b][:], [tv["svd", b]], [TD["vd", t0]], ("svd", b))

        def phase_exchange():
            if not fused:
                return
            with Phase() as ph:
                P.add("pool", lambda e: e.collective_compute(
                    "AllGather", ALU.bypass, replica_groups=[[0, 1], [2, 3], [4, 5], [6, 7]],
                    ins=[I["kv_own"]], outs=[I["kv_all"]]), (), [TD["kv_all"]], dma_key="cc")

        def phase_gm(l, with_ctx):
            with Phase() as ph:
                wsT = ph.sb([128, 4, 128], BF16)
                bsb = ph.sb([64, 4, 4, 128], F32)
                vb = [ph.sb([128, 4, 256], BF16) for _ in range(2)]
                us = [ph.sb([64, 4, 512], BF16) for _ in range(2)]
                tq = [ph.sb([64, 512], F32) for _ in range(2)]
                ob = [ph.sb([64, 4, 512], BF16) for _ in range(2)]
                t_ws, t_bs = Tok(), TF()
                tg = TF()
                DMA("pool", wsT[:], W["gm_wsT"][l].rearrange("g q p -> q g p"), (), [t_ws], "wsT")
                for g in range(4):
                    for c in range(4):
                        DMA("sp", bsb[:, g, c, :], W["gm_bs"][l][g].partition_broadcast(64), (), [t_bs[g, c]], ("bsb", (g * 4 + c) % 4))
                tbs = [t_bs[g, c] for g in range(4) for c in range(4)]
                uT_v = I["uT"].rearrange("(g w) t -> w g t", w=64)
                mx_v = I["mixT"][256:512, :].rearrange("(g w) t -> w g t", w=64)
                tiles = [(i * 512, 512) for i in range(TL // 512)] + ([(TL, TC)] if with_ctx else [])
                cnt = 0
                for ti, (t0, T) in enumerate(tiles):
                    bb = ti % 2
                    nck = T // 128
                    DMA("sp", vb[bb][:, 0:nck, :], I["vbn"][t0:t0 + T, :].rearrange("(c p) f -> p c f", p=128), (), [tg["vb", bb]], ("vb", bb))
                    DMA("sp", us[bb][:, :, 0:T], uT_v[:, :, t0:t0 + T], (), [tg["us", bb]], ("us", bb))
                    for g in range(4):
                        b = cnt % 2
                        cnt += 1
                        for c in range(nck):
                            MM(PS[1 + b][0:64, c * 128:(c + 1) * 128], vb[bb][:, c, g * 64:(g + 1) * 64], wsT[:, g, :], True, True,
                               [tg["vb", bb], t_ws], [TP[1 + b][0]])
                        TTo("dve", tq[b][:, 0:T], PS[1 + b][0:64, 0:T], bsb[:, g, 0:nck, :].rearrange("p c q -> p (c q)"), ALU.add,
                            [TP[1 + b][0]] + tbs, [tg["tq", b]])
                        TTo("dve", ob[bb][:, g, 0:T], tq[b][:, 0:T], us[bb][:, g, 0:T], ALU.mult, [tg["tq", b], tg["us", bb]], [tg["ob", bb, g]])
                    DMA("sp", mx_v[:, :, t0:t0 + T], ob[bb][:, :, 0:T], [tg["ob", bb, g] for g in range(4)], [TD["mixT", "b", t0]], ("ob", bb))

        def phase_na(l, with_ctx):
            kvo = I["kv_own"]
            NP = TL // 128
            NK = NP + 4
            with Phase() as ph:
                kT = ph.sb([64, NK * 128], BF16)
                vA = ph.sb([128, NK, 64], BF16)
                kTc = ph.sb([64, 256], BF16)
                vAc = ph.sb([128, 2, 64], BF16)
                tab = ph.sb([128, 5, 768], F32)
                qT = ph.sb([64, TT], BF16)
                ones64 = ph.sb([128, 64], BF16)
                sc = [ph.sb([128, 768], F32) for _ in range(2)]
                pr = [ph.sb([128, 1024], BF16) for _ in range(2)]
                rz = [ph.sb([64, 256], F32) for _ in range(2)]
                oa = [ph.sb([64, 512], BF16) for _ in range(2)]
                tn = TF()
                P.add("pool", lambda e: e.memset(ones64[:], 1.0), (), [tn["ones"]])
                P.add("pool", lambda e: e.memset(kT[:, 0:256], 0.0), (), [tn["kpad", 0]])
                P.add("pool", lambda e: e.memset(kT[:, (NK - 2) * 128:NK * 128], 0.0), (), [tn["kpad", 1]])
                P.add("pool", lambda e: e.memset(vA[:, 0:2, :], 0.0), (), [tn["vpad", 0]])
                P.add("pool", lambda e: e.memset(vA[:, NK - 2:NK, :], 0.0), (), [tn["vpad", 1]])
                va_v = kvo[R_VA:R_END, :].rearrange("r (a f) -> (r a) f", f=256)
                tk = [tn["k"], tn["kpad", 0], tn["kpad", 1]]
                tvv = [tn["v"], tn["vpad", 0], tn["vpad", 1]]
                cnt = 0
                ocnt = 0
                for hh in range(4):
                    hr = slice(hh * 64, (hh + 1) * 64)
                    DMA("sp", kT[:, 256:(NK - 2) * 128], kvo[R_KA + hh * 64:R_KA + (hh + 1) * 64, :], (), [tn["k"]], "k1")
                    DMA("sp", vA[:, 2:NK - 2, :], va_v[:, hr].rearrange("(c p) f -> p c f", p=128), (), [tn["v"]], "v1")
                    DMA("sp", kTc[:], I["kaTc"][hr, :], (), [tn["kc"]], "kc")
                    DMA("sp", vAc[:], I["vac"][:, hr].rearrange("(c p) f -> p c f", p=128), (), [tn["vc"]], "vc")
                    DMA("sp", tab[:].rearrange("p c n -> p (c n)"), W["nat"][l][:, hh * 3840:(hh + 1) * 3840], (), [tn["tab"]], "tab")
                    DMA("sp", qT[:], I["qaT"][hr, :], (), [tn["q"]], "q")
                    pend = None
                    for i in range(NP):
                        cls = 0 if i == 0 else 1 if i == 1 else 3 if i == NP - 2 else 4 if i == NP - 1 else 2
                        s0 = min(i, NP - 2)
                        if i % 4 == 0:
                            oi = ocnt % 2
                            ocnt += 1
                        ob = oa[oi]
                        b = cnt % 2
                        cnt += 1
                        A, B_ = PS[0 + b], PS[2 + b]
                        q_ap = qT[:, i * 128:(i + 1) * 128]
                        for c in range(6):
                            dst = A[:, c * 128:(c + 1) * 128] if c < 4 else B_[:, (c - 4) * 128:(c - 3) * 128]
                            MM(dst, kT[:, (s0 + c) * 128:(s0 + c + 1) * 128], q_ap, True, True, tk + [tn["q"]],
                               [TP[0 + b][0] if c < 4 else TP[2 + b][0]])
                        for c in range(2):
                            MM(B_[:, (2 + c) * 128:(3 + c) * 128], kTc[:, c * 128:(c + 1) * 128], q_ap, True, True,
                               [tn["kc"], tn["q"]], [TP[2 + b][0]])
                        STT(sc[b][:, 0:512], A[:, :], 0.125, tab[:, cls, 0:512], ALU.mult, ALU.add, [TP[0 + b][0], tn["tab"]], [tn["sc", b, 0]])
                        STT(sc[b][:, 512:768], B_[:, 0:256], 0.125, tab[:, cls, 512:768], ALU.mult, ALU.add, [TP[2 + b][0], tn["tab"]], [tn["sc", b, 1]])
                        ACT(pr[b][:, 0:768], sc[b][:, :], AF.Exp, [tn["sc", b, 0], tn["sc", b, 1]], [tn["pr", b, 0]])
                        ACT(pr[b][:, 768:1024], B_[:, 256:512], AF.Exp, [TP[2 + b][0]], [tn["pr", b, 1]], scale=0.125)
                        def st2(i=i, b=b, ob=ob, oi=oi, s0=s0, hr=hr, hh=hh):
                            O, Z = PS[4 + b], PS[6 + b]
                            for c in range(8):
                                vch = vA[:, s0 + c, :] if c < 6 else vAc[:, c - 6, :]
                                MM(O[0:64, 0:128], vch, pr[b][:, c * 128:(c + 1) * 128], c == 0, c == 7,
                                   tvv + [tn["vc"], tn["pr", b, 0], tn["pr", b, 1]], [TP[4 + b][0]])
                            for c in range(8):
                                MM(Z[0:64, 0:128], ones64[:], pr[b][:, c * 128:(c + 1) * 128], c == 0, c == 7,
                                   [tn["ones"], tn["pr", b, 0], tn["pr", b, 1]], [TP[6 + b][0]])
                            RECIP(rz[b][:, 0:128], Z[0:64, 0:128], [TP[6 + b][0]], [tn["rz", b]])
                            TTo("dve", ob[:, (i % 4) * 128:(i % 4 + 1) * 128], O[0:64, 0:128], rz[b][:, 0:128], ALU.mult,
                                [TP[4 + b][0], tn["rz", b]], [tn["oa", oi, i % 4]])
                            if i % 4 == 3:
                                t0 = (i // 4) * 512
                                DMA("sp", I["mixT"][hr, t0:t0 + 512], ob[:, :], [tn["oa", oi, k] for k in range(4)],
                                    [TD["mixT", "a", hh, t0]], ("oa", oi))

                        if pend is not None:
                            pend()
                        pend = st2
                    pend()
                    pend = None
                    if with_ctx:
                        oi = ocnt % 2
                        ocnt += 1
                        ob = oa[oi]
                        b = cnt % 2
                        cnt += 1
                        A = PS[0 + b]
                        q_ap = qT[:, TL:TL + 256]
                        for c in range(2):
                            MM(A[:, c * 256:(c + 1) * 256], kTc[:, c * 128:(c + 1) * 128], q_ap, True, True, [tn["kc"], tn["q"]], [TP[0 + b][0]])
                        ACT(pr[b][:, 0:512], A[:, :], AF.Exp, [TP[0 + b][0]], [tn["pr", b, 0]], scale=0.125)
                        O, Z = PS[4 + b], PS[6 + b]
                        for c in range(2):
                            MM(O[0:64, 0:256], vAc[:, c, :], pr[b][:, c * 256:(c + 1) * 256], c == 0, c == 1,
                               [tn["vc"], tn["pr", b, 0]], [TP[4 + b][0]])
                        for c in range(2):
                            MM(Z[0:64, 0:256], ones64[:], pr[b][:, c * 256:(c + 1) * 256], c == 0, c == 1,
                               [tn["ones"], tn["pr", b, 0]], [TP[6 + b][0]])
                        RECIP(rz[b][:, 0:256], Z[0:64, 0:256], [TP[6 + b][0]], [tn["rz", b]])
                        TTo("dve", ob[:, 0:256], O[0:64, 0:256], rz[b][:, 0:256], ALU.mult, [TP[4 + b][0], tn["rz", b]],
                            [tn["oa", oi, k] for k in range(4)])
                        DMA("sp", I["mixT"][hr, TL:TL + 256], ob[:, 0:256], [tn["oa", oi, k] for k in range(4)],
                            [TD["mixT", "a", hh, TL]], ("oa", oi))

        def phase_da(l, with_ctx):
            kvo = I["kv_own"]
            li = lambda_init(l)
            with Phase() as ph:
                KT = [ph.sb([128, TL + TC], BF16) for _ in range(2)]
                VV = [ph.sb([128, TL // 128 + 2, 128], BF16) for _ in range(2)]
                QT = [ph.sb([128, TT], BF16) for _ in range(2)]
                pt = [ph.sb([128, 1024], BF16) for _ in range(6)]
                sel = ph.sb([64, 256], F32)
                rzs = ph.sb([64, 512], F32)
                lv = [ph.sb([128, 64], F32) for _ in range(4)]
                lp = [ph.sb([128, 64], F32) for _ in range(2)]
                ls = ph.sb([128, 2], F32)
                le = ph.sb([128, 2], F32)
                nlam = ph.sb([128, 1], F32)
                gsub = ph.sb([128, 1], F32)
                gs0 = ph.sb([128, 1], F32)
                e1 = [ph.sb([128, 512], F32) for _ in range(2)]
                e2 = [ph.sb([128, 512], F32) for _ in range(2)]
                eo = ph.sb([128, 512], F32)
                esq = ph.sb([128, 512], BF16)
                esd = ph.sb([128, 512], F32)
                ers = ph.sb([128, 512], F32)
                eout = [ph.sb([128, 512], BF16) for _ in range(2)]
                td = TF()
                for i, k in enumerate(("da_lq1", "da_lk1", "da_lq2", "da_lk2")):
                    DMA("sp", lv[i][:], W[k][l].partition_broadcast(128), (), [td["lv", i]], ("lv", i))
                for c in range(2):
                    TTo("dve", lp[c][:], lv[2 * c][:], lv[2 * c + 1][:], ALU.mult, [td["lv", 2 * c], td["lv", 2 * c + 1]], [td["lp", c]])
                    P.add("dve", lambda e, c=c: e.reduce_sum(out=ls[:, c:c + 1], in_=lp[c][:], axis=AX.X), [td["lp", c]], [td["ls", c]])
                ACT(le[:], ls[:], AF.Exp, [td["ls", 0], td["ls", 1]], [td["le"]])
                TTo("dve", nlam[:], le[:, 1:2], le[:, 0:1], ALU.subtract, [td["le"]], [td["nl0"]])
                TS("dve", nlam[:], nlam[:], -li, None, ALU.add, None, [td["nl0"]], [td["nlam"]])
                DMA("sp", gs0[:], W["da_subln"][l].rearrange("(p o) -> p o", o=1), (), [td["gs0"]], "gs0")
                TS("dve", gsub[:], gs0[:], 1.0 - li, None, ALU.mult, None, [td["gs0"]], [td["gsub"]])

                vsrc = kvo[R_VD:R_KA, :].rearrange("r (a f) -> (r a) f", f=512)
                NCH = TL // 128
                HC = NCH // 2

                P.add("dve", lambda e: e.memset(sel[:], 0.0), (), [td["sel"]])
                P.add("dve", lambda e: e.memset(sel[0:32, 0:128], 1.0 / 32), [td["sel"]], [td["sel"]])
                P.add("dve", lambda e: e.memset(sel[32:64, 128:256], 1.0 / 32), [td["sel"]], [td["sel"]])
                ecnt = 0
                pcnt = 0
                pend_tail = []
                for hh in range(4):
                    hb = hh % 2
                    K_, V_, Q_ = KT[hb], VV[hb], QT[hb]
                    rows = slice(hh * 128, (hh + 1) * 128)
                    for r in range(2):
                        DMA("sp", K_[:, r * (TL // 2):(r + 1) * (TL // 2)], kvo[R_KD + hh * 128:R_KD + (hh + 1) * 128, r * (TL // 2):(r + 1) * (TL // 2)],
                            (), [td["K", hb, r]], ("K", hb, r))
                    DMA("sp", K_[:, TL:], I["kdTc"][rows, :], (), [td["K", hb, 2]], ("K", hb, 2))
                    for r in range(2):
                        DMA("sp", V_[:, r * HC:(r + 1) * HC, :],
                            vsrc[r * (TL // 2):(r + 1) * (TL // 2), hh * 128:(hh + 1) * 128].rearrange("(c p) f -> p c f", p=128),
                            (), [td["V", hb, r]], ("V", hb, r))
                    DMA("sp", V_[:, NCH:NCH + 2, :], I["vdc"][:, hh * 128:(hh + 1) * 128].rearrange("(c p) f -> p c f", p=128), (), [td["V", hb, 2]], ("V", hb, 2))
                    DMA("sp", Q_[:], I["qdT"][rows, :], (), [td["Q", hb]], ("Q", hb))
                    tK = [td["K", hb, i] for i in range(3)]
                    tV = [td["V", hb, i] for i in range(3)]
                    qtiles = [(i * 512, 512, list(range(NCH + 2))) for i in range(TL // 512)]
                    if with_ctx:
                        qtiles.append((TL, 256, [NCH, NCH + 1]))
                    for (q0, N, chunks) in qtiles:
                        nch = len(chunks)
                        SK = 2
                        pslots = {}

                        def emit_s(ci):
                            kc = chunks[ci]
                            sb_ = ci % 2
                            for c in range(2):
                                MM(PS[2 * sb_ + c][:, :N], K_[c * 64:(c + 1) * 64, kc * 128:(kc + 1) * 128], Q_[c * 64:(c + 1) * 64, q0:q0 + N],
                                   True, True, tK + [td["Q", hb]], [TP[2 * sb_ + c][0]])

                        def emit_e(ci):
                            nonlocal pcnt
                            sb_ = ci % 2
                            sl = pcnt % 6
                            pcnt += 1
                            pslots[ci] = sl
                            ACT(pt[sl][:, :].rearrange("p (c n) -> p c n", c=2)[:, :, 0:N],
                                PSP[sb_][:, :].rearrange("p (c n) -> p c n", c=2)[:, :, 0:N], AF.Exp,
                                [TP[2 * sb_][0], TP[2 * sb_ + 1][0]], [td["pt", sl]], scale=0.125)

                        def emit_pv(ci):
                            kc = chunks[ci]
                            sl = pslots[ci]
                            for c in range(2):
                                MM(PS[4 + c][:, :N], V_[:, kc, :], pt[sl][:, c * 512:c * 512 + N], ci == 0, ci == nch - 1, tV + [td["pt", sl]], [TP[4 + c][0]])

                        def emit_z(ci):
                            sl = pslots[ci]
                            for c in range(2):
                                MM(PS[6][c * 32:(c + 1) * 32, :N], ones_bf[:, 0:32], pt[sl][:, c * 512:c * 512 + N], ci == 0, ci == nch - 1,
                                   [t_ones, td["pt", sl]], [TP[6][c]], tp=(0, 32 * c))

                        npair = (nch + 1) // 2
                        for p_ in range(npair + 1):
                            if p_ < npair:
                                cur = [ci for ci in (2 * p_, 2 * p_ + 1) if ci < nch]
                                for ci in cur:
                                    emit_s(ci)
                                for ci in cur:
                                    emit_e(ci)
                            if p_ >= 1:
                                prv = [ci for ci in (2 * p_ - 2, 2 * p_ - 1) if ci < nch]
                                for ci in prv:
                                    emit_pv(ci)
                                for ci in prv:
                                    emit_z(ci)
                            if p_ == 1 and pend_tail:
                                pend_tail.pop(0)()
                        eb = ecnt % 2
                        ecnt += 1
                        RECIP(rzs[:, :N], PS[6][0:64, :N], [TP[6][0], TP[6][1]], [td["rzs"]])
                        for c in range(2):
                            ebuf = e1 if c == 0 else e2
                            MM(PS[7][:, :N], sel[:, c * 128:(c + 1) * 128], rzs[:, :N], True, True, [td["sel"], td["rzs"]], [TP[7][0]])
                            P.add("dve", lambda e, ebuf=ebuf, eb=eb, N=N: e.tensor_copy(out=ebuf[eb][:, :N], in_=PS[7][:, :N]), [TP[7][0]], [td["e", c, eb]])
                            TTo("dve", ebuf[eb][:, :N], PS[4 + c][:, :N], ebuf[eb][:, :N], ALU.mult, [TP[4 + c][0], td["e", c, eb]], [td["e", c, eb]])
                        STT(eo[:, :N], e2[eb][:, :N], nlam[:, 0:1], e1[eb][:, :N], ALU.mult, ALU.add,
                            [td["e", 0, eb], td["e", 1, eb], td["nlam"]], [td["eo"]])
                        TTo("dve", esq[:, :N], eo[:, :N], eo[:, :N], ALU.mult, [td["eo"]], [td["esq"]])

                        def tail(N=N, eb=eb, hh=hh, q0=q0):
                            MM(PS[7][:, :N], ones_bf[:], esq[:, :N], True, True, [t_ones, td["esq"]], [TP[7][0]])
                            ACT(esd[:, :N], PS[7][:, :N], AF.Sqrt, [TP[7][0]], [td["esd"]], scale=1.0 / 128, bias=eps_ap[:, 0:1])
                            RECIP(ers[:, :N], esd[:, :N], [td["esd"]], [td["ers"]])
                            STT(eout[eb][:, :N], eo[:, :N], gsub[:, 0:1], ers[:, :N], ALU.mult, ALU.mult, [td["eo"], td["ers"], td["gsub"]], [td["eout", eb]])
                            DMA("sp", I["mixT"][512 + hh * 128:512 + (hh + 1) * 128, q0:q0 + N], eout[eb][:, :N], [td["eout", eb]],
                                [TD["mixT", "c", hh, q0]], ("eout", eb))

                        pend_tail.append(tail)

                while pend_tail:
                    pend_tail.pop(0)()

        def phase_wout(l, tiles):
            with Phase() as ph:
                wo = ph.sb([128, KC, D], BF16)
                xts = [ph.sb([128, KC, 1024], F32) for _ in range(2)]
                mxs = [ph.sb([128, KC, 1024], BF16) for _ in range(2)]
                t_wo, t_xts, t_mxs = Tok(), [TF(), TF()], [Tok(), Tok()]
                DMA("pool", wo[:], W["w_out"][l].rearrange("(kc p) c -> p kc c", p=128), (), [t_wo], "wo")
                mx_v = I["mixT"].rearrange("(kc p) t -> p kc t", p=128)

                def load(idx):
                    t0, T, j = tiles[idx]
                    nsub = (T + 511) // 512
                    k = idx % 2
                    DMA("sp", xts[k][:, :, 0:T], xs_v[:, :, t0:t0 + T], [TD["xs", t0]], [t_xts[k][kc, s] for kc in range(KC) for s in range(nsub)], ("xld", k))
                    DMA("sp", mxs[k][:, :, 0:T], mx_v[:, :, t0:t0 + T], (), [t_mxs[k]], ("mxld", k))

                cnt = 0
                load(0)
                for idx, (t0, T, j) in enumerate(tiles):
                    k = idx % 2
                    xt, t_xt, mx, t_mx = xts[k], t_xts[k], mxs[k], t_mxs[k]
                    nsub = (T + 511) // 512
                    cw = min(T, 512)
                    if idx + 1 < len(tiles):
                        load(idx + 1)
                    for m in range(KC):
                        for s in range(nsub):
                            b = cnt % 2
                            cnt += 1
                            c0 = s * 512
                            for kc in range(KC):
                                MM(PS[1 + b][:, :cw], wo[:, kc, m * 128:(m + 1) * 128], mx[:, kc, c0:c0 + cw], kc == 0, kc == KC - 1,
                                   [t_wo, t_mx], [TP[1 + b][0]])
                            STT(xt[:, m, c0:c0 + cw], PS[1 + b][:, :cw], modT[:, l, 5 * 8 + m, j:j + 1], xt[:, m, c0:c0 + cw],
                                ALU.mult, ALU.add, [TP[1 + b][0], t_mod, t_xt[m, s]], [t_xt[m, s]])
                    DMA("sp", xs_v[:, :, t0:t0 + T], xt[:, :, 0:T], [t_xt[kc, s] for kc in range(KC) for s in range(nsub)], [TD["xs", t0]], ("xst", k))

        def phase_final():
            with Phase() as ph:
                xt = ph.sb([128, KC, 1024], F32)
                fn = ph.sb([128, KC], F32)
                nb = norm_bufs(ph)
                sqb, t_sqb, sd, t_sd, rstd, t_rstd, tmp, t_tmp = nb
                t_xt, t_fn = TF(), Tok()
                DMA("sp", fn[:], W["fnT"], (), [t_fn], "fn")
                o_v = I["outT"].rearrange("(kc p) t -> p kc t", p=128)
                for (t0, T, j) in TILES[:-1]:
                    DMA("sp", xt[:, :, 0:T], xs_v[:, :, t0:t0 + T], [TD["xs", t0]], [t_xt[kc, s] for kc in range(KC) for s in range(2)], "xld")
                    for s in range(2):
                        c0 = s * 512
                        cw = 512
                        for kc in range(KC):
                            b = kc % 2
                            ACT(sqb[b][:, :cw], xt[:, kc, c0:c0 + cw], AF.Square, [t_xt[kc, s]], [t_sqb[b]])
                            MM(PS[0][:, :cw], ones_bf[:], sqb[b][:, :cw], kc == 0, kc == KC - 1, [t_sqb[b], t_ones], [TP[0][0]])
                        ACT(sd[:, :cw], PS[0][:, :cw], AF.Sqrt, [TP[0][0]], [t_sd], scale=1.0 / D, bias=eps_ap[:, 0:1])
                        RECIP(rstd[:, :cw], sd[:, :cw], [t_sd], [t_rstd])
                        for kc in range(KC):
                            STT(xt[:, kc, c0:c0 + cw], xt[:, kc, c0:c0 + cw], fn[:, kc:kc + 1], rstd[:, :cw], ALU.mult, ALU.mult,
                                [t_xt[kc, s], t_fn, t_rstd], [t_xt[kc, s]])
                    DMA("sp", o_v[:, :, t0:t0 + T], xt[:, :, 0:T], [t_xt[kc, s] for kc in range(KC) for s in range(2)], [TD["outT", t0]], "ost")

        for (name, l) in phases:
            last = (l == DEPTH - 1)
            if name == "mod":
                phase_mod(l)
            elif name == "modld":
                phase_modld()
            elif name == "ffn1":
                phase_ffn(l, 1, TILES)
            elif name == "proj":
                phase_proj(l)
            elif name == "xchg":
                phase_exchange()
            elif name == "gm":
                phase_gm(l, not last)
            elif name == "na":
                phase_na(l, not last)
            elif name == "da":
                phase_da(l, not last)
            elif name == "wout":
                phase_wout(l, TILES[:-1] if last else TILES)
            elif name == "ffn2":
                phase_ffn(l, 2, TILES[:-1] if last else TILES)
            elif name == "final":
                phase_final()
            else:
                raise ValueError(name)

        with Phase() as ph:
            for k in ext_out:
                shp = ISHAPES[k][0]
                nr = shp[0]
                step = max(1, nr // 4)
                for r0 in range(0, nr, step):
                    r1 = min(nr, r0 + step)
                    DMA("sp", EO[k][r0:r1, :], I[k][r0:r1, :], (), [TD[k, "cpo", r0]], "cpo")
        build_program.ninstr = P.ninstr
        build_program.used = list(used.keys())
    return nc


GRID_W = 64


def _rope_tables():
    t = np.arange(TL)
    row = (t // GRID_W).astype(np.float32)
    col = (t % GRID_W).astype(np.float32)
    nf = 16
    freqs = (np.float32(10000.0) ** (-(np.arange(nf, dtype=np.float32) / np.float32(nf)))).astype(np.float32)
    ang = np.concatenate([row[:, None] * freqs[None, :], col[:, None] * freqs[None, :]], axis=-1).astype(np.float32)
    cos = np.cos(ang).astype(np.float32)
    sin = np.sin(ang).astype(np.float32)
    cosT = np.empty((128, TL), np.float32)
    sinT = np.empty((128, TL), np.float32)
    for p in range(128):
        d = p % 64
        jx = d % 32
        cosT[p] = cos[:, jx]
        sinT[p] = -sin[:, jx] if d < 32 else sin[:, jx]
    return cosT, sinT


def _na_tables(rpb):
    NP = TL // 128
    out = np.full((128, 4, 5, 6, 128), NEG, np.float32)
    kl = np.arange(128) // 64
    kcol = np.arange(128) % 64
    rr = np.arange(128) // 64
    qcol = np.arange(128) % 64
    w0 = np.clip(qcol - 8, 0, 48)
    colok = (kcol[:, None] >= w0[None, :]) & (kcol[:, None] < w0[None, :] + 16)
    dc = np.clip(kcol[:, None] - qcol[None, :] + 15, 0, 30)
    for cls, i in enumerate((0, 1, 5, NP - 2, NP - 1)):
        s0 = min(i, NP - 2)
        for c in range(6):
            li = s0 + c
            gp = li - 2
            if gp < 0 or gp >= NP:
                continue
            krow = 2 * gp + kl
            qrow = 2 * i + rr
            kstart = np.clip(qrow - 4, 0, 2 * NP - 8)
            rowok = (krow[:, None] >= kstart[None, :]) & (krow[:, None] < kstart[None, :] + 8)
            dr = krow[:, None] - qrow[None, :] + 7
            ok = rowok & colok
            drc = np.clip(dr, 0, 14)
            for hh in range(4):
                vals = rpb[hh][drc, dc]
                out[:, hh, cls, c, :] = np.where(ok, vals, np.float32(NEG))
    return np.ascontiguousarray(out.reshape(128, 4 * 5 * 768))


def _swap_cols(wcols):
    n = wcols.shape[-1]
    idx = np.arange(n).reshape(-1, 2, 32)[:, ::-1, :].reshape(-1)
    return wcols[..., idx]


def prepare_inputs(inp):
    f = lambda a: np.ascontiguousarray(np.asarray(a, dtype=np.float32))
    x, c, ctx, c_ctx = f(inp["x"]), f(inp["c"]), f(inp["ctx"]), f(inp["c_ctx"])
    w_in = f(inp["w_in"])
    qa, ka, va, u, v, qd, kd, vd = (w_in[:, :, 0:256], w_in[:, :, 256:512], w_in[:, :, 512:768], w_in[:, :, 768:1024],
                                    w_in[:, :, 1024:1280], w_in[:, :, 1280:1792], w_in[:, :, 1792:2304], w_in[:, :, 2304:2816])
    w_in_ext = np.ascontiguousarray(np.concatenate([qa, ka, u, qd, kd, _swap_cols(qd), _swap_cols(kd), va, v, vd], axis=-1))
    shared = {
        "w_ada": f(inp["w_ada"]),
        "b_adaT": np.ascontiguousarray(f(inp["b_ada"]).reshape(DEPTH, 72, 128).transpose(0, 2, 1)),
        "w_in_ext": w_in_ext,
        "w_out": f(inp["w_out"]),
        "gm_wsT": np.ascontiguousarray(f(inp["gm_ws"]).transpose(0, 1, 3, 2)),
        "gm_bs": f(inp["gm_bs"]),
        "gm_norm": f(inp["gm_norm"]).reshape(DEPTH, 256),
        "da_subln": f(inp["da_subln"]),
        "fnT": np.ascontiguousarray(f(inp["final_norm"]).reshape(KC, 128).T),
    }
    for k in ("ffn1_w1", "ffn1_w3", "ffn1_w2", "ffn2_w1", "ffn2_w3", "ffn2_w2", "da_lq1", "da_lk1", "da_lq2", "da_lk2"):
        shared[k] = f(inp[k])
    rpb = f(inp["na_rpb"])
    cosT, sinT = _rope_tables()
    shared["cosT"] = cosT
    shared["sinT"] = sinT
    shared["nat"] = np.stack([_na_tables(rpb[l]) for l in range(DEPTH)], axis=0)
    cores = []
    xs0 = []
    for b in range(NCORES):
        m = dict(shared)
        cT = np.stack([c[b].reshape(KC, 128).T, c_ctx.reshape(KC, 128).T], axis=-1)
        m["cT"] = np.ascontiguousarray(cT)
        cores.append(m)
        xT = np.concatenate([x[b].T, ctx[b].T], axis=1)
        xs0.append(np.ascontiguousarray(xT))
    return cores, xs0


FUSED = dict(
    phases=[("mod", 0), ("mod", 1),
            ("ffn1", 0), ("proj", 0), ("gm", 0), ("na", 0), ("da", 0), ("wout", 0), ("ffn2", 0),
            ("ffn1", 1), ("proj", 1), ("gm", 1), ("na", 1), ("da", 1), ("wout", 1), ("ffn2", 1), ("final", 1)],
    ins=["xs"], outs=["outT"])


def run_launch(cfg, cores, state):
    nc = build_program(cfg["phases"], cfg["ins"], cfg["outs"], fused=False)
    in_maps = []
    ncores = len(state)
    for core in range(ncores):
        m = {}
        for name in build_program.used:
            if name in cores[core]:
                m[name] = cores[core][name]
            else:
                k, l = name.rsplit("_", 1)
                m[name] = np.ascontiguousarray(cores[core][k][int(l)])
        for k in cfg["ins"]:
            m[k + "_i"] = state[core][k]
        in_maps.append(m)
    res = run_bass_kernel_spmd(nc, in_maps, core_ids=list(range(ncores)))
    for core in range(ncores):
        for k in cfg["outs"]:
            state[core][k] = res.results[core][k + "_o"]
    return state


def kernel(**inputs):
    cores, xs0 = prepare_inputs(inputs)
    state = [{"xs": xs0[i]} for i in range(NCORES)]
    state = run_launch(FUSED, cores, state)
    out = np.empty((NCORES, TL, D), np.float32)
    for b in range(NCORES):
        out[b] = state[b]["outT"].T
    return out
```

```python
import math
import contextlib
import numpy as np
import concourse.bass as bass
import concourse.mybir as mybir
from concourse.bass_utils import run_bass_kernel_spmd

F32 = mybir.dt.float32
BF16 = mybir.dt.bfloat16
AF = mybir.ActivationFunctionType
ALU = mybir.AluOpType
AX = mybir.AxisListType

D = 1024
KC = 8
TL = 8192
NCORES = 4
TC = 256
TT = TL + TC
FF = 2816
NJ = 22
DEPTH = 2
EPS = 1e-6
NEG = -30000.0
WEXT = 3840
C_QA, C_KA, C_U, C_QD, C_KD, C_QDS, C_KDS, C_VAB, C_VD = 0, 256, 512, 768, 1280, 1792, 2304, 2816, 3328
R_KD, R_VD, R_KA, R_VA, R_END = 0, 512, 1024, 1280, 1536

ENGS = ("pe", "act", "dve", "pool", "sp")
NDSEM = 56


class Tok:
    __slots__ = ("w", "rs")

    def __init__(self):
        self.w = None
        self.rs = {}


class TF(dict):
    def __missing__(self, k):
        t = Tok()
        self[k] = t
        return t


class Ins:
    __slots__ = ("eng", "fn", "deps", "idx", "need_inc", "val", "dsem", "dval", "is_dma")

    def __init__(self, eng, fn):
        self.eng = eng
        self.fn = fn
        self.deps = []
        self.idx = -1
        self.need_inc = False
        self.val = 0
        self.dsem = None
        self.dval = 0
        self.is_dma = False


class Prog:
    def __init__(self, nc, stack):
        self.nc = nc
        self.q = {e: [] for e in ENGS}
        self.seen = {e: {} for e in ENGS}
        self.nidx = {e: 0 for e in ENGS}
        self.ecount = {e: 0 for e in ENGS}
        self.esem = {e: stack.enter_context(nc.semaphore("s_" + e)) for e in ENGS}
        self.dsems = [stack.enter_context(nc.semaphore("d%d" % i)) for i in range(NDSEM)]
        self.dval = [0] * NDSEM
        self.dlast = [None] * NDSEM
        self.keymap = {}
        self.ninstr = 0

    def add(self, eng, fn, reads=(), writes=(), dma_key=None):
        ins = Ins(eng, fn)
        ins.idx = self.nidx[eng]
        self.nidx[eng] += 1
        cand = []
        for t in reads:
            if t.w is not None:
                cand.append(t.w)
        for t in writes:
            if t.w is not None:
                cand.append(t.w)
            cand.extend(t.rs.values())
        if dma_key is not None:
            ins.is_dma = True
            if dma_key not in self.keymap:
                assert len(self.keymap) < NDSEM, "out of dma sems"
                self.keymap[dma_key] = len(self.keymap)
            si = self.keymap[dma_key]
            if self.dlast[si] is not None:
                cand.append(self.dlast[si])
            self.dval[si] += 16
            self.dlast[si] = ins
            ins.dsem = si
            ins.dval = self.dval[si]
        seen = self.seen[eng]
        best = {}
        for p in cand:
            if p is ins:
                continue
            if p.is_dma:
                k = ("d", p.dsem)
                v = p.dval
            else:
                if p.eng == eng and eng == "pe":
                    continue
                k = p.eng
                v = p.idx
            if seen.get(k, -1) >= v:
                continue
            if k not in best or (best[k].dval if p.is_dma else best[k].idx) < v:
                best[k] = p
        for k, p in best.items():
            seen[k] = p.dval if p.is_dma else p.idx
            if not p.is_dma:
                p.need_inc = True
            ins.deps.append(p)
        self.q[eng].append(ins)
        rk = ("d", ins.dsem, ins.dval) if ins.is_dma else eng
        for t in reads:
            t.rs[rk] = ins
        for t in writes:
            t.w = ins
            t.rs = {}
        return ins

    def flush(self):
        nc = self.nc
        last = {}
        for e in ENGS:
            for ins in reversed(self.q[e]):
                if not ins.is_dma:
                    ins.need_inc = True
                    last[e] = ins
                    break
        for e in ENGS:
            c = self.ecount[e]
            for ins in self.q[e]:
                if ins.need_inc and not ins.is_dma:
                    c += 1
                    ins.val = c
            self.ecount[e] = c
        used_d = [i for i in range(NDSEM) if self.dlast[i] is not None]
        prog = self

        def run(engname, eh):
            for ins in prog.q[engname]:
                for p in ins.deps:
                    if p.is_dma:
                        eh.wait_ge(prog.dsems[p.dsem], p.dval)
                    else:
                        eh.wait_ge(prog.esem[p.eng], p.val)
                r = ins.fn(eh)
                if ins.is_dma:
                    r.then_inc(prog.dsems[ins.dsem], 16)
                elif ins.need_inc:
                    r.then_inc(prog.esem[engname], 1)
                prog.ninstr += 1
            for e2 in ENGS:
                if e2 != engname and prog.ecount[e2] > 0:
                    eh.wait_ge(prog.esem[e2], prog.ecount[e2])
            for i in used_d:
                eh.wait_ge(prog.dsems[i], prog.dval[i])

        with nc.Block() as block:
            @block.tensor
            def _(eh):
                run("pe", eh)

            @block.scalar
            def _(eh):
                run("act", eh)

            @block.vector
            def _(eh):
                run("dve", eh)

            @block.gpsimd
            def _(eh):
                run("pool", eh)

            @block.sync
            def _(eh):
                run("sp", eh)

        for e in ENGS:
            for e2 in ENGS:
                self.seen[e][e2] = self.nidx[e2] - 1
            for i in range(NDSEM):
                self.seen[e][("d", i)] = self.dval[i]
            self.q[e] = []
        self.keymap = {}
        self.dlast = [None] * NDSEM


def lambda_init(l):
    return 0.8 - 0.6 * math.exp(-0.3 * l)


def build_program(phases, ext_in, ext_out, fused=False):
    nc = bass.Bass("TRN2", target_bir_lowering=False)
    st = contextlib.ExitStack()
    with st:
        P = Prog(nc, st)

        def din(name, shape, dt=F32):
            return nc.dram_tensor(name, list(shape), dt, kind="ExternalInput").ap()

        SH_L = {"w_ada": [D, 9 * D], "b_adaT": [128, 72], "ffn1_w1": [D, FF], "ffn1_w3": [D, FF], "ffn2_w1": [D, FF],
                "ffn2_w3": [D, FF], "ffn1_w2": [FF, D], "ffn2_w2": [FF, D], "w_in_ext": [D, WEXT], "w_out": [D, D],
                "nat": [128, 4 * 5 * 768], "gm_wsT": [4, 128, 128], "gm_bs": [4, 128], "gm_norm": [256],
                "da_lq1": [64], "da_lk1": [64], "da_lq2": [64], "da_lk2": [64], "da_subln": [128]}
        SH_G = {"cT": [128, KC, 2], "fnT": [128, KC], "cosT": [128, TL], "sinT": [128, TL]}
        used = {}

        class _LW:
            def __init__(self, k):
                self.k = k

            def __getitem__(self, l):
                name = "%s_%d" % (self.k, l)
                if name not in used:
                    used[name] = din(name, SH_L[self.k])
                return used[name]

        class _WD:
            def __getitem__(self, k):
                if k in SH_L:
                    return _LW(k)
                if k not in used:
                    used[k] = din(k, SH_G[k])
                return used[k]

        W = _WD()

        ISHAPES = {
            "xs": ([D, TT], F32),
            "qaT": ([256, TT], BF16),
            "uT": ([256, TT], BF16),
            "qdT": ([512, TT], BF16),
            "vbn": ([TT, 256], BF16),
            "kv_own": ([R_END, TL], BF16),
            "kdTc": ([512, TC], BF16),
            "vdc": ([TC, 512], BF16),
            "kaTc": ([256, TC], BF16),
            "vac": ([TC, 256], BF16),
            "mixT": ([D, TT], BF16),
            "outT": ([D, TL], F32),
            "modD": ([128, DEPTH * 144], F32),
        }
        I = {}
        for k, (shp, dt) in ISHAPES.items():
            I[k] = nc.dram_tensor(k, shp, dt, kind="Internal").ap()
        EI = {k: nc.dram_tensor(k + "_i", ISHAPES[k][0], ISHAPES[k][1], kind="ExternalInput").ap() for k in ext_in}
        EO = {k: nc.dram_tensor(k + "_o", ISHAPES[k][0], ISHAPES[k][1], kind="ExternalOutput").ap() for k in ext_out}
        TD = TF()

        modT = st.enter_context(nc.sbuf_tensor("modT", [128, DEPTH, 72, 2], F32))
        sc1T = st.enter_context(nc.sbuf_tensor("sc1T", [128, DEPTH, 72, 2], F32))
        ghT = st.enter_context(nc.sbuf_tensor("ghT", [128, DEPTH, 72, 2], F32))
        ones_bf = st.enter_context(nc.sbuf_tensor("ones_bf", [128, 128], BF16))
        PSP = [st.enter_context(nc.psum_tensor("psp%d" % i, [128, 1024], F32)) for i in range(4)]
        PS = [PSP[i // 2][:, (i % 2) * 512:(i % 2 + 1) * 512] for i in range(8)]
        TP = [TF() for _ in range(8)]
        t_mod = Tok()
        t_ones = Tok()

        def MM(out, lhsT, rhs, s, e_, R, Wt, tp=None):
            if tp is None:
                P.add("pe", lambda e: e.matmul(out, lhsT=lhsT, rhs=rhs, start=s, stop=e_), R, Wt)
            else:
                P.add("pe", lambda e: e.matmul(out, lhsT=lhsT, rhs=rhs, start=s, stop=e_, tile_position=tp), R, Wt)

        def ACT(out, in_, func, R, Wt, **kw):
            P.add("act", lambda e: e.activation(out=out, in_=in_, func=func, **kw), R, Wt)

        def TTo(eng, out, a, b, op, R, Wt):
            P.add(eng, lambda e: e.tensor_tensor(out=out, in0=a, in1=b, op=op), R, Wt)

        def TS(eng, out, a, s1, s2, op0, op1, R, Wt):
            if op1 is None:
                P.add(eng, lambda e: e.tensor_scalar(out=out, in0=a, scalar1=s1, scalar2=None, op0=op0), R, Wt)
            else:
                P.add(eng, lambda e: e.tensor_scalar(out=out, in0=a, scalar1=s1, scalar2=s2, op0=op0, op1=op1), R, Wt)

        def STT(out, in0, scalar, in1, op0, op1, R, Wt):
            P.add("dve", lambda e: e.scalar_tensor_tensor(out=out, in0=in0, scalar=scalar, in1=in1, op0=op0, op1=op1), R, Wt)

        def RECIP(out, in_, R, Wt):
            P.add("dve", lambda e: e.reciprocal(out=out, in_=in_), R, Wt)

        def DMA(qn, out, in_, R, Wt, key):
            P.add(qn, lambda e: e.dma_start(out=out, in_=in_), R, Wt, dma_key=key)

        class Phase:
            cnt = [0]

            def __init__(self):
                self.st = contextlib.ExitStack()
                self.n = 0
                Phase.cnt[0] += 1
                self.pid = Phase.cnt[0]

            def __enter__(self):
                self.st.__enter__()
                return self

            def sb(self, shape, dt):
                self.n += 1
                return self.st.enter_context(nc.sbuf_tensor("t%d_%d" % (self.pid, self.n), list(shape), dt))

            def __exit__(self, *a):
                P.flush()
                return self.st.__exit__(*a)

        TILES = [(i * 1024, 1024, 0) for i in range(TL // 1024)] + [(TL, TC, 1)]

        with Phase() as ph:
            P.add("dve", lambda e: e.memset(ones_bf[:], 1.0), (), [t_ones])
            for k in ext_in:
                shp = ISHAPES[k][0]
                nr = shp[0]
                step = max(1, nr // 4)
                for r0 in range(0, nr, step):
                    r1 = min(nr, r0 + step)
                    DMA("sp", I[k][r0:r1, :], EI[k][r0:r1, :], (), [TD[k, "cp", r0]], "cp")

        def norm_mod(ph, bufs, xt, t_xt, h, t_h, s, cw, l, gS, gC, j):
            sqb, t_sqb, sd, t_sd, rstd, t_rstd, tmp, t_tmp = bufs
            c0 = s * 512
            for kc in range(KC):
                b = kc % 2
                ACT(sqb[b][:, :cw], xt[:, kc, c0:c0 + cw], AF.Square, [t_xt[kc, s]], [t_sqb[b]])
                MM(PS[0][:, :cw], ones_bf[:], sqb[b][:, :cw], kc == 0, kc == KC - 1, [t_sqb[b], t_ones], [TP[0][0]])
            ACT(sd[:, :cw], PS[0][:, :cw], AF.Sqrt, [TP[0][0]], [t_sd], scale=1.0 / D, bias=eps_ap[:, 0:1])
            RECIP(rstd[:, :cw], sd[:, :cw], [t_sd], [t_rstd])
            for kc in range(KC):
                b = kc % 2
                TTo("dve", tmp[b][:, :cw], xt[:, kc, c0:c0 + cw], rstd[:, :cw], ALU.mult, [t_xt[kc, s], t_rstd], [t_tmp[b]])
                ACT(h[:, kc, c0:c0 + cw], tmp[b][:, :cw], AF.Identity, [t_tmp[b], t_mod], [t_h[kc, s]],
                    scale=sc1T[:, l, gC * 8 + kc, j:j + 1], bias=modT[:, l, gS * 8 + kc, j:j + 1])

        def norm_bufs(ph):
            sqb = [ph.sb([128, 512], BF16) for _ in range(2)]
            sd = ph.sb([128, 512], F32)
            rstd = ph.sb([128, 512], F32)
            tmp = [ph.sb([128, 512], F32) for _ in range(2)]
            return (sqb, [Tok(), Tok()], sd, Tok(), rstd, Tok(), tmp, [Tok(), Tok()])

        xs_v = I["xs"].rearrange("(kc p) t -> p kc t", p=128)

        eps_ap = st.enter_context(nc.sbuf_tensor("eps_ap", [128, 1], F32))

        with Phase() as ph:
            P.add("dve", lambda e: e.memset(eps_ap[:], EPS), (), [Tok()])

        def phase_mod(l):
            with Phase() as ph:
                cs = ph.sb([128, KC, 2], F32)
                csl = ph.sb([128, KC, 2], F32)
                wa = [ph.sb([128, KC, 1024], F32) for _ in range(2)]
                bT = ph.sb([128, 72], F32)
                t_cs, t_csl, t_b = Tok(), Tok(), Tok()
                t_wa = [Tok(), Tok()]
                DMA("sp", cs[:], W["cT"], (), [t_cs], "cs")
                DMA("sp", bT[:], W["b_adaT"][l], (), [t_b], "bT")
                ACT(csl[:], cs[:], AF.Silu, [t_cs], [t_csl])
                pm = PS[1][:, 0:144].rearrange("p (g j) -> p g j", j=2)
                for g in range(9):
                    b = g % 2
                    DMA("sp", wa[b][:], W["w_ada"][l][:, g * 1024:(g + 1) * 1024].rearrange("(kc p) c -> p kc c", p=128),
                        (), [t_wa[b]], ("wa", b))
                    for m in range(8):
                        for kc in range(KC):
                            MM(pm[:, g * 8 + m, :], wa[b][:, kc, m * 128:(m + 1) * 128], csl[:, kc, :], kc == 0, kc == KC - 1,
                               [t_wa[b], t_csl], [TP[1][0]])
                for j in range(2):
                    TTo("dve", modT[:, l, :, j], pm[:, :, j], bT[:], ALU.add, [TP[1][0], t_b], [t_mod])
                TS("dve", sc1T[:, l, :, :], modT[:, l, :, :], 1.0, None, ALU.add, None, [t_mod], [t_mod])
                TS("dve", ghT[:, l, :, :], modT[:, l, :, :], 0.5, None, ALU.mult, None, [t_mod], [t_mod])
                DMA("sp", I["modD"][:, l * 144:(l + 1) * 144], modT[:, l, :, :].rearrange("p g j -> p (g j)"), [t_mod], [TD["modD", l]], "modst")

        def phase_modld():
            with Phase() as ph:
                DMA("sp", modT[:].rearrange("p l g j -> p (l g j)"), I["modD"], (), [t_mod], "modld")
                TS("dve", sc1T[:].rearrange("p l g j -> p (l g j)"), modT[:].rearrange("p l g j -> p (l g j)"), 1.0, None, ALU.add, None, [t_mod], [t_mod])
                TS("dve", ghT[:].rearrange("p l g j -> p (l g j)"), modT[:].rearrange("p l g j -> p (l g j)"), 0.5, None, ALU.mult, None, [t_mod], [t_mod])

        def phase_ffn(l, which, tiles):
            w1 = W["ffn%d_w1" % which][l]
            w3 = W["ffn%d_w3" % which][l]
            w2 = W["ffn%d_w2" % which][l]
            gS, gC, gG = (0, 1, 2) if which == 1 else (6, 7, 8)
            with Phase() as ph:
                xts = [ph.sb([128, KC, 1024], F32) for _ in range(2)]
                h = ph.sb([128, KC, 1024], BF16)
                a = ph.sb([128, NJ, 1024], BF16)
                w13 = [ph.sb([128, KC, 256], BF16) for _ in range(3)]
                w2c = [ph.sb([128, NJ, 128], BF16) for _ in range(3)]
                sil = [ph.sb([128, 512], F32) for _ in range(2)]
                nb = norm_bufs(ph)
                t_xts, t_h, t_a = [TF(), TF()], TF(), TF()
                t_w1, t_w3, t_w2, t_sil = TF(), TF(), TF(), TF()
                cnt = 0

                def geom(idx):
                    t0, T, j = tiles[idx]
                    return t0, T, j, (T + 511) // 512, min(T, 512)

                def load(idx):
                    t0, T, j, nsub, cw = geom(idx)
                    DMA("sp", xts[idx % 2][:, :, 0:T], xs_v[:, :, t0:t0 + T], [TD["xs", t0]],
                        [t_xts[idx % 2][kc, s] for kc in range(KC) for s in range(nsub)], ("xld", idx % 2))

                def norm(idx, s):
                    t0, T, j, nsub, cw = geom(idx)
                    if s < nsub:
                        norm_mod(ph, nb, xts[idx % 2], t_xts[idx % 2], h, t_h, s, cw, l, gS, gC, j)

                load(0)
                norm(0, 0)
                norm(0, 1)
                for idx in range(len(tiles)):
                    t0, T, j, nsub, cw = geom(idx)
                    xt, t_xt = xts[idx % 2], t_xts[idx % 2]
                    nxt = idx + 1 < len(tiles)
                    if nxt:
                        load(idx + 1)
                    for jj in range(NJ):
                        sl = jj % 3
                        DMA("pool", w13[sl][:, :, 0:128], w1[:, jj * 128:(jj + 1) * 128].rearrange("(kc p) c -> p kc c", p=128),
                            (), [t_w1[sl]], ("w1", sl))
                        DMA("pool", w13[sl][:, :, 128:256], w3[:, jj * 128:(jj + 1) * 128].rearrange("(kc p) c -> p kc c", p=128),
                            (), [t_w3[sl]], ("w3", sl))
                        for s in range(nsub):
                            b = cnt % 2
                            cnt += 1
                            c0 = s * 512
                            for kc in range(KC):
                                MM(PS[1 + b][:, :cw], w13[sl][:, kc, 0:128], h[:, kc, c0:c0 + cw], kc == 0, kc == KC - 1,
                                   [t_w1[sl], t_h[kc, s]], [TP[1 + b][0]])
                            for kc in range(KC):
                                MM(PS[3 + b][:, :cw], w13[sl][:, kc, 128:256], h[:, kc, c0:c0 + cw], kc == 0, kc == KC - 1,
                                   [t_w3[sl], t_h[kc, s]], [TP[3 + b][0]])
                            ACT(sil[b][:, :cw], PS[1 + b][:, :cw], AF.Silu, [TP[1 + b][0]], [t_sil[b]])
                            TTo("dve", a[:, jj, c0:c0 + cw], sil[b][:, :cw], PS[3 + b][:, :cw], ALU.mult,
                                [t_sil[b], TP[3 + b][0]], [t_a[jj, s]])
                    for m in range(KC):
                        sl = m % 3
                        DMA("pool", w2c[sl][:], w2[:, m * 128:(m + 1) * 128].rearrange("(j p) c -> p j c", p=128),
                            (), [t_w2[sl]], ("w2", sl))
                        if nxt and m == 2:
                            norm(idx + 1, 0)
                        if nxt and m == 5:
                            norm(idx + 1, 1)
                        for s in range(nsub):
                            b = cnt % 2
                            cnt += 1
                            c0 = s * 512
                            for jj in range(NJ):
                                MM(PS[5 + b][:, :cw], w2c[sl][:, jj, :], a[:, jj, c0:c0 + cw], jj == 0, jj == NJ - 1,
                                   [t_w2[sl], t_a[jj, s]], [TP[5 + b][0]])
                            STT(xt[:, m, c0:c0 + cw], PS[5 + b][:, :cw], ghT[:, l, gG * 8 + m, j:j + 1], xt[:, m, c0:c0 + cw],
                                ALU.mult, ALU.add, [TP[5 + b][0], t_mod, t_xt[m, s]], [t_xt[m, s]])
                    DMA("sp", xs_v[:, :, t0:t0 + T], xt[:, :, 0:T],
                        [t_xt[kc, s] for kc in range(KC) for s in range(nsub)], [TD["xs", t0]], ("xst", idx % 2))

        def phase_proj(l):
            win = W["w_in_ext"][l]
            kvo = I["kv_own"]
            vd_lat = kvo[R_VD:R_KA, :].rearrange("r (a f) -> (r a) f", f=512)
            va_lat = kvo[R_VA:R_END, :].rearrange("r (a f) -> (r a) f", f=256)
            with Phase() as ph:
                wsb = ph.sb([128, KC, WEXT], BF16)
                xts = [ph.sb([128, KC, 1024], F32) for _ in range(2)]
                h = ph.sb([128, KC, 1024], BF16)
                ct = ph.sb([128, 1024], F32)
                sn = ph.sb([128, 1024], F32)
                stg = [ph.sb([128, 1024], BF16) for _ in range(4)]
                r1 = [ph.sb([128, 512], F32) for _ in range(2)]
                r2 = [ph.sb([128, 512], F32) for _ in range(2)]
                gnb = ph.sb([128, 256], F32)
                gl = [ph.sb([128, 256], F32) for _ in range(2)]
                sq = [ph.sb([128, 256], F32) for _ in range(2)]
                gn = [ph.sb([128, 256], F32) for _ in range(2)]
                ss4 = [ph.sb([128, 4], F32) for _ in range(2)]
                sd4 = [ph.sb([128, 4], F32) for _ in range(2)]
                r4 = [ph.sb([128, 4], F32) for _ in range(2)]
                sva = [ph.sb([128, 256], BF16) for _ in range(2)]
                svb = [ph.sb([128, 256], BF16) for _ in range(2)]
                svd = [ph.sb([128, 512], BF16) for _ in range(2)]
                nb = norm_bufs(ph)
                t_w, t_xts, t_h, t_stg, t_r1, t_r2 = TF(), [TF(), TF()], TF(), TF(), TF(), TF()
                t_ct, t_sn, t_gnb = Tok(), Tok(), Tok()
                tv = TF()
                NWP = 8
                wp = WEXT // NWP
                for i in range(NWP):
                    DMA("pool", wsb[:, :, i * wp:(i + 1) * wp], win[:, i * wp:(i + 1) * wp].rearrange("(kc p) c -> p kc c", p=128),
                        (), [t_w[i]], ("win", i))
                tw_all = [t_w[i] for i in range(NWP)]

                def twc(col0, ncols):
                    return [t_w[i] for i in range(col0 // wp, (col0 + ncols - 1) // wp + 1)]

                DMA("sp", gnb[:], W["gm_norm"][l].partition_broadcast(128), (), [t_gnb], "gnb")
                cnt = 0
                sgi = 0
                vcnt = 0
                def xload(idx):
                    t0_, T_, j_ = TILES[idx]
                    ns_ = (T_ + 511) // 512
                    DMA("sp", xts[idx % 2][:, :, 0:T_], xs_v[:, :, t0_:t0_ + T_], [TD["xs", t0_]],
                        [t_xts[idx % 2][kc, s] for kc in range(KC) for s in range(ns_)], ("xld", idx % 2))

                xload(0)
                for tix, (t0, T, j) in enumerate(TILES):
                    nsub = (T + 511) // 512
                    cw = min(T, 512)
                    xt, t_xt = xts[tix % 2], t_xts[tix % 2]
                    if tix + 1 < len(TILES):
                        xload(tix + 1)
                    if j == 0:
                        DMA("sp", ct[:, 0:T], W["cosT"][:, t0:t0 + T], (), [t_ct], "ct")
                        DMA("sp", sn[:, 0:T], W["sinT"][:, t0:t0 + T], (), [t_sn], "sn")
                    for s in range(nsub):
                        norm_mod(ph, nb, xt, t_xt, h, t_h, s, cw, l, 3, 4, j)
                    fm = []
                    for c in range(2):
                        fm.append(("copy", C_QA + c * 128, None, I["qaT"][c * 128:(c + 1) * 128, t0:t0 + T], ("qaT", t0)))
                    for c in range(2):
                        dst = kvo[R_KA + c * 128:R_KA + (c + 1) * 128, t0:t0 + T] if j == 0 else I["kaTc"][c * 128:(c + 1) * 128, :]
                        fm.append(("copy", C_KA + c * 128, None, dst, ("ka", t0)))
                    for c in range(2):
                        fm.append(("gelu", C_U + c * 128, None, I["uT"][c * 128:(c + 1) * 128, t0:t0 + T], ("uT", t0)))
                    for c in range(4):
                        fm.append(("rope" if j == 0 else "copy", C_QD + c * 128, C_QDS + c * 128,
                                   I["qdT"][c * 128:(c + 1) * 128, t0:t0 + T], ("qdT", t0)))
                    for c in range(4):
                        dst = kvo[R_KD + c * 128:R_KD + (c + 1) * 128, t0:t0 + T] if j == 0 else I["kdTc"][c * 128:(c + 1) * 128, :]
                        fm.append(("rope" if j == 0 else "copy", C_KD + c * 128, C_KDS + c * 128, dst, ("kd", t0)))
                    for (kind, col, cols, dst, dtk) in fm:
                        sg = stg[sgi % 4]
                        tsg = t_stg[sgi % 4]
                        sgi += 1
                        for s in range(nsub):
                            b = cnt % 2
                            cnt += 1
                            c0 = s * 512
                            for kc in range(KC):
                                MM(PS[1 + b][:, :cw], wsb[:, kc, col:col + 128], h[:, kc, c0:c0 + cw], kc == 0, kc == KC - 1,
                                   twc(col, 128) + [t_h[kc, s]], [TP[1 + b][0]])
                            if kind == "copy":
                                ACT(sg[:, c0:c0 + cw], PS[1 + b][:, :cw], AF.Copy, [TP[1 + b][0]], [tsg])
                            elif kind == "gelu":
                                ACT(sg[:, c0:c0 + cw], PS[1 + b][:, :cw], AF.Gelu, [TP[1 + b][0]], [tsg])
                            else:
                                for kc in range(KC):
                                    MM(PS[3 + b][:, :cw], wsb[:, kc, cols:cols + 128], h[:, kc, c0:c0 + cw], kc == 0, kc == KC - 1,
                                       twc(cols, 128) + [t_h[kc, s]], [TP[3 + b][0]])
                                TTo("dve", r1[b][:, :cw], PS[1 + b][:, :cw], ct[:, c0:c0 + cw], ALU.mult, [TP[1 + b][0], t_ct], [t_r1[b]])
                                TTo("dve", r2[b][:, :cw], PS[3 + b][:, :cw], sn[:, c0:c0 + cw], ALU.mult, [TP[3 + b][0], t_sn], [t_r2[b]])
                                TTo("pool", sg[:, c0:c0 + cw], r1[b][:, :cw], r2[b][:, :cw], ALU.add, [t_r1[b], t_r2[b]], [tsg])
                        DMA("sp", dst, sg[:, 0:T], [tsg], [TD[dtk]], ("stg", (sgi - 1) % 4))
                    for tb in range(T // 128):
                        b = vcnt % 2
                        vcnt += 1
                        tk0 = tb * 128
                        hs = [t_h[kc, tk0 // 512] for kc in range(KC)]
                        for kc in range(KC):
                            MM(PS[5 + b][:, :], h[:, kc, tk0:tk0 + 128], wsb[:, kc, C_VAB:C_VAB + 512], kc == 0, kc == KC - 1,
                               twc(C_VAB, 512) + [hs[kc]], [TP[5 + b][0]])
                        ACT(sva[b][:], PS[5 + b][:, 0:256], AF.Copy, [TP[5 + b][0]], [tv["sva", b]])
                        dst = va_lat[t0 + tk0:t0 + tk0 + 128, :] if j == 0 else I["vac"][tk0:tk0 + 128, :]
                        DMA("sp", dst, sva[b][:], [tv["sva", b]], [TD["va", t0]], ("sva", b))
                        ACT(gl[b][:], PS[5 + b][:, 256:512], AF.Gelu, [TP[5 + b][0]], [tv["gl", b]])
                        TTo("dve", sq[b][:], gl[b][:], gl[b][:], ALU.mult, [tv["gl", b]], [tv["sq", b]])
                        P.add("dve", lambda e, b=b: e.reduce_sum(out=ss4[b][:], in_=sq[b][:].rearrange("p (g w) -> p g w", w=64), axis=AX.X),
                              [tv["sq", b]], [tv["ss4", b]])
                        ACT(sd4[b][:], ss4[b][:], AF.Sqrt, [tv["ss4", b]], [tv["sd4", b]], scale=1.0 / 64, bias=eps_ap[:, 0:1])
                        RECIP(r4[b][:], sd4[b][:], [tv["sd4", b]], [tv["r4", b]])
                        for g in range(4):
                            TS("dve", gn[b][:, g * 64:(g + 1) * 64], gl[b][:, g * 64:(g + 1) * 64], r4[b][:, g:g + 1], None, ALU.mult, None,
                               [tv["gl", b], tv["r4", b]], [tv["gn", b, g]])
                        TTo("dve", svb[b][:], gn[b][:], gnb[:], ALU.mult, [tv["gn", b, g] for g in range(4)] + [t_gnb], [tv["svb", b]])
                        DMA("sp", I["vbn"][t0 + tk0:t0 + tk0 + 128, :], svb[b][:], [tv["svb", b]], [TD["vbn", t0]], ("svb", b))
                        for kc in range(KC):
                            MM(PS[3 + b][:, :], h[:, kc, tk0:tk0 + 128], wsb[:, kc, C_VD:C_VD + 512], kc == 0, kc == KC - 1,
                               twc(C_VD, 512) + [hs[kc]], [TP[3 + b][0]])
                        ACT(svd[b][:], PS[3 + b][:, :], AF.Copy, [TP[3 + b][0]], [tv["svd", b]])
                        dst = vd_lat[t0 + tk0:t0 + tk0 + 128, :] if j == 0 else I["vdc"][tk0:tk0 + 128, :]
                        DMA("sp", dst, svd[b][:], [tv["svd", b]], [TD["vd", t0]], ("svd", b))

        def phase_gm(l, with_ctx):
            with Phase() as ph:
                wsT = ph.sb([128, 4, 128], BF16)
                bsb = ph.sb([64, 4, 4, 128], F32)
                vb = [ph.sb([128, 4, 256], BF16) for _ in range(2)]
                us = [ph.sb([64, 4, 512], BF16) for _ in range(2)]
                tq = [ph.sb([64, 512], F32) for _ in range(2)]
                ob = [ph.sb([64, 4, 512], BF16) for _ in range(2)]
                t_ws, t_bs = Tok(), TF()
                tg = TF()
                DMA("pool", wsT[:], W["gm_wsT"][l].rearrange("g q p -> q g p"), (), [t_ws], "wsT")
                for g in range(4):
                    for c in range(4):
                        DMA("sp", bsb[:, g, c, :], W["gm_bs"][l][g].partition_broadcast(64), (), [t_bs[g, c]], ("bsb", (g * 4 + c) % 4))
                tbs = [t_bs[g, c] for g in range(4) for c in range(4)]
                uT_v = I["uT"].rearrange("(g w) t -> w g t", w=64)
                mx_v = I["mixT"][256:512, :].rearrange("(g w) t -> w g t", w=64)
                tiles = [(i * 512, 512) for i in range(TL // 512)] + ([(TL, TC)] if with_ctx else [])
                cnt = 0
                for ti, (t0, T) in enumerate(tiles):
                    bb = ti % 2
                    nck = T // 128
                    DMA("sp", vb[bb][:, 0:nck, :], I["vbn"][t0:t0 + T, :].rearrange("(c p) f -> p c f", p=128), (), [tg["vb", bb]], ("vb", bb))
                    DMA("sp", us[bb][:, :, 0:T], uT_v[:, :, t0:t0 + T], (), [tg["us", bb]], ("us", bb))
                    for g in range(4):
                        b = cnt % 2
                        cnt += 1
                        for c in range(nck):
                            MM(PS[1 + b][0:64, c * 128:(c + 1) * 128], vb[bb][:, c, g * 64:(g + 1) * 64], wsT[:, g, :], True, True,
                               [tg["vb", bb], t_ws], [TP[1 + b][0]])
                        TTo("dve", tq[b][:, 0:T], PS[1 + b][0:64, 0:T], bsb[:, g, 0:nck, :].rearrange("p c q -> p (c q)"), ALU.add,
                            [TP[1 + b][0]] + tbs, [tg["tq", b]])
                        TTo("dve", ob[bb][:, g, 0:T], tq[b][:, 0:T], us[bb][:, g, 0:T], ALU.mult, [tg["tq", b], tg["us", bb]], [tg["ob", bb, g]])
                    DMA("sp", mx_v[:, :, t0:t0 + T], ob[bb][:, :, 0:T], [tg["ob", bb, g] for g in range(4)], [TD["mixT", "b", t0]], ("ob", bb))

        def phase_na(l, with_ctx):
            kvo = I["kv_own"]
            NP = TL // 128
            NK = NP + 4
            with Phase() as ph:
                kT = ph.sb([64, NK * 128], BF16)
                vA = ph.sb([128, NK, 64], BF16)
                kTc = ph.sb([64, 256], BF16)
                vAc = ph.sb([128, 2, 64], BF16)
                tab = ph.sb([128, 5, 768], F32)
                qT = ph.sb([64, TT], BF16)
                ones64 = ph.sb([128, 64], BF16)
                sc = [ph.sb([128, 768], F32) for _ in range(2)]
                pr = [ph.sb([128, 1024], BF16) for _ in range(2)]
                rz = [ph.sb([64, 256], F32) for _ in range(2)]
                oa = [ph.sb([64, 512], BF16) for _ in range(2)]
                tn = TF()
                P.add("pool", lambda e: e.memset(ones64[:], 1.0), (), [tn["ones"]])
                P.add("pool", lambda e: e.memset(kT[:, 0:256], 0.0), (), [tn["kpad", 0]])
                P.add("pool", lambda e: e.memset(kT[:, (NK - 2) * 128:NK * 128], 0.0), (), [tn["kpad", 1]])
                P.add("pool", lambda e: e.memset(vA[:, 0:2, :], 0.0), (), [tn["vpad", 0]])
                P.add("pool", lambda e: e.memset(vA[:, NK - 2:NK, :], 0.0), (), [tn["vpad", 1]])
                va_v = kvo[R_VA:R_END, :].rearrange("r (a f) -> (r a) f", f=256)
                tk = [tn["k"], tn["kpad", 0], tn["kpad", 1]]
                tvv = [tn["v"], tn["vpad", 0], tn["vpad", 1]]
                cnt = 0
                ocnt = 0
                for hh in range(4):
                    hr = slice(hh * 64, (hh + 1) * 64)
                    DMA("sp", kT[:, 256:(NK - 2) * 128], kvo[R_KA + hh * 64:R_KA + (hh + 1) * 64, :], (), [tn["k"]], "k1")
                    DMA("sp", vA[:, 2:NK - 2, :], va_v[:, hr].rearrange("(c p) f -> p c f", p=128), (), [tn["v"]], "v1")
                    DMA("sp", kTc[:], I["kaTc"][hr, :], (), [tn["kc"]], "kc")
                    DMA("sp", vAc[:], I["vac"][:, hr].rearrange("(c p) f -> p c f", p=128), (), [tn["vc"]], "vc")
                    DMA("sp", tab[:].rearrange("p c n -> p (c n)"), W["nat"][l][:, hh * 3840:(hh + 1) * 3840], (), [tn["tab"]], "tab")
                    DMA("sp", qT[:], I["qaT"][hr, :], (), [tn["q"]], "q")
                    pend = None
                    for i in range(NP):
                        cls = 0 if i == 0 else 1 if i == 1 else 3 if i == NP - 2 else 4 if i == NP - 1 else 2
                        s0 = min(i, NP - 2)
                        if i % 4 == 0:
                            oi = ocnt % 2
                            ocnt += 1
                        ob = oa[oi]
                        b = cnt % 2
                        cnt += 1
                        A, B_ = PS[0 + b], PS[2 + b]
                        q_ap = qT[:, i * 128:(i + 1) * 128]
                        for c in range(6):
                            dst = A[:, c * 128:(c + 1) * 128] if c < 4 else B_[:, (c - 4) * 128:(c - 3) * 128]
                            MM(dst, kT[:, (s0 + c) * 128:(s0 + c + 1) * 128], q_ap, True, True, tk + [tn["q"]],
                               [TP[0 + b][0] if c < 4 else TP[2 + b][0]])
                        for c in range(2):
                            MM(B_[:, (2 + c) * 128:(3 + c) * 128], kTc[:, c * 128:(c + 1) * 128], q_ap, True, True,
                               [tn["kc"], tn["q"]], [TP[2 + b][0]])
                        STT(sc[b][:, 0:512], A[:, :], 0.125, tab[:, cls, 0:512], ALU.mult, ALU.add, [TP[0 + b][0], tn["tab"]], [tn["sc", b, 0]])
                        STT(sc[b][:, 512:768], B_[:, 0:256], 0.125, tab[:, cls, 512:768], ALU.mult, ALU.add, [TP[2 + b][0], tn["tab"]], [tn["sc", b, 1]])
                        ACT(pr[b][:, 0:768], sc[b][:, :], AF.Exp, [tn["sc", b, 0], tn["sc", b, 1]], [tn["pr", b, 0]])
                        ACT(pr[b][:, 768:1024], B_[:, 256:512], AF.Exp, [TP[2 + b][0]], [tn["pr", b, 1]], scale=0.125)
                        def st2(i=i, b=b, ob=ob, oi=oi, s0=s0, hr=hr, hh=hh):
                            O, Z = PS[4 + b], PS[6 + b]
                            for c in range(8):
                                vch = vA[:, s0 + c, :] if c < 6 else vAc[:, c - 6, :]
                                MM(O[0:64, 0:128], vch, pr[b][:, c * 128:(c + 1) * 128], c == 0, c == 7,
                                   tvv + [tn["vc"], tn["pr", b, 0], tn["pr", b, 1]], [TP[4 + b][0]])
                            for c in range(8):
                                MM(Z[0:64, 0:128], ones64[:], pr[b][:, c * 128:(c + 1) * 128], c == 0, c == 7,
                                   [tn["ones"], tn["pr", b, 0], tn["pr", b, 1]], [TP[6 + b][0]])
                            RECIP(rz[b][:, 0:128], Z[0:64, 0:128], [TP[6 + b][0]], [tn["rz", b]])
                            TTo("dve", ob[:, (i % 4) * 128:(i % 4 + 1) * 128], O[0:64, 0:128], rz[b][:, 0:128], ALU.mult,
                                [TP[4 + b][0], tn["rz", b]], [tn["oa", oi, i % 4]])
                            if i % 4 == 3:
                                t0 = (i // 4) * 512
                                DMA("sp", I["mixT"][hr, t0:t0 + 512], ob[:, :], [tn["oa", oi, k] for k in range(4)],
                                    [TD["mixT", "a", hh, t0]], ("oa", oi))

                        if pend is not None:
                            pend()
                        pend = st2
                    pend()
                    pend = None
                    if with_ctx:
                        oi = ocnt % 2
                        ocnt += 1
                        ob = oa[oi]
                        b = cnt % 2
                        cnt += 1
                        A = PS[0 + b]
                        q_ap = qT[:, TL:TL + 256]
                        for c in range(2):
                            MM(A[:, c * 256:(c + 1) * 256], kTc[:, c * 128:(c + 1) * 128], q_ap, True, True, [tn["kc"], tn["q"]], [TP[0 + b][0]])
                        ACT(pr[b][:, 0:512], A[:, :], AF.Exp, [TP[0 + b][0]], [tn["pr", b, 0]], scale=0.125)
                        O, Z = PS[4 + b], PS[6 + b]
                        for c in range(2):
                            MM(O[0:64, 0:256], vAc[:, c, :], pr[b][:, c * 256:(c + 1) * 256], c == 0, c == 1,
                               [tn["vc"], tn["pr", b, 0]], [TP[4 + b][0]])
                        for c in range(2):
                            MM(Z[0:64, 0:256], ones64[:], pr[b][:, c * 256:(c + 1) * 256], c == 0, c == 1,
                               [tn["ones"], tn["pr", b, 0]], [TP[6 + b][0]])
                        RECIP(rz[b][:, 0:256], Z[0:64, 0:256], [TP[6 + b][0]], [tn["rz", b]])
                        TTo("dve", ob[:, 0:256], O[0:64, 0:256], rz[b][:, 0:256], ALU.mult, [TP[4 + b][0], tn["rz", b]],
                            [tn["oa", oi, k] for k in range(4)])
                        DMA("sp", I["mixT"][hr, TL:TL + 256], ob[:, 0:256], [tn["oa", oi, k] for k in range(4)],
                            [TD["mixT", "a", hh, TL]], ("oa", oi))

        def phase_da(l, with_ctx):
            kvo = I["kv_own"]
            li = lambda_init(l)
            with Phase() as ph:
                KT = [ph.sb([128, TL + TC], BF16) for _ in range(2)]
                VV = [ph.sb([128, TL // 128 + 2, 128], BF16) for _ in range(2)]
                QT = [ph.sb([128, TT], BF16) for _ in range(2)]
                pt = [ph.sb([128, 1024], BF16) for _ in range(6)]
                sel = ph.sb([64, 256], F32)
                rzs = ph.sb([64, 512], F32)
                lv = [ph.sb([128, 64], F32) for _ in range(4)]
                lp = [ph.sb([128, 64], F32) for _ in range(2)]
                ls = ph.sb([128, 2], F32)
                le = ph.sb([128, 2], F32)
                nlam = ph.sb([128, 1], F32)
                gsub = ph.sb([128, 1], F32)
                gs0 = ph.sb([128, 1], F32)
                e1 = [ph.sb([128, 512], F32) for _ in range(2)]
                e2 = [ph.sb([128, 512], F32) for _ in range(2)]
                eo = ph.sb([128, 512], F32)
                esq = ph.sb([128, 512], BF16)
                esd = ph.sb([128, 512], F32)
                ers = ph.sb([128, 512], F32)
                eout = [ph.sb([128, 512], BF16) for _ in range(2)]
                td = TF()
                for i, k in enumerate(("da_lq1", "da_lk1", "da_lq2", "da_lk2")):
                    DMA("sp", lv[i][:], W[k][l].partition_broadcast(128), (), [td["lv", i]], ("lv", i))
                for c in range(2):
                    TTo("dve", lp[c][:], lv[2 * c][:], lv[2 * c + 1][:], ALU.mult, [td["lv", 2 * c], td["lv", 2 * c + 1]], [td["lp", c]])
                    P.add("dve", lambda e, c=c: e.reduce_sum(out=ls[:, c:c + 1], in_=lp[c][:], axis=AX.X), [td["lp", c]], [td["ls", c]])
                ACT(le[:], ls[:], AF.Exp, [td["ls", 0], td["ls", 1]], [td["le"]])
                TTo("dve", nlam[:], le[:, 1:2], le[:, 0:1], ALU.subtract, [td["le"]], [td["nl0"]])
                TS("dve", nlam[:], nlam[:], -li, None, ALU.add, None, [td["nl0"]], [td["nlam"]])
                DMA("sp", gs0[:], W["da_subln"][l].rearrange("(p o) -> p o", o=1), (), [td["gs0"]], "gs0")
                TS("dve", gsub[:], gs0[:], 1.0 - li, None, ALU.mult, None, [td["gs0"]], [td["gsub"]])

                vsrc = kvo[R_VD:R_KA, :].rearrange("r (a f) -> (r a) f", f=512)
                NCH = TL // 128
                HC = NCH // 2

                P.add("dve", lambda e: e.memset(sel[:], 0.0), (), [td["sel"]])
                P.add("dve", lambda e: e.memset(sel[0:32, 0:128], 1.0 / 32), [td["sel"]], [td["sel"]])
                P.add("dve", lambda e: e.memset(sel[32:64, 128:256], 1.0 / 32), [td["sel"]], [td["sel"]])
                ecnt = 0
                pcnt = 0
                pend_tail = []
                for hh in range(4):
                    hb = hh % 2
                    K_, V_, Q_ = KT[hb], VV[hb], QT[hb]
                    rows = slice(hh * 128, (hh + 1) * 128)
                    for r in range(2):
                        DMA("sp", K_[:, r * (TL // 2):(r + 1) * (TL // 2)], kvo[R_KD + hh * 128:R_KD + (hh + 1) * 128, r * (TL // 2):(r + 1) * (TL // 2)],
                            (), [td["K", hb, r]], ("K", hb, r))
                    DMA("sp", K_[:, TL:], I["kdTc"][rows, :], (), [td["K", hb, 2]], ("K", hb, 2))
                    for r in range(2):
                        DMA("sp", V_[:, r * HC:(r + 1) * HC, :],
                            vsrc[r * (TL // 2):(r + 1) * (TL // 2), hh * 128:(hh + 1) * 128].rearrange("(c p) f -> p c f", p=128),
                            (), [td["V", hb, r]], ("V", hb, r))
                    DMA("sp", V_[:, NCH:NCH + 2, :], I["vdc"][:, hh * 128:(hh + 1) * 128].rearrange("(c p) f -> p c f", p=128), (), [td["V", hb, 2]], ("V", hb, 2))
                    DMA("sp", Q_[:], I["qdT"][rows, :], (), [td["Q", hb]], ("Q", hb))
                    tK = [td["K", hb, i] for i in range(3)]
                    tV = [td["V", hb, i] for i in range(3)]
                    qtiles = [(i * 512, 512, list(range(NCH + 2))) for i in range(TL // 512)]
                    if with_ctx:
                        qtiles.append((TL, 256, [NCH, NCH + 1]))
                    for (q0, N, chunks) in qtiles:
                        nch = len(chunks)
                        SK = 2
                        pslots = {}

                        def emit_s(ci):
                            kc = chunks[ci]
                            sb_ = ci % 2
                            for c in range(2):
                                MM(PS[2 * sb_ + c][:, :N], K_[c * 64:(c + 1) * 64, kc * 128:(kc + 1) * 128], Q_[c * 64:(c + 1) * 64, q0:q0 + N],
                                   True, True, tK + [td["Q", hb]], [TP[2 * sb_ + c][0]])

                        def emit_e(ci):
                            nonlocal pcnt
                            sb_ = ci % 2
                            sl = pcnt % 6
                            pcnt += 1
                            pslots[ci] = sl
                            ACT(pt[sl][:, :].rearrange("p (c n) -> p c n", c=2)[:, :, 0:N],
                                PSP[sb_][:, :].rearrange("p (c n) -> p c n", c=2)[:, :, 0:N], AF.Exp,
                                [TP[2 * sb_][0], TP[2 * sb_ + 1][0]], [td["pt", sl]], scale=0.125)

                        def emit_pv(ci):
                            kc = chunks[ci]
                            sl = pslots[ci]
                            for c in range(2):
                                MM(PS[4 + c][:, :N], V_[:, kc, :], pt[sl][:, c * 512:c * 512 + N], ci == 0, ci == nch - 1, tV + [td["pt", sl]], [TP[4 + c][0]])

                        def emit_z(ci):
                            sl = pslots[ci]
                            for c in range(2):
                                MM(PS[6][c * 32:(c + 1) * 32, :N], ones_bf[:, 0:32], pt[sl][:, c * 512:c * 512 + N], ci == 0, ci == nch - 1,
                                   [t_ones, td["pt", sl]], [TP[6][c]], tp=(0, 32 * c))

                        npair = (nch + 1) // 2
                        for p_ in range(npair + 1):
                            if p_ < npair:
                                cur = [ci for ci in (2 * p_, 2 * p_ + 1) if ci < nch]
                                for ci in cur:
                                    emit_s(ci)
                                for ci in cur:
                                    emit_e(ci)
                            if p_ >= 1:
                                prv = [ci for ci in (2 * p_ - 2, 2 * p_ - 1) if ci < nch]
                                for ci in prv:
                                    emit_pv(ci)
                                for ci in prv:
                                    emit_z(ci)
                            if p_ == 1 and pend_tail:
                                pend_tail.pop(0)()
                        eb = ecnt % 2
                        ecnt += 1
                        RECIP(rzs[:, :N], PS[6][0:64, :N], [TP[6][0], TP[6][1]], [td["rzs"]])
                        for c in range(2):
                            ebuf = e1 if c == 0 else e2
                            MM(PS[7][:, :N], sel[:, c * 128:(c + 1) * 128], rzs[:, :N], True, True, [td["sel"], td["rzs"]], [TP[7][0]])
                            P.add("dve", lambda e, ebuf=ebuf, eb=eb, N=N: e.tensor_copy(out=ebuf[eb][:, :N], in_=PS[7][:, :N]), [TP[7][0]], [td["e", c, eb]])
                            TTo("dve", ebuf[eb][:, :N], PS[4 + c][:, :N], ebuf[eb][:, :N], ALU.mult, [TP[4 + c][0], td["e", c, eb]], [td["e", c, eb]])
                        STT(eo[:, :N], e2[eb][:, :N], nlam[:, 0:1], e1[eb][:, :N], ALU.mult, ALU.add,
                            [td["e", 0, eb], td["e", 1, eb], td["nlam"]], [td["eo"]])
                        TTo("dve", esq[:, :N], eo[:, :N], eo[:, :N], ALU.mult, [td["eo"]], [td["esq"]])

                        def tail(N=N, eb=eb, hh=hh, q0=q0):
                            MM(PS[7][:, :N], ones_bf[:], esq[:, :N], True, True, [t_ones, td["esq"]], [TP[7][0]])
                            ACT(esd[:, :N], PS[7][:, :N], AF.Sqrt, [TP[7][0]], [td["esd"]], scale=1.0 / 128, bias=eps_ap[:, 0:1])
                            RECIP(ers[:, :N], esd[:, :N], [td["esd"]], [td["ers"]])
                            STT(eout[eb][:, :N], eo[:, :N], gsub[:, 0:1], ers[:, :N], ALU.mult, ALU.mult, [td["eo"], td["ers"], td["gsub"]], [td["eout", eb]])
                            DMA("sp", I["mixT"][512 + hh * 128:512 + (hh + 1) * 128, q0:q0 + N], eout[eb][:, :N], [td["eout", eb]],
                                [TD["mixT", "c", hh, q0]], ("eout", eb))

                        pend_tail.append(tail)

                while pend_tail:
                    pend_tail.pop(0)()

        def phase_wout(l, tiles):
            with Phase() as ph:
                wo = ph.sb([128, KC, D], BF16)
                xts = [ph.sb([128, KC, 1024], F32) for _ in range(2)]
                mxs = [ph.sb([128, KC, 1024], BF16) for _ in range(2)]
                t_wo, t_xts, t_mxs = Tok(), [TF(), TF()], [Tok(), Tok()]
                DMA("pool", wo[:], W["w_out"][l].rearrange("(kc p) c -> p kc c", p=128), (), [t_wo], "wo")
                mx_v = I["mixT"].rearrange("(kc p) t -> p kc t", p=128)

                def load(idx):
                    t0, T, j = tiles[idx]
                    nsub = (T + 511) // 512
                    k = idx % 2
                    DMA("sp", xts[k][:, :, 0:T], xs_v[:, :, t0:t0 + T], [TD["xs", t0]], [t_xts[k][kc, s] for kc in range(KC) for s in range(nsub)], ("xld", k))
                    DMA("sp", mxs[k][:, :, 0:T], mx_v[:, :, t0:t0 + T], (), [t_mxs[k]], ("mxld", k))

                cnt = 0
                load(0)
                for idx, (t0, T, j) in enumerate(tiles):
                    k = idx % 2
                    xt, t_xt, mx, t_mx = xts[k], t_xts[k], mxs[k], t_mxs[k]
                    nsub = (T + 511) // 512
                    cw = min(T, 512)
                    if idx + 1 < len(tiles):
                        load(idx + 1)
                    for m in range(KC):
                        for s in range(nsub):
                            b = cnt % 2
                            cnt += 1
                            c0 = s * 512
                            for kc in range(KC):
                                MM(PS[1 + b][:, :cw], wo[:, kc, m * 128:(m + 1) * 128], mx[:, kc, c0:c0 + cw], kc == 0, kc == KC - 1,
                                   [t_wo, t_mx], [TP[1 + b][0]])
                            STT(xt[:, m, c0:c0 + cw], PS[1 + b][:, :cw], modT[:, l, 5 * 8 + m, j:j + 1], xt[:, m, c0:c0 + cw],
                                ALU.mult, ALU.add, [TP[1 + b][0], t_mod, t_xt[m, s]], [t_xt[m, s]])
                    DMA("sp", xs_v[:, :, t0:t0 + T], xt[:, :, 0:T], [t_xt[kc, s] for kc in range(KC) for s in range(nsub)], [TD["xs", t0]], ("xst", k))

        def phase_final():
            with Phase() as ph:
                xt = ph.sb([128, KC, 1024], F32)
                fn = ph.sb([128, KC], F32)
                nb = norm_bufs(ph)
                sqb, t_sqb, sd, t_sd, rstd, t_rstd, tmp, t_tmp = nb
                t_xt, t_fn = TF(), Tok()
                DMA("sp", fn[:], W["fnT"], (), [t_fn], "fn")
                o_v = I["outT"].rearrange("(kc p) t -> p kc t", p=128)
                for (t0, T, j) in TILES[:-1]:
                    DMA("sp", xt[:, :, 0:T], xs_v[:, :, t0:t0 + T], [TD["xs", t0]], [t_xt[kc, s] for kc in range(KC) for s in range(2)], "xld")
                    for s in range(2):
                        c0 = s * 512
                        cw = 512
                        for kc in range(KC):
                            b = kc % 2
                            ACT(sqb[b][:, :cw], xt[:, kc, c0:c0 + cw], AF.Square, [t_xt[kc, s]], [t_sqb[b]])
                            MM(PS[0][:, :cw], ones_bf[:], sqb[b][:, :cw], kc == 0, kc == KC - 1, [t_sqb[b], t_ones], [TP[0][0]])
                        ACT(sd[:, :cw], PS[0][:, :cw], AF.Sqrt, [TP[0][0]], [t_sd], scale=1.0 / D, bias=eps_ap[:, 0:1])
                        RECIP(rstd[:, :cw], sd[:, :cw], [t_sd], [t_rstd])
                        for kc in range(KC):
                            STT(xt[:, kc, c0:c0 + cw], xt[:, kc, c0:c0 + cw], fn[:, kc:kc + 1], rstd[:, :cw], ALU.mult, ALU.mult,
                                [t_xt[kc, s], t_fn, t_rstd], [t_xt[kc, s]])
                    DMA("sp", o_v[:, :, t0:t0 + T], xt[:, :, 0:T], [t_xt[kc, s] for kc in range(KC) for s in range(2)], [TD["outT", t0]], "ost")

        for (name, l) in phases:
            last = (l == DEPTH - 1)
            if name == "mod":
                phase_mod(l)
            elif name == "modld":
                phase_modld()
            elif name == "ffn1":
                phase_ffn(l, 1, TILES)
            elif name == "proj":
                phase_proj(l)
            elif name == "gm":
                phase_gm(l, not last)
            elif name == "na":
                phase_na(l, not last)
            elif name == "da":
                phase_da(l, not last)
            elif name == "wout":
                phase_wout(l, TILES[:-1] if last else TILES)
            elif name == "ffn2":
                phase_ffn(l, 2, TILES[:-1] if last else TILES)
            elif name == "final":
                phase_final()
            else:
                raise ValueError(name)

        with Phase() as ph:
            for k in ext_out:
                shp = ISHAPES[k][0]
                nr = shp[0]
                step = max(1, nr // 4)
                for r0 in range(0, nr, step):
                    r1 = min(nr, r0 + step)
                    DMA("sp", EO[k][r0:r1, :], I[k][r0:r1, :], (), [TD[k, "cpo", r0]], "cpo")
        build_program.ninstr = P.ninstr
        build_program.used = list(used.keys())
    return nc


GRID_W = 64


def _rope_tables():
    t = np.arange(TL)
    row = (t // GRID_W).astype(np.float32)
    col = (t % GRID_W).astype(np.float32)
    nf = 16
    freqs = (np.float32(10000.0) ** (-(np.arange(nf, dtype=np.float32) / np.float32(nf)))).astype(np.float32)
    ang = np.concatenate([row[:, None] * freqs[None, :], col[:, None] * freqs[None, :]], axis=-1).astype(np.float32)
    cos = np.cos(ang).astype(np.float32)
    sin = np.sin(ang).astype(np.float32)
    cosT = np.empty((128, TL), np.float32)
    sinT = np.empty((128, TL), np.float32)
    for p in range(128):
        d = p % 64
        jx = d % 32
        cosT[p] = cos[:, jx]
        sinT[p] = -sin[:, jx] if d < 32 else sin[:, jx]
    return cosT, sinT


def _na_tables(rpb):
    NP = TL // 128
    out = np.full((128, 4, 5, 6, 128), NEG, np.float32)
    kl = np.arange(128) // 64
    kcol = np.arange(128) % 64
    rr = np.arange(128) // 64
    qcol = np.arange(128) % 64
    w0 = np.clip(qcol - 8, 0, 48)
    colok = (kcol[:, None] >= w0[None, :]) & (kcol[:, None] < w0[None, :] + 16)
    dc = np.clip(kcol[:, None] - qcol[None, :] + 15, 0, 30)
    for cls, i in enumerate((0, 1, 5, NP - 2, NP - 1)):
        s0 = min(i, NP - 2)
        for c in range(6):
            li = s0 + c
            gp = li - 2
            if gp < 0 or gp >= NP:
                continue
            krow = 2 * gp + kl
            qrow = 2 * i + rr
            kstart = np.clip(qrow - 4, 0, 2 * NP - 8)
            rowok = (krow[:, None] >= kstart[None, :]) & (krow[:, None] < kstart[None, :] + 8)
            dr = krow[:, None] - qrow[None, :] + 7
            ok = rowok & colok
            drc = np.clip(dr, 0, 14)
            for hh in range(4):
                vals = rpb[hh][drc, dc]
                out[:, hh, cls, c, :] = np.where(ok, vals, np.float32(NEG))
    return np.ascontiguousarray(out.reshape(128, 4 * 5 * 768))


def _swap_cols(wcols):
    n = wcols.shape[-1]
    idx = np.arange(n).reshape(-1, 2, 32)[:, ::-1, :].reshape(-1)
    return wcols[..., idx]


def prepare_inputs(inp):
    f = lambda a: np.ascontiguousarray(np.asarray(a, dtype=np.float32))
    x, c, ctx, c_ctx = f(inp["x"]), f(inp["c"]), f(inp["ctx"]), f(inp["c_ctx"])
    w_in = f(inp["w_in"])
    qa, ka, va, u, v, qd, kd, vd = (w_in[:, :, 0:256], w_in[:, :, 256:512], w_in[:, :, 512:768], w_in[:, :, 768:1024],
                                    w_in[:, :, 1024:1280], w_in[:, :, 1280:1792], w_in[:, :, 1792:2304], w_in[:, :, 2304:2816])
    w_in_ext = np.ascontiguousarray(np.concatenate([qa, ka, u, qd, kd, _swap_cols(qd), _swap_cols(kd), va, v, vd], axis=-1))
    shared = {
        "w_ada": f(inp["w_ada"]),
        "b_adaT": np.ascontiguousarray(f(inp["b_ada"]).reshape(DEPTH, 72, 128).transpose(0, 2, 1)),
        "w_in_ext": w_in_ext,
        "w_out": f(inp["w_out"]),
        "gm_wsT": np.ascontiguousarray(f(inp["gm_ws"]).transpose(0, 1, 3, 2)),
        "gm_bs": f(inp["gm_bs"]),
        "gm_norm": f(inp["gm_norm"]).reshape(DEPTH, 256),
        "da_subln": f(inp["da_subln"]),
        "fnT": np.ascontiguousarray(f(inp["final_norm"]).reshape(KC, 128).T),
    }
    for k in ("ffn1_w1", "ffn1_w3", "ffn1_w2", "ffn2_w1", "ffn2_w3", "ffn2_w2", "da_lq1", "da_lk1", "da_lq2", "da_lk2"):
        shared[k] = f(inp[k])
    rpb = f(inp["na_rpb"])
    cosT, sinT = _rope_tables()
    shared["cosT"] = cosT
    shared["sinT"] = sinT
    shared["nat"] = np.stack([_na_tables(rpb[l]) for l in range(DEPTH)], axis=0)
    cores = []
    xs0 = []
    for b in range(NCORES):
        m = dict(shared)
        cT = np.stack([c[b].reshape(KC, 128).T, c_ctx.reshape(KC, 128).T], axis=-1)
        m["cT"] = np.ascontiguousarray(cT)
        cores.append(m)
        xT = np.concatenate([x[b].T, ctx[b].T], axis=1)
        xs0.append(np.ascontiguousarray(xT))
    return cores, xs0


FUSED = dict(
    phases=[("mod", 0), ("mod", 1),
            ("ffn1", 0), ("proj", 0), ("gm", 0), ("na", 0), ("da", 0), ("wout", 0), ("ffn2", 0),
            ("ffn1", 1), ("proj", 1), ("gm", 1), ("na", 1), ("da", 1), ("wout", 1), ("ffn2", 1), ("final", 1)],
    ins=["xs"], outs=["outT"])


def run_launch(cfg, cores, state):
    nc = build_program(cfg["phases"], cfg["ins"], cfg["outs"], fused=False)
    in_maps = []
    ncores = len(state)
    for core in range(ncores):
        m = {}
        for name in build_program.used:
            if name in cores[core]:
                m[name] = cores[core][name]
            else:
                k, l = name.rsplit("_", 1)
                m[name] = np.ascontiguousarray(cores[core][k][int(l)])
        for k in cfg["ins"]:
            m[k + "_i"] = state[core][k]
        in_maps.append(m)
    res = run_bass_kernel_spmd(nc, in_maps, core_ids=list(range(ncores)))
    for core in range(ncores):
        for k in cfg["outs"]:
            state[core][k] = res.results[core][k + "_o"]
    return state


def kernel(**inputs):
    cores, xs0 = prepare_inputs(inputs)
    state = [{"xs": xs0[i]} for i in range(NCORES)]
    state = run_launch(FUSED, cores, state)
    out = np.empty((NCORES, TL, D), np.float32)
    for b in range(NCORES):
        out[b] = state[b]["outT"].T
    return out
```
